# Optimizing a Trainium2 kernel written in Bass

```python
import math
import jax, jax.numpy as jnp
from jax import lax
import numpy as np

D_MODEL = 2048
BATCH = 4
SEQ = 4096
DEPTH = 2

GRID_W = 64
CTX_LEN = 256
EPS = 1e-6
ADA_SCALE = 0.5

HG_WIDTH = D_MODEL // 4
DA_WIDTH = D_MODEL // 4
SSM_WIDTH = D_MODEL // 2
MIX_WIDTH = HG_WIDTH + DA_WIDTH + SSM_WIDTH

HG_DK = 128
HG_DV = 128
HG_HEADS = HG_WIDTH // HG_DV
HG_KEYS = HG_HEADS * HG_DK
HG_CHUNK = 64

DA_DH = 64
DA_DV = 2 * DA_DH
DA_HEADS = DA_WIDTH // DA_DV
DA_QK = DA_HEADS * 2 * DA_DH
Q_BLOCK = 128
ROPE_THETA = 10000.0

SSM_P = 64
SSM_HEADS = SSM_WIDTH // SSM_P
SSM_GROUPS = 2
SSM_HPG = SSM_HEADS // SSM_GROUPS
SSM_N = 128
SSM_GN = SSM_GROUPS * SSM_N
SSM_CONV = 3
SSM_CONV_DIM = SSM_WIDTH + 2 * SSM_GN
SSM_CHUNK = 64

D_FF = -(-8 * D_MODEL // (3 * 256)) * 256

IN_SPLITS = (HG_KEYS, HG_KEYS, HG_KEYS, HG_WIDTH, HG_WIDTH,
             DA_QK, DA_QK, DA_WIDTH,
             SSM_WIDTH, SSM_CONV_DIM, SSM_HEADS, SSM_HEADS)
IN_COLS = sum(IN_SPLITS)

kernel_name = "hybrid_hgrn2_diffattn_ssd_flow_block"

F32 = jnp.float32


def rmsnorm(x, w):
    xf = x.astype(F32)
    y = xf * lax.rsqrt(jnp.mean(xf * xf, axis=-1, keepdims=True) + EPS)
    return (y * w.astype(F32)).astype(x.dtype)


def flip(a):
    return jnp.flip(a, axis=1)


def to_chunks(a, L):
    B, T = a.shape[:2]
    return jnp.moveaxis(a.reshape(B, T // L, L, *a.shape[2:]), 1, 0)


def from_chunks(a):
    nc, B, L = a.shape[:3]
    return jnp.moveaxis(a, 0, 1).reshape(B, nc * L, *a.shape[3:])


def split_cols(p):
    idx, acc = [], 0
    for s in IN_SPLITS[:-1]:
        acc += s
        idx.append(acc)
    return jnp.split(p, idx, axis=-1)


def rope_2d_tables(rows):
    half = DA_DH // 2
    inv = 1.0 / (ROPE_THETA ** (jnp.arange(0, half, 2, dtype=F32) / half))
    r = jnp.repeat(jnp.arange(rows, dtype=F32), GRID_W)
    col = jnp.tile(jnp.arange(GRID_W, dtype=F32), rows)
    ar, ac = r[:, None] * inv, col[:, None] * inv
    sh = lambda a: a[:, None, None, :]
    return (sh(jnp.cos(ar)), sh(jnp.sin(ar)), sh(jnp.cos(ac)), sh(jnp.sin(ac)))


def rotate(x, cos, sin):
    x1, x2 = jnp.split(x, 2, axis=-1)
    return jnp.concatenate([x1 * cos - x2 * sin, x1 * sin + x2 * cos], axis=-1)


def apply_rope_2d(x, tabs):
    cr, sr, cc, sc = tabs
    xr, xc = jnp.split(x.astype(F32), 2, axis=-1)
    return jnp.concatenate([rotate(xr, cr, sr), rotate(xc, cc, sc)], axis=-1).astype(x.dtype)


def gla_chunk_scan(q, k, v, logf, s0):
    L = HG_CHUNK
    mask = jnp.tril(jnp.ones((L, L), bool))[None, :, :, None, None]

    def body(S, inp):
        qc, kc, vc, gc = inp
        b = jnp.cumsum(gc, axis=1)
        dec = jnp.exp(jnp.where(mask, b[:, :, None] - b[:, None], -jnp.inf))
        att = jnp.einsum('bthk,bshk,btshk->bhts', qc, kc, dec)
        o = jnp.einsum('bhts,bshv->bthv', att, vc) + jnp.einsum('bthk,bhkv->bthv', qc * jnp.exp(b), S)
        bl = b[:, -1]
        S = S * jnp.exp(bl)[..., None] + jnp.einsum('bshk,bshv->bhkv', kc * jnp.exp(bl[:, None] - b), vc)
        return S, o

    S, o = lax.scan(body, s0, (to_chunks(q, L), to_chunks(k, L), to_chunks(v, L), to_chunks(logf, L)))
    return from_chunks(o), S


def gla_final_state(k, logf, v):
    b = jnp.cumsum(logf, axis=1)
    return jnp.einsum('bshk,bshv->bhkv', k * jnp.exp(b[:, -1:] - b), v)


def gla_bidir(q, kf, gf, kb, gb, v, sf, sb):
    of, Sf = gla_chunk_scan(q, kf, v, gf, sf)
    ob, Sb = gla_chunk_scan(flip(q), flip(kb), flip(v), flip(gb), sb)
    return of + flip(ob), Sf, Sb


def hgrn2_gates(ffp, fbp, ip, lb):
    B, T, _ = ffp.shape
    hd = lambda a, d: a.astype(F32).reshape(B, T, HG_HEADS, d)
    f_f = lb + (1.0 - lb) * jax.nn.sigmoid(hd(ffp, HG_DK))
    f_b = lb + (1.0 - lb) * jax.nn.sigmoid(hd(fbp, HG_DK))
    return 1.0 - f_f, jnp.log(f_f), 1.0 - f_b, jnp.log(f_b), hd(ip, HG_DV)


def hgrn2_query(qp):
    B, T, _ = qp.shape
    return jax.nn.silu(qp.astype(F32).reshape(B, T, HG_HEADS, HG_DK))


def hgrn2_out(o, gp, norm_w):
    B, T, _ = gp.shape
    g = gp.astype(F32).reshape(B, T, HG_HEADS, HG_DV)
    return (rmsnorm(o, norm_w) * jax.nn.silu(g)).reshape(B, T, HG_WIDTH).astype(gp.dtype)


def hgrn2_mixer(lat, ctx, lb, norm_w, need_ctx):
    qp, ffp, fbp, ip, gp = lat
    qpc, ffpc, fbpc, ipc, gpc = ctx
    kf, gf, kb, gb, v = hgrn2_gates(ffp, fbp, ip, lb)
    kfc, gfc, kbc, gbc, vc = hgrn2_gates(ffpc, fbpc, ipc, lb)
    if need_ctx:
        s0 = jnp.zeros((qp.shape[0], HG_HEADS, HG_DK, HG_DV), F32)
        oc, sf, sb = gla_bidir(hgrn2_query(qpc), kfc, gfc, kbc, gbc, vc, s0, s0)
        out_c = hgrn2_out(oc, gpc, norm_w)
    else:
        sf = gla_final_state(kfc, gfc, vc)
        sb = gla_final_state(flip(kbc), flip(gbc), flip(vc))
        out_c = None
    o, _, _ = gla_bidir(hgrn2_query(qp), kf, gf, kb, gb, v, sf, sb)
    return hgrn2_out(o, gp, norm_w), out_c


def diff_attn_sweep(q, k, v, lam):
    B, Tq = q.shape[:2]
    qb = to_chunks(q, Q_BLOCK)
    scale = DA_DH ** -0.5

    def one(qi):
        s = jnp.einsum('bqhcd,bkhcd->bhcqk', qi, k).astype(F32) * scale
        p = jax.nn.softmax(s, axis=-1)
        a = p[:, :, 0] - lam * p[:, :, 1]
        return jnp.einsum('bhqk,bkhv->bqhv', a.astype(v.dtype), v)

    return from_chunks(lax.map(one, qb))


def diff_mixer(lat, ctx, rope, lam_p, subln_w, layer_idx, need_ctx):
    dq, dk, dv = lat
    dqc, dkc, dvc = ctx
    B, T, _ = dq.shape
    Tc = dqc.shape[1]
    qk_heads = lambda a, n: a.reshape(B, n, DA_HEADS, 2, DA_DH)
    q = apply_rope_2d(qk_heads(dq, T), rope)
    k = apply_rope_2d(qk_heads(dk, T), rope)
    v = dv.reshape(B, T, DA_HEADS, DA_DV)
    kc = qk_heads(dkc, Tc)
    vc = dvc.reshape(B, Tc, DA_HEADS, DA_DV)
    lam_init = 0.8 - 0.6 * math.exp(-0.3 * layer_idx)
    lp = lam_p.astype(F32)
    lam = jnp.exp(jnp.sum(lp[0] * lp[1])) - jnp.exp(jnp.sum(lp[2] * lp[3])) + lam_init
    finish = lambda o: (rmsnorm(o, subln_w) * (1.0 - lam_init)).reshape(B, o.shape[1], DA_WIDTH)
    keys = jnp.concatenate([k, kc], axis=1)
    vals = jnp.concatenate([v, vc], axis=1)
    out = finish(diff_attn_sweep(q, keys, vals, lam))
    out_c = finish(diff_attn_sweep(qk_heads(dqc, Tc), kc, vc, lam)) if need_ctx else None
    return out, out_c


def dwconv_centred(x, w, b):
    y = lax.conv_general_dilated(x, w[:, None, :].astype(x.dtype), window_strides=(1,),
                                 padding=[(SSM_CONV // 2, SSM_CONV // 2)],
                                 dimension_numbers=('NWC', 'WIO', 'NWC'),
                                 feature_group_count=x.shape[-1])
    return y + b.astype(x.dtype)


def ssd_chunk_scan(x, dt, A, Bm, Cm, s0):
    L = SSM_CHUNK
    mask = jnp.tril(jnp.ones((L, L), bool))[None, :, :, None]

    def body(S, inp):
        xc, dtc, Bc, Cc = inp
        cs = jnp.cumsum(dtc * A, axis=1)
        Lm = jnp.exp(jnp.where(mask, cs[:, :, None] - cs[:, None], -jnp.inf))
        xdt = xc * dtc[..., None]
        y = (jnp.einsum('bthn,bshn,btsh,bshp->bthp', Cc, Bc, Lm, xdt)
             + jnp.einsum('bthn,bhpn->bthp', Cc, S) * jnp.exp(cs)[..., None])
        last = cs[:, -1]
        S = S * jnp.exp(last)[:, :, None, None] + jnp.einsum(
            'bshn,bshp->bhpn', Bc * jnp.exp(last[:, None] - cs)[..., None], xdt)
        return S, y

    S, y = lax.scan(body, s0, (to_chunks(x, L), to_chunks(dt, L), to_chunks(Bm, L), to_chunks(Cm, L)))
    return from_chunks(y), S


def ssd_final_state(x, dt, Bm, A):
    cs = jnp.cumsum(dt * A, axis=1)
    return jnp.einsum('bshn,bshp->bhpn', Bm * jnp.exp(cs[:, -1:] - cs)[..., None], x * dt[..., None])


def ssd_bidir(x, dtf, dtb, Bm, Cm, A, sf, sb):
    yf, Sf = ssd_chunk_scan(x, dtf, A[0], Bm, Cm, sf)
    yb, Sb = ssd_chunk_scan(flip(x), flip(dtb), A[1], flip(Bm), flip(Cm), sb)
    return yf + flip(yb), Sf, Sb


def ssm_streams(sxbc, sdtf, sdtb, conv_w, conv_b, dt_bias):
    B, T, _ = sxbc.shape
    xbc = jax.nn.silu(dwconv_centred(sxbc, conv_w, conv_b)).astype(F32)
    xs, Bs, Cs = jnp.split(xbc, [SSM_WIDTH, SSM_WIDTH + SSM_GN], axis=-1)
    x = xs.reshape(B, T, SSM_HEADS, SSM_P)
    Bm = jnp.repeat(Bs.reshape(B, T, SSM_GROUPS, SSM_N), SSM_HPG, axis=2)
    Cm = jnp.repeat(Cs.reshape(B, T, SSM_GROUPS, SSM_N), SSM_HPG, axis=2)
    dbias = dt_bias.astype(F32)
    dtf = jax.nn.softplus(sdtf.astype(F32) + dbias[0])
    dtb = jax.nn.softplus(sdtb.astype(F32) + dbias[1])
    return x, Bm, Cm, dtf, dtb


def ssm_out(y, x, z, d_skip, norm_w):
    B, T, _ = z.shape
    y = y + d_skip.astype(F32)[:, None] * x
    yz = y.reshape(B, T, SSM_WIDTH) * jax.nn.silu(z.astype(F32))
    yz = rmsnorm(yz.reshape(B, T, SSM_GROUPS, SSM_WIDTH // SSM_GROUPS),
                 norm_w.reshape(SSM_GROUPS, SSM_WIDTH // SSM_GROUPS))
    return yz.reshape(B, T, SSM_WIDTH).astype(z.dtype)


def ssm_mixer(lat, ctx, conv_w, conv_b, dt_bias, a_log, d_skip, norm_w, need_ctx):
    z, sxbc, sdtf, sdtb = lat
    zc, sxbcc, sdtfc, sdtbc = ctx
    A = -jnp.exp(a_log.astype(F32))
    x, Bm, Cm, dtf, dtb = ssm_streams(sxbc, sdtf, sdtb, conv_w, conv_b, dt_bias)
    xc, Bc, Cc, dtfc, dtbc = ssm_streams(sxbcc, sdtfc, sdtbc, conv_w, conv_b, dt_bias)
    if need_ctx:
        s0 = jnp.zeros((z.shape[0], SSM_HEADS, SSM_P, SSM_N), F32)
        yc, sf, sb = ssd_bidir(xc, dtfc, dtbc, Bc, Cc, A, s0, s0)
        out_c = ssm_out(yc, xc, zc, d_skip, norm_w)
    else:
        sf = ssd_final_state(xc, dtfc, Bc, A[0])
        sb = ssd_final_state(flip(xc), flip(dtbc), flip(Bc), A[1])
        out_c = None
    y, _, _ = ssd_bidir(x, dtf, dtb, Bm, Cm, A, sf, sb)
    return ssm_out(y, x, z, d_skip, norm_w), out_c


def swiglu(u, w_g, w_u, w_d):
    return (jax.nn.silu(u @ w_g) * (u @ w_u)) @ w_d


def trunk_layer(h, hc, mod, mod_c, rope, layer_idx, lb, norm1_w, w_in, hg_norm_w, da_lam, da_subln_w,
                conv_w, conv_b, dt_bias, a_log, d_skip, ssm_norm_w, w_out, norm2_w, w_g, w_u, w_d, need_ctx):
    sh1, sc1, g1, sh2, sc2, g2 = jnp.split(mod[:, None, :], 6, axis=-1)
    sh1c, sc1c, g1c, sh2c, sc2c, g2c = jnp.split(mod_c, 6, axis=-1)
    u = rmsnorm(h, norm1_w) * (1 + sc1) + sh1
    uc = rmsnorm(hc, norm1_w) * (1 + sc1c) + sh1c
    p = split_cols(u @ w_in)
    pc = split_cols(uc @ w_in)
    hg, hg_c = hgrn2_mixer(p[0:5], pc[0:5], lb, hg_norm_w, need_ctx)
    da, da_c = diff_mixer(p[5:8], pc[5:8], rope, da_lam, da_subln_w, layer_idx, need_ctx)
    sm, sm_c = ssm_mixer(p[8:12], pc[8:12], conv_w, conv_b, dt_bias, a_log, d_skip, ssm_norm_w, need_ctx)
    h = h + g1 * (jnp.concatenate([hg, da, sm], axis=-1) @ w_out)
    u = rmsnorm(h, norm2_w) * (1 + sc2) + sh2
    h = h + g2 * swiglu(u, w_g, w_u, w_d)
    if need_ctx:
        hc = hc + g1c * (jnp.concatenate([hg_c, da_c, sm_c], axis=-1) @ w_out)
        uc = rmsnorm(hc, norm2_w) * (1 + sc2c) + sh2c
        hc = hc + g2c * swiglu(uc, w_g, w_u, w_d)
    return h, hc


def setup_inputs(seed: int = 0) -> dict:
    key = jax.random.key(seed)
    ks = jax.random.split(key, 24)
    nrm = lambda k, shape, s: jax.random.normal(k, shape, F32) * s
    dt = jnp.exp(jax.random.uniform(ks[14], (DEPTH, 2, SSM_HEADS), F32,
                                    minval=math.log(1e-3), maxval=math.log(1e-1)))
    return {
        "x": nrm(ks[0], (BATCH, SEQ, D_MODEL), 1.0),
        "c": nrm(ks[1], (BATCH, D_MODEL), 1.0),
        "ctx": nrm(ks[2], (BATCH, CTX_LEN, D_MODEL), 1.0),
        "c_ctx": nrm(ks[3], (D_MODEL,), 1.0),
        "w_ada": nrm(ks[4], (DEPTH, D_MODEL, 6 * D_MODEL), ADA_SCALE * D_MODEL ** -0.5),
        "b_ada": nrm(ks[5], (DEPTH, 6 * D_MODEL), 0.02),
        "norm1_w": 1.0 + nrm(ks[6], (DEPTH, D_MODEL), 0.02),
        "w_in": nrm(ks[7], (DEPTH, D_MODEL, IN_COLS), D_MODEL ** -0.5),
        "hg_lb_logits": nrm(ks[8], (DEPTH, HG_KEYS), 0.1),
        "hg_norm_w": 1.0 + nrm(ks[9], (DEPTH, HG_DV), 0.02),
        "da_lambda": nrm(ks[10], (DEPTH, 4, DA_DH), 0.1),
        "da_subln_w": 1.0 + nrm(ks[11], (DEPTH, DA_DV), 0.02),
        "ssm_conv_w": nrm(ks[12], (DEPTH, SSM_CONV, SSM_CONV_DIM), SSM_CONV ** -0.5),
        "ssm_conv_b": nrm(ks[13], (DEPTH, SSM_CONV_DIM), 0.02),
        "ssm_dt_bias": dt + jnp.log(-jnp.expm1(-dt)),
        "ssm_a_log": jnp.log(jax.random.uniform(ks[15], (DEPTH, 2, SSM_HEADS), F32, minval=1.0, maxval=16.0)),
        "ssm_d": 1.0 + nrm(ks[16], (DEPTH, SSM_HEADS), 0.1),
        "ssm_norm_w": 1.0 + nrm(ks[17], (DEPTH, SSM_WIDTH), 0.02),
        "w_out": nrm(ks[18], (DEPTH, MIX_WIDTH, D_MODEL), MIX_WIDTH ** -0.5),
        "norm2_w": 1.0 + nrm(ks[19], (DEPTH, D_MODEL), 0.02),
        "w_ffn_gate": nrm(ks[20], (DEPTH, D_MODEL, D_FF), D_MODEL ** -0.5),
        "w_ffn_up": nrm(ks[21], (DEPTH, D_MODEL, D_FF), D_MODEL ** -0.5),
        "w_ffn_down": nrm(ks[22], (DEPTH, D_FF, D_MODEL), D_FF ** -0.5),
        "final_norm_w": 1.0 + nrm(ks[23], (D_MODEL,), 0.02),
    }


def reference(x, c, ctx, c_ctx, w_ada, b_ada, norm1_w, w_in, hg_lb_logits, hg_norm_w, da_lambda, da_subln_w,
              ssm_conv_w, ssm_conv_b, ssm_dt_bias, ssm_a_log, ssm_d, ssm_norm_w, w_out, norm2_w,
              w_ffn_gate, w_ffn_up, w_ffn_down, final_norm_w):
    T = x.shape[1]
    ROWS = T // GRID_W
    rope = rope_2d_tables(ROWS)
    lb_soft = jax.nn.softmax(hg_lb_logits.astype(F32), axis=0)
    lb_all = jnp.cumsum(lb_soft, axis=0) - lb_soft[0]
    sc = jax.nn.silu(c)
    scc = jax.nn.silu(c_ctx)
    h, hc = x, ctx
    for l in range(DEPTH):
        mod = sc @ w_ada[l] + b_ada[l]
        mod_c = scc @ w_ada[l] + b_ada[l]
        h, hc = trunk_layer(h, hc, mod, mod_c, rope, l, lb_all[l].reshape(HG_HEADS, HG_DK),
                            norm1_w[l], w_in[l], hg_norm_w[l], da_lambda[l], da_subln_w[l],
                            ssm_conv_w[l], ssm_conv_b[l], ssm_dt_bias[l], ssm_a_log[l], ssm_d[l], ssm_norm_w[l],
                            w_out[l], norm2_w[l], w_ffn_gate[l], w_ffn_up[l], w_ffn_down[l],
                            need_ctx=(l < DEPTH - 1))
    return rmsnorm(h, final_norm_w)
```

```python
import math
from contextlib import ExitStack

import numpy as np
import concourse.bass as bass
import concourse.mybir as mybir
from concourse.bass_utils import run_bass_kernel_spmd

F32 = mybir.dt.float32
BF16 = mybir.dt.bfloat16
AF = mybir.ActivationFunctionType
ALU = mybir.AluOpType

D = 2048
DEPTH = 2
CTX = 256
GRID_W = 64
EPS = 1e-6
DFF = 5632
NCOL = 7712
N_CORES = 8
SAME_ENGINE_SYNC = True
STORES_ON_POOL = True
ACTIVE_CORES = (0, 1, 4, 5)


class Slot:
    __slots__ = ("sem", "cnt")

    def __init__(self, sem):
        self.sem = sem
        self.cnt = 0


class Buf:
    __slots__ = ("name", "lw", "rd", "slot")

    def __init__(self, name):
        self.name = name
        self.lw = {}
        self.rd = {}
        self.slot = None


class View:
    __slots__ = ("ap", "buf")

    def __init__(self, ap, buf):
        self.ap = ap
        self.buf = buf

    def __getitem__(self, k):
        return View(self.ap[k], self.buf)

    def rearrange(self, s, **kw):
        return View(self.ap.rearrange(s, **kw), self.buf)

    def to_broadcast(self, shape):
        return View(self.ap.to_broadcast(list(shape)), self.buf)

    def partition_broadcast(self, n):
        return View(self.ap.partition_broadcast(n), self.buf)

    def bitcast(self, dt):
        return View(self.ap.bitcast(dt), self.buf)


class TT:
    def __init__(self, handle, name, is_ap=False):
        self.h = handle
        self.buf = Buf(name)
        self.is_ap = is_ap

    def __getitem__(self, k):
        return View(self.h[k], self.buf)

    def v(self):
        return View(self.h[:] if not self.is_ap else self.h, self.buf)


def _aps(x):
    return x.ap if isinstance(x, View) else x


def _bufs(*xs):
    out = []
    for x in xs:
        if isinstance(x, View) and x.buf is not None and x.buf not in out:
            out.append(x.buf)
    return out


class Sched:
    ENGS = ("pe", "act", "dve", "pool", "sp")

    def __init__(self, nc, stack, same_engine_sync=True):
        self.nc = nc
        self.stacks = [stack]
        self.prog = {e: [] for e in self.ENGS}
        self.sem = {}
        self.tick = {}
        for e in ("pe", "act", "dve", "pool"):
            self.sem[e] = stack.enter_context(nc.semaphore("s_" + e))
            self.tick[e] = 0
        self.seen = {e: {} for e in self.ENGS}
        self.same = same_engine_sync
        self.slots = []
        self.free_slots = []
        self.scope_bufs = [[]]
        self.ninst = 0
        self.base = stack
        self.uid = 0

    def push(self):
        st = ExitStack()
        self.stacks.append(st)
        self.scope_bufs.append([])
        return st

    def pop(self):
        self.barrier()
        st = self.stacks.pop()
        st.close()
        for b in self.scope_bufs.pop():
            if b.slot is not None:
                self.free_slots.append(b.slot)
                b.slot = None

    def sb(self, name, shape, dtype):
        self.uid += 1
        nm = "%s_%d" % (name, self.uid)
        h = self.stacks[-1].enter_context(self.nc.sbuf_tensor(nm, list(shape), dtype))
        t = TT(h, nm)
        self.scope_bufs[-1].append(t.buf)
        return t

    def ps(self, name, shape, dtype):
        self.uid += 1
        nm = "%s_%d" % (name, self.uid)
        h = self.stacks[-1].enter_context(self.nc.psum_tensor(nm, list(shape), dtype))
        return TT(h, nm)

    def dram(self, name, shape, dtype, kind="Internal"):
        h = self.nc.dram_tensor(name, list(shape), dtype, kind=kind)
        return TT(h.ap(), name, is_ap=True)

    def dslot(self, buf):
        if buf.slot is None:
            if self.free_slots:
                buf.slot = self.free_slots.pop()
            else:
                buf.slot = Slot(self.base.enter_context(self.nc.semaphore("d%d" % len(self.slots))))
                self.slots.append(buf.slot)
        return buf.slot

    def _needs(self, eng, reads, writes, partial, is_dma=False):
        needs = {}

        def add(d):
            for s, v in d.items():
                if needs.get(s, 0) < v:
                    needs[s] = v
        for b in reads:
            add(b.lw)
        for b in writes:
            add(b.rd)
            if not partial:
                add(b.lw)
        waits = []
        seen = self.seen[eng]
        own = self.sem.get(eng)
        for s, v in needs.items():
            if s is own and not is_dma and (eng == "pe" or not self.same):
                continue
            if seen.get(s, 0) < v:
                seen[s] = v
                waits.append((s, v))
        return waits

    def _commit(self, ev, reads, writes, partial):
        s, v = ev
        for b in reads:
            if b.rd.get(s, 0) < v:
                b.rd[s] = v
        for b in writes:
            if partial:
                if b.lw.get(s, 0) < v:
                    b.lw[s] = v
            else:
                b.lw = {s: v}
                b.rd = {}

    def op(self, eng, fn, reads=(), writes=(), partial=False):
        waits = self._needs(eng, reads, writes, partial)
        self.tick[eng] += 1
        ev = (self.sem[eng], self.tick[eng])
        self.prog[eng].append((waits, fn, ev[0], 1))
        self._commit(ev, reads, writes, partial)
        self.ninst += 1
        return ev

    def dma(self, q, out, in_, sembuf=None, partial=None, slow=False):
        ob, ib = out.buf, in_.buf
        o_ap, i_ap = out.ap, in_.ap
        o_is_dram = "DRAM" in str(o_ap.space).upper() or "HBM" in str(o_ap.space).upper()
        i_is_dram = "DRAM" in str(i_ap.space).upper() or "HBM" in str(i_ap.space).upper()
        if q == "sp" and o_is_dram and not i_is_dram and STORES_ON_POOL:
            q = "pool"
        if q == "sp!":
            q = "sp"
        if sembuf is None:
            o_dram = "DRAM" in str(o_ap.space).upper() or "HBM" in str(o_ap.space).upper()
            sembuf = ib if o_dram else ob
        if partial is None:
            partial = "DRAM" in str(o_ap.space).upper() or "HBM" in str(o_ap.space).upper()
        slot = self.dslot(sembuf)
        sem = slot.sem
        reads, writes = [ib], [ob]
        waits = self._needs(q, reads, writes, partial, True)
        prev = 16 * slot.cnt
        if prev and self.seen[q].get(sem, 0) < prev:
            self.seen[q][sem] = prev
            waits.append((sem, prev))
        slot.cnt += 1
        ev = (sem, 16 * slot.cnt)
        kw = {"allow_slow_non_contiguous": True} if slow else {}
        self.prog[q].append((waits, lambda e: e.dma_start(out=o_ap, in_=i_ap, **kw), sem, 16))
        self._commit(ev, reads, writes, partial)
        self.ninst += 1
        return ev

    def barrier(self):
        allv = [(self.sem[e], self.tick[e]) for e in ("pe", "act", "dve", "pool") if self.tick[e]]
        allv += [(sl.sem, 16 * sl.cnt) for sl in self.slots if sl.cnt]
        for e in self.ENGS:
            waits = []
            seen = self.seen[e]
            for s, v in allv:
                if s is self.sem.get(e):
                    continue
                if seen.get(s, 0) < v:
                    seen[s] = v
                    waits.append((s, v))
            if waits:
                self.prog[e].append((waits, None, None, 0))

    def act(self, out, in_, func, scale=None, bias=None, accum_out=None, eng="act"):
        kw = {}
        rd = _bufs(in_)
        wr = _bufs(out)
        if scale is not None:
            kw["scale"] = _aps(scale)
            rd += _bufs(scale)
        if bias is not None:
            kw["bias"] = _aps(bias)
            rd += _bufs(bias)
        if accum_out is not None:
            kw["accum_out"] = _aps(accum_out)
            wr += _bufs(accum_out)
        o, i = out.ap, in_.ap
        return self.op("act", lambda e: e.activation(out=o, in_=i, func=func, **kw), rd, wr)

    def tt(self, eng, out, in0, in1, op):
        o, a, b = out.ap, in0.ap, in1.ap
        return self.op(eng, lambda e: e.tensor_tensor(out=o, in0=a, in1=b, op=op), _bufs(in0, in1), _bufs(out))

    def ts(self, eng, out, in0, s1, op0, s2=None, op1=None):
        o, a = out.ap, in0.ap
        x1, x2 = _aps(s1), _aps(s2)
        kw = {}
        if op1 is not None:
            kw["op1"] = op1
        return self.op(eng, lambda e: e.tensor_scalar(out=o, in0=a, scalar1=x1, scalar2=x2, op0=op0, **kw),
                       _bufs(in0, s1, s2), _bufs(out))

    def stt(self, out, in0, scalar, in1, op0, op1, eng="dve"):
        o, a, b = out.ap, in0.ap, in1.ap
        sc = _aps(scalar)
        return self.op(eng, lambda e: e.scalar_tensor_tensor(out=o, in0=a, scalar=sc, in1=b, op0=op0, op1=op1),
                       _bufs(in0, scalar, in1), _bufs(out))

    def copy(self, eng, out, in_):
        o, i = out.ap, in_.ap
        if eng == "act":
            return self.op("act", lambda e: e.activation(out=o, in_=i, func=AF.Copy), _bufs(in_), _bufs(out))
        return self.op(eng, lambda e: e.tensor_copy(out=o, in_=i), _bufs(in_), _bufs(out))

    def memset(self, eng, out, val):
        o = out.ap
        return self.op(eng, lambda e: e.memset(o, val), [], _bufs(out))

    def recip(self, out, in_, eng="dve"):
        o, i = out.ap, in_.ap
        return self.op(eng, lambda e: e.reciprocal(out=o, in_=i), _bufs(in_), _bufs(out))

    def ttr(self, out, in0, in1, accum_out, op0=ALU.mult, op1=ALU.add, scale=1.0, scalar=0.0):
        o, a, b, acc = out.ap, in0.ap, in1.ap, accum_out.ap
        return self.op("dve", lambda e: e.tensor_tensor_reduce(out=o, in0=a, in1=b, scale=scale, scalar=scalar,
                                                               op0=op0, op1=op1, accum_out=acc),
                       _bufs(in0, in1), _bufs(out, accum_out))

    def scan(self, out, d0, d1, initial=0.0, op0=ALU.mult, op1=ALU.add):
        o, a, b = out.ap, d0.ap, d1.ap
        ini = _aps(initial)
        return self.op("dve", lambda e: e.tensor_tensor_scan(out=o, data0=a, data1=b, initial=ini, op0=op0, op1=op1),
                       _bufs(d0, d1, initial), _bufs(out))

    def reduce(self, out, in_, op=ALU.add, eng="dve"):
        o, i = out.ap, in_.ap
        return self.op(eng, lambda e: e.tensor_reduce(out=o, in_=i, axis=mybir.AxisListType.X, op=op),
                       _bufs(in_), _bufs(out))

    def mm(self, out, lhsT, rhs, start=True, stop=True, skip=False):
        o, a, b = out.ap, lhsT.ap, rhs.ap
        kw = {"skip_group_check": True} if skip else {}
        return self.op("pe", lambda e: e.matmul(o, a, b, start=start, stop=stop, **kw), _bufs(lhsT, rhs), _bufs(out))

    def tr(self, out, in_, ident):
        o, a, b = out.ap, in_.ap, ident.ap
        return self.op("pe", lambda e: e.transpose(out=o, in_=a, identity=b), _bufs(in_, ident), _bufs(out))

    def aselect(self, out, in_, pattern, cmp, fill, base, cm):
        o, i = out.ap, in_.ap
        return self.op("pool", lambda e: e.affine_select(out=o, in_=i, pattern=pattern, compare_op=cmp, fill=fill,
                                                         base=base, channel_multiplier=cm), _bufs(in_), _bufs(out))

    def final_wait(self, eng, bufs):
        needs = {}
        for b in bufs:
            for s, v in b.lw.items():
                if needs.get(s, 0) < v:
                    needs[s] = v
        self.prog[eng].append((list(needs.items()), None, None, 0))

    def emit(self):
        nc = self.nc
        prog = self.prog

        def run(engine, items):
            for waits, fn, sem, inc in items:
                for s, v in waits:
                    engine.wait_ge(s, v)
                if fn is not None:
                    fn(engine).then_inc(sem, inc)

        with nc.Block() as block:
            @block.tensor
            def _(e):
                run(e, prog["pe"])

            @block.scalar
            def _(e):
                run(e, prog["act"])

            @block.vector
            def _(e):
                run(e, prog["dve"])

            @block.gpsimd
            def _(e):
                run(e, prog["pool"])

            @block.sync
            def _(e):
                run(e, prog["sp"])


class Cfg:
    def __init__(self, seq=4096, debug=(), stop_after=None, depth=DEPTH):
        self.seq = seq
        self.nt = CTX + seq
        self.debug = tuple(debug)
        self.stop_after = stop_after
        self.depth = depth
        self.groups = [(0, CTX)] + [(CTX + i * 512, 512) for i in range(seq // 512)]
        self.nch = self.nt // 64
        self.ntile = self.nt // 128


INPUT_SPECS = None


def input_shapes(cfg):
    nt = cfg.nt
    return {
        "xin": ([nt, D], F32),
        "cc": ([2, D], F32),
        "w_ada": ([DEPTH, D, 6 * D], F32),
        "b_ada": ([DEPTH, 6 * D], F32),
        "norm1_w": ([DEPTH, D], F32),
        "norm2_w": ([DEPTH, D], F32),
        "final_norm_w": ([1, D], F32),
        "w_in_ext": ([DEPTH, D, NCOL], F32),
        "hg_lb": ([DEPTH, 512], F32),
        "hg_norm_w": ([DEPTH, 128], F32),
        "da_lambda": ([DEPTH, 256], F32),
        "da_subln_w": ([DEPTH, 128], F32),
        "conv_w": ([DEPTH, 3, 1536], F32),
        "conv_b": ([DEPTH, 1536], F32),
        "dt_bias": ([DEPTH, 32], F32),
        "a_log": ([DEPTH, 32], F32),
        "ssm_d": ([DEPTH, 16], F32),
        "ssm_norm_w": ([DEPTH, 1024], F32),
        "w_out": ([DEPTH, D, D], F32),
        "w_g": ([DEPTH, D, DFF], F32),
        "w_u": ([DEPTH, D, DFF], F32),
        "w_d": ([DEPTH, DFF, D], F32),
        "rope_c": ([128, nt], F32),
        "rope_s": ([128, nt], F32),
    }


def build_program(cfg):
    nc = bass.Bass("TRN2", target_bir_lowering=False)
    nt, seq = cfg.nt, cfg.seq
    with ExitStack() as st:
        S = Sched(nc, st, same_engine_sync=SAME_ENGINE_SYNC)
        I = {}
        for name, (shape, dt) in input_shapes(cfg).items():
            I[name] = S.dram(name, shape, dt, kind="ExternalInput")
        Y = S.dram("y", [seq, D], F32, kind="ExternalOutput")
        P = Prog(S, cfg, I, Y)
        P.build()
        S.emit()
    return nc


class Prog:
    def __init__(self, S, cfg, I, Y):
        self.S, self.cfg, self.I, self.Y = S, cfg, I, Y
        self.dbg = {}

    def dbg_out(self, name, src, shape, dtype):
        S = self.S
        o = S.dram("dbg_" + name, shape, dtype, kind="ExternalOutput")
        S.dma("sp", o.v(), src.v(), sembuf=o.buf)
        self.outs.append(o)

    def dbg_sb(self, name, view, shape, dtype):
        if name not in self.cfg.debug:
            return
        o = self.S.dram("dbg_" + name, shape, dtype, kind="ExternalOutput")
        self.S.dma("sp", o.v(), view)
        self.outs.append(o)

    def build(self):
        S, cfg, I = self.S, self.cfg, self.I
        nt = cfg.nt
        self.outs = [self.Y]
        self.WIN = [S.dram("WIN%d" % l, [D, NCOL], BF16) for l in range(DEPTH)]
        self.WO = [S.dram("WO%d" % l, [D, D], BF16) for l in range(DEPTH)]
        self.WG = [S.dram("WG%d" % l, [DFF // 512, 128, 16, 512], BF16) for l in range(DEPTH)]
        self.WU = [S.dram("WU%d" % l, [DFF // 512, 128, 16, 512], BF16) for l in range(DEPTH)]
        self.WD = [S.dram("WD%d" % l, [DFF, D], BF16) for l in range(DEPTH)]
        self.MOD = [S.dram("MOD%d" % l, [2, 6 * D], F32) for l in range(DEPTH)]
        ng = len(cfg.groups)
        self.UT = [S.dram("UT%d" % g, [128, 16, 512], BF16) for g in range(ng)]
        self.MT = [S.dram("MT%d" % g, [128, 16, 512], BF16) for g in range(ng)]
        self.U2T = [S.dram("U2T%d" % g, [128, 16, 512], BF16) for g in range(ng)]
        self.H1 = [S.dram("H1_%d" % g, [512, D], F32) for g in range(ng)]
        self.H = [S.dram("H_%d" % g, [512, D], F32) for g in range(ng)]
        self.HQT = S.dram("HQT", [4, 128, nt], F32)
        self.SFT = S.dram("SFT", [4, 128, nt], F32)
        self.SBT = S.dram("SBT", [4, 128, nt], F32)
        self.QT = S.dram("QT", [4, 128, nt], BF16)
        self.KT = S.dram("KT", [4, 128, nt], BF16)
        self.XBCT = S.dram("XBCT", [12, 128, nt], F32)
        self.HV = S.dram("HV", [nt, 512], BF16)
        self.HG = S.dram("HG", [nt, 512], F32)
        self.DV = S.dram("DV", [nt, 512], BF16)
        self.SZ = S.dram("SZ", [nt, 1024], F32)
        self.DT = S.dram("DT", [nt, 32], F32)
        self.HOF = S.dram("HOF", [nt, 512], F32)
        self.HOB = S.dram("HOB", [nt, 512], F32)
        self.MDA = S.dram("MDA", [nt, 512], BF16)
        self.XTM = S.dram("XTM", [nt, 1024], BF16)
        self.YF = S.dram("YF", [nt, 1024], F32)
        self.YB = S.dram("YB", [nt, 1024], F32)

        self.ident = S.sb("ident", [128, 128], BF16)
        identf = S.sb("identf", [128, 128], F32)
        self.identf = identf
        self.epsc = S.sb("epsc", [128, 1], F32)
        S.memset("pool", self.epsc.v(), EPS)
        self.onescol = S.sb("onescol", [128, 1], F32)
        S.memset("pool", self.onescol.v(), 1.0)
        S.memset("pool", identf.v(), 0.0)
        S.aselect(identf.v(), identf.v(), [[-1, 128]], ALU.not_equal, 1.0, 0, 1)
        S.copy("dve", self.ident.v(), identf.v())
        onesf = S.sb("onesf", [64, 64], F32)
        self.mask_f = S.sb("mask_f", [64, 64], BF16)
        self.mask_b = S.sb("mask_b", [64, 64], BF16)
        tmpm = S.sb("tmpm", [64, 64], F32)
        S.memset("pool", onesf.v(), 1.0)
        S.aselect(tmpm.v(), onesf.v(), [[1, 64]], ALU.is_ge, 0.0, 0, -1)
        S.copy("dve", self.mask_f.v(), tmpm.v())
        tmpm2 = S.sb("tmpm2", [64, 64], F32)
        S.aselect(tmpm2.v(), onesf.v(), [[-1, 64]], ALU.is_ge, 0.0, 0, 1)
        S.copy("dve", self.mask_b.v(), tmpm2.v())

        self.convert_weights(0, ["in"])
        for l in range(cfg.depth):
            last = (l == DEPTH - 1)
            self.phase_mod(l)
            if l == 0:
                self.convert_weights(0, ["o", "g", "u", "d"])
            if cfg.stop_after == ("mod", l):
                break
            self.phase_norm1(l)
            if cfg.stop_after == ("norm1", l):
                break
            self.phase_inproj(l)
            if cfg.stop_after == ("inproj", l):
                break
            if l + 1 < cfg.depth:
                self.convert_weights(l + 1, ["in", "o", "g", "u", "d"])
            self.phase_hgrn(l)
            if cfg.stop_after == ("hgrn", l):
                break
            self.phase_da(l)
            if cfg.stop_after == ("da", l):
                break
            self.phase_ssd(l)
            if cfg.stop_after == ("ssd", l):
                break
            self.phase_mixout(l)
            if cfg.stop_after == ("mixout", l):
                break
            self.phase_outproj(l)
            if cfg.stop_after == ("outproj", l):
                break
            self.phase_ffn(l)
            if cfg.stop_after == ("ffn", l):
                break
        for name, (src, shape, dt) in self.dbg.items():
            if name in cfg.debug:
                self.dbg_out(name, src, shape, dt)
        S.final_wait("sp", [o.buf for o in self.outs])

    def convert_weights(self, l, which):
        S, I = self.S, self.I
        if not hasattr(self, "_cvsems"):
            self._cvsems = [Buf("cv%d" % i) for i in range(4)]
            self._cvk = 0
        for name in which:
            if name in ("g", "u"):
                src, dst = (I["w_g"], self.WG[l]) if name == "g" else (I["w_u"], self.WU[l])
                for kc in range(16):
                    S.dma("pool", dst[:, :, kc, :].rearrange("f p n -> p f n"),
                          src[l, kc * 128:(kc + 1) * 128, :].rearrange("p (f n) -> p f n", n=512),
                          sembuf=self._cvsems[self._cvk % 4])
                    self._cvk += 1
                continue
            src, dst, rows = {"in": (I["w_in_ext"], self.WIN[l], D), "o": (I["w_out"], self.WO[l], D),
                              "d": (I["w_d"], self.WD[l], DFF)}[name]
            for r0 in range(0, rows, 128):
                S.dma("pool", dst[r0:r0 + 128, :], src[l, r0:r0 + 128, :], sembuf=self._cvsems[self._cvk % 4])
                self._cvk += 1

    def phase_mod(self, l):
        S, I = self.S, self.I
        S.push()
        cT = S.sb("cT", [128, 16, 2], F32)
        scT = S.sb("scT", [128, 16, 2], BF16)
        bada = S.sb("bada", [2, 6 * D], F32)
        modsb = S.sb("modsb", [2, 6 * D], F32)
        wt = [S.sb("wada%d" % i, [128, 16, 512], BF16) for i in range(2)]
        pm = [S.ps("pmod%d" % i, [128, 512], F32) for i in range(2)]
        for t in range(2):
            S.dma("sp", cT[:, :, t], I["cc"][t, :].rearrange("(k p) -> p k", p=128), slow=True)
        S.dma("sp", bada.v(), I["b_ada"][l:l + 1, :].partition_broadcast(2))
        S.act(scT.v(), cT.v(), AF.Silu)
        for j in range(24):
            w = wt[j % 2]
            S.dma("pool", w.v(), I["w_ada"][l, :, j * 512:(j + 1) * 512].rearrange("(k p) n -> p k n", p=128))
            p = pm[j % 2]
            for k in range(16):
                S.mm(p[0:2, :], scT[:, k, :], w[:, k, :], start=(k == 0), stop=(k == 15))
            S.tt("dve", modsb[:, j * 512:(j + 1) * 512], p[0:2, :], bada[:, j * 512:(j + 1) * 512], ALU.add)
        S.dma("sp", self.MOD[l].v(), modsb.v())
        self.dbg["mod%d" % l] = (self.MOD[l], [2, 6 * D], F32)
        S.pop()

    def _bc_row(self, dst, src_row):
        self.S.dma("sp", dst.v(), src_row.partition_broadcast(128))

    def _gain_shift(self, l, which, norm_key, t):
        S, I = self.S, self.I
        nw = S.sb("nw", [128, D], F32)
        self._bc_row(nw, I[norm_key][l:l + 1, :])
        gain = S.sb("gain", [128, D], F32)
        shift = S.sb("shift", [128, D], F32)
        base = which * 3 * D
        self._bc_row(shift, self.MOD[l][t:t + 1, base:base + D])
        self._bc_row(gain, self.MOD[l][t:t + 1, base + D:base + 2 * D])
        S.stt(gain.v(), gain.v(), 1.0, nw.v(), ALU.add, ALU.mult)
        return gain, shift

    def _gate(self, l, which, t):
        gate = self.S.sb("gate", [128, D], F32)
        base = which * 3 * D
        self._bc_row(gate, self.MOD[l][t:t + 1, base + 2 * D:base + 3 * D])
        return gate

    def _rms_mod_transpose(self, src_view, gain, shift, uT, j, tiles):
        S = self.S
        junk, ss, rstd, ub, ptr = tiles["junk"], tiles["ss"], tiles["rstd"], tiles["ub"], tiles["ptr"]
        S.act(junk.v(), src_view, AF.Square, accum_out=ss.v())
        S.act(rstd.v(), ss.v(), AF.Ln, scale=1.0 / D, bias=self.epsc.v())
        S.act(rstd.v(), rstd.v(), AF.Exp, scale=-0.5)
        S.stt(junk.v(), src_view, rstd[:, 0:1], gain.v(), ALU.mult, ALU.mult)
        S.tt("dve", ub.v(), junk.v(), shift.v(), ALU.add)
        for k in range(16):
            S.tr(ptr[:, k * 128:(k + 1) * 128], ub[:, k * 128:(k + 1) * 128], self.ident.v())
        S.copy("act", uT[:, :, j * 128:(j + 1) * 128], ptr.v().rearrange("p (k t) -> p k t", t=128))

    def phase_norm1(self, l):
        S, I, cfg = self.S, self.I, self.cfg
        S.push()
        g_lat, s_lat = self._gain_shift(l, 0, "norm1_w", 0)
        g_ctx, s_ctx = self._gain_shift(l, 0, "norm1_w", 1)
        ht = [S.sb("ht%d" % i, [128, D], F32) for i in range(2)]
        tiles = {"junk": S.sb("junk", [128, D], F32), "ss": S.sb("ss", [128, 1], F32),
                 "rstd": S.sb("rstd", [128, 1], F32), "ub": S.sb("ub", [128, D], BF16),
                 "ptr": S.ps("ptr", [128, D], BF16)}
        uTs = [S.sb("uT%d" % i, [128, 16, 512], BF16) for i in range(2)]
        n = 0
        for g, (t0, gsz) in enumerate(cfg.groups):
            uT = uTs[g % 2]
            gain, shift = (g_ctx, s_ctx) if g == 0 else (g_lat, s_lat)
            for j in range(gsz // 128):
                h = ht[n % 2]
                n += 1
                if l == 0:
                    src = I["xin"][t0 + j * 128:t0 + (j + 1) * 128, :]
                else:
                    src = self.H[g][j * 128:(j + 1) * 128, :]
                S.dma("sp", h.v(), src)
                self._rms_mod_transpose(h.v(), gain, shift, uT, j, tiles)
            S.dma("sp", self.UT[g][:, :, 0:gsz], uT[:, :, 0:gsz])
        S.pop()

    def phase_inproj(self, l):
        S, I, cfg = self.S, self.I, self.cfg
        nt = cfg.nt
        S.push()
        WIN = self.WIN[l]
        uTs = [S.sb("uTi%d" % i, [128, 16, 512], BF16) for i in range(2)]
        wts = [S.sb("wi%d" % i, [128, 16, 1024], BF16) for i in range(2)]
        rc = [S.sb("rc%d" % i, [128, 512], F32) for i in range(2)]
        rs = [S.sb("rs%d" % i, [128, 512], F32) for i in range(2)]
        stg = [S.sb("stg%d" % i, [128, 4, 512], F32) for i in range(2)]
        stgb = [S.sb("stgb%d" % i, [128, 4, 512], BF16) for i in range(2)]
        t1 = S.sb("ropet1", [128, 512], F32)
        t2 = S.sb("ropet2", [128, 512], F32)
        psA = [S.ps("psA%d" % i, [128, 512], F32) for i in range(4)]
        nstg = 0
        npsum = 0
        nw = 0
        for g, (t0, gsz) in enumerate(cfg.groups):
            uT = uTs[g % 2]
            S.dma("sp", uT[:, :, 0:gsz], self.UT[g][:, :, 0:gsz])
            S.dma("sp", rc[g % 2][:, 0:gsz], I["rope_c"][:, t0:t0 + gsz])
            S.dma("sp", rs[g % 2][:, 0:gsz], I["rope_s"][:, t0:t0 + gsz])
            RC, RS = rc[g % 2], rs[g % 2]
            nsub = gsz // 128
            for pair in range(8):
                w = wts[nw % 2]
                nw += 1
                c0 = pair * 1024
                ncols = min(1024, NCOL - c0)
                S.dma("sp", w[:, :, 0:ncols], WIN[:, c0:c0 + ncols].rearrange("(k p) n -> p k n", p=128))

                def fm_chunk(colofs, ps):
                    for k in range(16):
                        S.mm(ps[:, 0:gsz], w[:, k, colofs:colofs + 128], uT[:, k, 0:gsz], start=(k == 0), stop=(k == 15))

                def fm_block(half, func, dst, dst_c0, bf=False):
                    nonlocal nstg, npsum
                    sg = (stgb if bf else stg)[nstg % 2]
                    nstg += 1
                    for c in range(4):
                        ps = psA[npsum % 4]
                        npsum += 1
                        fm_chunk(half * 512 + c * 128, ps)
                        if func is None:
                            S.copy("dve", sg[:, c, 0:gsz], ps[:, 0:gsz])
                        else:
                            S.act(sg[:, c, 0:gsz], ps[:, 0:gsz], func)
                    S.dma("sp", dst[dst_c0:dst_c0 + 4, :, t0:t0 + gsz].rearrange("c p t -> p c t"), sg[:, :, 0:gsz])

                def rope_block(dst):
                    nonlocal nstg, npsum
                    sg = stgb[nstg % 2]
                    nstg += 1
                    for c in range(4):
                        ps1 = psA[npsum % 4]
                        ps2 = psA[(npsum + 1) % 4]
                        npsum += 2
                        fm_chunk(c * 128, ps1)
                        fm_chunk(512 + c * 128, ps2)
                        S.tt("dve", t1[:, 0:gsz], ps1[:, 0:gsz], RC[:, 0:gsz], ALU.mult)
                        S.tt("dve", t2[:, 0:gsz], ps2[:, 0:gsz], RS[:, 0:gsz], ALU.mult)
                        S.tt("dve", sg[:, c, 0:gsz], t1[:, 0:gsz], t2[:, 0:gsz], ALU.add)
                    S.dma("sp", dst[:, :, t0:t0 + gsz].rearrange("c p t -> p c t"), sg[:, :, 0:gsz])

                def tm_block(half, width, func, dst, dst_c0, bf):
                    nonlocal nstg, npsum
                    sg = (stgb if bf else stg)[nstg % 2]
                    nstg += 1
                    for s in range(nsub):
                        ps = psA[npsum % 4]
                        npsum += 1
                        for k in range(16):
                            S.mm(ps[:, 0:width], uT[:, k, s * 128:(s + 1) * 128], w[:, k, half * 512:half * 512 + width],
                                 start=(k == 0), stop=(k == 15))
                        if func is None:
                            S.copy("dve", sg[:, s, 0:width], ps[:, 0:width])
                        else:
                            S.act(sg[:, s, 0:width], ps[:, 0:width], func)
                    S.dma("sp", dst[t0:t0 + gsz, dst_c0:dst_c0 + width].rearrange("(s p) c -> p s c", p=128),
                          sg[:, 0:nsub, 0:width])

                if pair == 0:
                    fm_block(0, AF.Silu, self.HQT, 0)
                    fm_block(1, AF.Sigmoid, self.SFT, 0)
                elif pair == 1:
                    fm_block(0, AF.Sigmoid, self.SBT, 0)
                    fm_block(1, None, self.XBCT, 0)
                elif pair == 2:
                    rope_block(self.QT)
                elif pair == 3:
                    rope_block(self.KT)
                elif pair == 4:
                    fm_block(0, None, self.XBCT, 4)
                    fm_block(1, None, self.XBCT, 8)
                elif pair == 5:
                    tm_block(0, 512, None, self.HV, 0, True)
                    tm_block(1, 512, AF.Silu, self.HG, 0, False)
                elif pair == 6:
                    tm_block(0, 512, None, self.DV, 0, True)
                    tm_block(1, 512, AF.Silu, self.SZ, 0, False)
                elif pair == 7:
                    tm_block(0, 512, AF.Silu, self.SZ, 512, False)
                    tm_block(1, 32, None, self.DT, 0, False)
        for nm, t, shape, dt in (("hqt", self.HQT, [4, 128, nt], F32), ("sft", self.SFT, [4, 128, nt], F32),
                                 ("sbt", self.SBT, [4, 128, nt], F32), ("qt", self.QT, [4, 128, nt], BF16),
                                 ("kt", self.KT, [4, 128, nt], BF16), ("xbct", self.XBCT, [12, 128, nt], F32),
                                 ("hv", self.HV, [nt, 512], BF16), ("hg", self.HG, [nt, 512], F32),
                                 ("dv", self.DV, [nt, 512], BF16), ("sz", self.SZ, [nt, 1024], F32),
                                 ("dt", self.DT, [nt, 32], F32)):
            self.dbg["%s%d" % (nm, l)] = (t, shape, dt)
        S.pop()


    def _orders(self, L=64):
        nch = self.cfg.nt // L
        nctx = CTX // L
        fwd = list(range(nch))
        bwd = list(range(nctx - 1, -1, -1)) + list(range(nch - 1, nctx - 1, -1))
        return fwd, bwd

    def phase_hgrn(self, l):
        S, I, cfg = self.S, self.I, self.cfg
        LH = 32
        nt, nch = cfg.nt, cfg.nt // LH
        S.push()
        lbraw = S.sb("lbraw", [128, 2, 4], F32)
        for ll in range(2):
            S.dma("sp", lbraw[:, ll, :], I["hg_lb"][ll, :].rearrange("(h k) -> k h", k=128), slow=True)
        lb = S.sb("lb", [128, 4], F32)
        omlb = S.sb("omlb", [128, 4], F32)
        if l == 0:
            S.memset("dve", lb.v(), 0.0)
        else:
            S.tt("dve", lb.v(), lbraw[:, 1, :], lbraw[:, 0, :], ALU.subtract)
            S.act(lb.v(), lb.v(), AF.Sigmoid)
        S.ts("dve", omlb.v(), lb.v(), -1.0, ALU.mult, 1.0, ALU.add)
        ones = S.sb("ones", [128, nt], BF16)
        S.memset("pool", ones.v(), 1.0)
        Q = S.sb("hQ", [128, nt], F32)
        X1 = S.sb("hX1", [128, nt], F32)
        X2 = S.sb("hX2", [128, nt], F32)
        X3 = S.sb("hX3", [128, nt], F32)
        qt = [S.sb("hqt%d" % d, [128, nt], BF16) for d in range(2)]
        kt = [S.sb("hkt%d" % d, [128, nt], BF16) for d in range(2)]
        V = S.sb("hV", [LH, nch, 128], BF16)
        gg = [S.sb("hgg%d" % d, [128, nch], F32) for d in range(2)]
        rt = S.sb("hrt", [128, nch], F32)
        din = S.sb("hdin", [128, nch], F32)
        dout = S.sb("hdout", [128, nch], F32)
        Sf = [S.sb("hS%d" % d, [128, 128], F32) for d in range(2)]
        St = [S.sb("hSt%d" % d, [128, 128], F32) for d in range(2)]
        Sb = [S.sb("hSb%d" % d, [128, 128], BF16) for d in range(2)]
        attm = [[S.sb("hattm%d_%d" % (d, i), [LH, LH], BF16) for i in range(2)] for d in range(2)]
        ktT = [[S.sb("hktT%d_%d" % (d, i), [LH, 128], BF16) for i in range(2)] for d in range(2)]
        osb = [[S.sb("hosb%d_%d" % (d, i), [LH, 512], F32) for i in range(2)] for d in range(2)]
        ps_att_ = [S.ps("hpsatt%d" % d, [128, 512], F32) for d in range(2)]
        ps_att = [t[0:LH, 0:LH] for t in ps_att_]
        ps_kT_ = [S.ps("hpskT%d" % d, [128, 1024], BF16) for d in range(2)]
        ps_kT = [t[0:LH, 0:128] for t in ps_kT_]
        ps_o_ = [S.ps("hpso%d" % d, [128, 512], F32) for d in range(2)]
        ps_o = [t[0:LH, 0:128] for t in ps_o_]
        ps_dS_ = [S.ps("hpsdS%d" % d, [128, 512], F32) for d in range(2)]
        ps_dS = [t[:, 0:128] for t in ps_dS_]
        masks = [self.mask_f[0:LH, 0:LH], self.mask_b[0:LH, 0:LH]]
        orders = self._orders(LH)
        outs = [self.HOF, self.HOB]
        v3 = lambda t: t.v().rearrange("p (c l) -> p c l", l=LH)
        for h in range(4):
            S.dma("sp", Q.v(), self.HQT[h, :, :])
            S.dma("sp", V.v(), self.HV[:, h * 128:(h + 1) * 128].rearrange("(c p) v -> p c v", p=LH))
            for d in range(2):
                src = self.SFT if d == 0 else self.SBT
                S.dma("sp", X1.v(), src[h, :, :])
                S.ts("dve", X1.v(), X1.v(), omlb[:, h:h + 1], ALU.mult, lb[:, h:h + 1], ALU.add)
                S.ts("dve", X2.v(), X1.v(), -1.0, ALU.mult, 1.0, ALU.add)
                S.act(X1.v(), X1.v(), AF.Ln)
                S.scan(X3.v(), ones.v(), X1.v())
                if d == 1:
                    S.tt("dve", X3.v(), X1.v(), X3.v(), ALU.subtract)
                R3, L3 = v3(X3), v3(X1)
                i_en, i_ex = (0, LH - 1) if d == 0 else (LH - 1, 0)
                S.copy("dve", rt.v(), R3[:, :, LH // 2])
                S.tt("dve", din.v(), rt.v(), R3[:, :, i_en], ALU.subtract)
                S.tt("dve", din.v(), din.v(), L3[:, :, i_en], ALU.add)
                S.act(din.v(), din.v(), AF.Exp)
                S.tt("dve", dout.v(), R3[:, :, i_ex], rt.v(), ALU.subtract)
                S.act(dout.v(), dout.v(), AF.Exp)
                if d == 0:
                    S.tt("dve", gg[d][:, 0:nch - 1], dout[:, 0:nch - 1], din[:, 1:nch], ALU.mult)
                else:
                    S.tt("dve", gg[d][:, 1:nch], dout[:, 1:nch], din[:, 0:nch - 1], ALU.mult)
                    S.tt("dve", gg[d][:, 0:1], dout[:, 0:1], din[:, nch - 1:nch], ALU.mult)
                S.tt("dve", v3(X1), R3, rt.v().rearrange("p (c o) -> p c o", o=1).to_broadcast([128, nch, LH]), ALU.subtract)
                S.act(X3.v(), X1.v(), AF.Exp)
                S.tt("dve", qt[d].v(), Q.v(), X3.v(), ALU.mult)
                S.act(X3.v(), X1.v(), AF.Exp, scale=-1.0)
                S.tt("dve", kt[d].v(), X2.v(), X3.v(), ALU.mult)
                S.memset("dve", Sf[d].v(), 0.0)
                S.memset("pool", Sb[d].v(), 0.0)
            def prep(i, d):
                c = orders[d][i]
                c0 = c * LH
                S.mm(ps_att[d], kt[d][:, c0:c0 + LH], qt[d][:, c0:c0 + LH])
                S.tr(ps_kT[d], kt[d][:, c0:c0 + LH], self.ident.v())
                S.tt("dve", attm[d][i % 2].v(), ps_att[d], masks[d], ALU.mult)
                S.copy("act", ktT[d][i % 2].v(), ps_kT[d])

            def use(i, d):
                c = orders[d][i]
                c0 = c * LH
                j4 = i % 4
                slot = j4 if d == 0 else 3 - j4
                po = ps_o_[d][0:LH, slot * 128:(slot + 1) * 128]
                S.mm(po, attm[d][i % 2].v(), V[:, c, :], start=(j4 == 0), stop=False, skip=True)
                S.mm(po, qt[d][:, c0:c0 + LH], Sb[d].v(), start=False, stop=True, skip=True)
                if i < nch - 1:
                    S.mm(ps_dS[d], ktT[d][i % 2].v(), V[:, c, :])
                    S.tt("dve", St[d].v(), Sf[d].v(), ps_dS[d], ALU.add)
                    S.act(Sb[d].v(), St[d].v(), AF.Identity, scale=gg[d][:, c:c + 1])
                    S.ts("dve", Sf[d].v(), St[d].v(), gg[d][:, c:c + 1], ALU.mult)
                if j4 == 3:
                    o = osb[d][(i // 4) % 2]
                    S.copy("act", o.v(), ps_o_[d][0:LH, :])
                    base = min(orders[d][i - 3 + jj] for jj in range(4)) * LH
                    dst = outs[d][base:base + 4 * LH, h * 128:(h + 1) * 128].rearrange("(j p) v -> p j v", p=LH)
                    S.dma("sp", dst, o.v().rearrange("p (j v) -> p j v", v=128))
            for d in range(2):
                prep(0, d)
            for i in range(nch):
                for d in range(2):
                    if i + 1 < nch:
                        prep(i + 1, d)
                    use(i, d)
        self.dbg["hof%d" % l] = (self.HOF, [nt, 512], F32)
        self.dbg["hob%d" % l] = (self.HOB, [nt, 512], F32)
        S.pop()

    def phase_da(self, l):
        S, I, cfg = self.S, self.I, self.cfg
        nt, ntile, seq = cfg.nt, cfg.ntile, cfg.seq
        lam_init = 0.8 - 0.6 * math.exp(-0.3 * l)
        S.push()
        lamraw = S.sb("lamraw", [128, 256], F32)
        S.dma("sp", lamraw.v(), I["da_lambda"][l:l + 1, :].partition_broadcast(128))
        prod = S.sb("lamprod", [128, 2, 64], F32)
        lr = lamraw.v().rearrange("p (a b k) -> p a b k", a=2, b=2)
        S.tt("dve", prod.v(), lr[:, :, 0, :], lr[:, :, 1, :], ALU.mult)
        lsum = S.sb("lsum", [128, 2], F32)
        S.reduce(lsum.v(), prod.v())
        S.act(lsum.v(), lsum.v(), AF.Exp)
        neglam = S.sb("neglam", [128, 1], F32)
        S.tt("dve", neglam.v(), lsum[:, 1:2], lsum[:, 0:1], ALU.subtract)
        S.ts("dve", neglam.v(), neglam.v(), -lam_init, ALU.add)
        self.dbg_sb("neglam%d" % l, neglam.v(), [128, 1], F32)
        self.dbg_sb("lsum%d" % l, lsum.v(), [128, 2], F32)
        sw = S.sb("sublnw", [128, 128], F32)
        S.dma("sp", sw.v(), I["da_subln_w"][l:l + 1, :].partition_broadcast(128))
        S.ts("dve", sw.v(), sw.v(), 1.0 - lam_init, ALU.mult)
        QTh = S.sb("dQT", [128, nt], BF16)
        KTh = S.sb("dKT", [128, nt], BF16)
        Vh = S.sb("dV", [128, ntile, 128], BF16)
        NS = 3
        NP = 4
        ps_s = [S.ps("dps%d" % i, [128, 2, 512], F32) for i in range(NS)]
        Pb = [S.sb("dP%d" % i, [128, 2, 512], BF16) for i in range(NP)]
        ps_oT = [S.ps("dpoT%d" % c, [128, 512], F32) for c in range(2)]
        ps_tr = ps_s[0].v().rearrange("p a (b v) -> p a b v", v=128)
        ps_l = ps_s[1][:, 0, :]
        Pacc = [S.sb("dPacc%d" % c, [128, 512], F32) for c in range(2)]
        Pacc2 = [S.sb("dPacc2_%d" % c, [128, 512], F32) for c in range(2)]
        oT = [S.sb("doT%d" % c, [128, 512], F32) for c in range(2)]
        rc8 = S.sb("drc8", [128, 8], F32)
        t04 = S.sb("dt04", [128, 4, 128], F32)
        a4 = S.sb("da4", [128, 4, 128], F32)
        ss4d = S.sb("dss4", [128, 4], F32)
        stg = [S.sb("dstg%d" % i, [128, 4, 128], BF16) for i in range(2)]
        n = 0
        nst = 0
        qblocks = [(0, CTX, [0, 1])] + [(CTX + i * 512, 512, list(range(ntile))) for i in range(seq // 512)]
        for h in range(4):
            S.dma("sp", QTh.v(), self.QT[h, :, :])
            S.dma("sp", KTh.v(), self.KT[h, :, :])
            S.dma("sp", Vh.v(), self.DV[:, h * 128:(h + 1) * 128].rearrange("(kb p) v -> p kb v", p=128))
            for (q0, nq, kbs) in qblocks:
                nqs = nq // 128
                npair = len(kbs) // 2
                its = [(c, pi) for c in range(2) for pi in range(npair)]

                def score(it, m):
                    c, pi = it
                    for e in range(2):
                        kb = kbs[2 * pi + e]
                        S.mm(ps_s[m % NS][:, e, 0:nq], KTh[c * 64:(c + 1) * 64, kb * 128:(kb + 1) * 128],
                             QTh[c * 64:(c + 1) * 64, q0:q0 + nq])
                    S.act(Pb[m % NP][:, :, 0:nq], ps_s[m % NS][:, :, 0:nq], AF.Exp, scale=0.125)

                def pv(it, m):
                    c, pi = it
                    P = Pb[m % NP]
                    for e in range(2):
                        kb = kbs[2 * pi + e]
                        S.mm(ps_oT[c][:, 0:nq], Vh[:, kb, :], P[:, e, 0:nq],
                             start=(pi == 0 and e == 0), stop=(pi == npair - 1 and e == 1))
                    if pi == 0:
                        S.copy("dve", Pacc[c][:, 0:nq], P[:, 0, 0:nq])
                        S.copy("pool", Pacc2[c][:, 0:nq], P[:, 1, 0:nq])
                    else:
                        S.tt("dve", Pacc[c][:, 0:nq], Pacc[c][:, 0:nq], P[:, 0, 0:nq], ALU.add)
                        S.tt("pool", Pacc2[c][:, 0:nq], Pacc2[c][:, 0:nq], P[:, 1, 0:nq], ALU.add)
                    if pi == npair - 1:
                        S.tt("dve", Pacc[c][:, 0:nq], Pacc[c][:, 0:nq], Pacc2[c][:, 0:nq], ALU.add)
                LOOK = 2
                for j in range(min(LOOK, len(its))):
                    score(its[j], n + j)
                for j in range(len(its)):
                    if j + LOOK < len(its):
                        score(its[j + LOOK], n + j + LOOK)
                    pv(its[j], n + j)
                n += len(its)
                for c in range(2):
                    S.copy("act", oT[c][:, 0:nq], ps_oT[c][:, 0:nq])
                for c in range(2):
                    for qs in range(nqs):
                        S.tr(ps_tr[:, c, qs, :], oT[c][:, qs * 128:(qs + 1) * 128], self.identf.v())
                        S.mm(ps_l[:, c * 4 + qs:c * 4 + qs + 1], Pacc[c][:, qs * 128:(qs + 1) * 128], self.onescol.v(), skip=True)
                sg = stg[nst % 2]
                nst += 1
                bq = lambda v: v.rearrange("p (a o) -> p a o", o=1).to_broadcast([128, nqs, 128])
                S.recip(rc8.v(), ps_l[:, 0:8])
                S.ts("dve", rc8[:, 4:8], rc8[:, 4:8], neglam[:, 0:1], ALU.mult)
                S.tt("dve", t04[:, 0:nqs, :], ps_tr[:, 0, 0:nqs, :], bq(rc8[:, 0:nqs]), ALU.mult)
                S.tt("dve", a4[:, 0:nqs, :], ps_tr[:, 1, 0:nqs, :], bq(rc8[:, 4:4 + nqs]), ALU.mult)
                S.tt("dve", a4[:, 0:nqs, :], a4[:, 0:nqs, :], t04[:, 0:nqs, :], ALU.add)
                S.tt("dve", t04[:, 0:nqs, :], a4[:, 0:nqs, :], a4[:, 0:nqs, :], ALU.mult)
                S.reduce(ss4d[:, 0:nqs], t04[:, 0:nqs, :])
                S.act(ss4d[:, 0:nqs], ss4d[:, 0:nqs], AF.Ln, scale=1.0 / 128, bias=self.epsc.v())
                S.act(ss4d[:, 0:nqs], ss4d[:, 0:nqs], AF.Exp, scale=-0.5)
                S.tt("dve", a4[:, 0:nqs, :], a4[:, 0:nqs, :], bq(ss4d[:, 0:nqs]), ALU.mult)
                S.tt("dve", sg[:, 0:nqs, :], a4[:, 0:nqs, :],
                     sw.v().rearrange("p (o v) -> p o v", o=1).to_broadcast([128, nqs, 128]), ALU.mult)
                S.dma("sp", self.MDA[q0:q0 + nq, h * 128:(h + 1) * 128].rearrange("(s p) v -> p s v", p=128), sg[:, 0:nqs, :])
        self.dbg["mda%d" % l] = (self.MDA, [nt, 512], BF16)
        S.pop()

    def phase_ssd(self, l):
        S, I, cfg = self.S, self.I, self.cfg
        nt, nch, ntile = cfg.nt, cfg.nch, cfg.ntile
        S.push()
        one64 = S.sb("one64", [64, 128], F32)
        S.memset("pool", one64.v(), 1.0)
        U = S.sb("sU", [64, 64], F32)
        SU = S.sb("sSU", [64, 64], F32)
        S.aselect(U.v(), one64[:, 0:64], [[1, 64]], ALU.is_ge, 0.0, 0, -1)
        S.aselect(SU.v(), one64[:, 0:64], [[1, 64]], ALU.is_gt, 0.0, 0, -1)
        nU = S.sb("snU", [64, 64], F32)
        nSU = S.sb("snSU", [64, 64], F32)
        GT = S.sb("sGT", [64, 64], F32)
        GE = S.sb("sGE", [64, 64], F32)
        S.ts("dve", nU.v(), U.v(), -1.0, ALU.mult)
        S.ts("dve", nSU.v(), SU.v(), -1.0, ALU.mult)
        S.ts("dve", GT.v(), U.v(), -1.0, ALU.mult, 1.0, ALU.add)
        S.ts("dve", GE.v(), SU.v(), -1.0, ALU.mult, 1.0, ALU.add)
        DTs = S.sb("sDT", [64, nch, 32], F32)
        S.dma("sp", DTs.v(), self.DT.v().rearrange("(c p) j -> p c j", p=64))
        dtb = S.sb("sdtb", [64, 32], F32)
        Aa = S.sb("sA", [64, 32], F32)
        S.dma("sp", dtb.v(), I["dt_bias"][l:l + 1, :].partition_broadcast(64))
        S.dma("sp", Aa.v(), I["a_log"][l:l + 1, :].partition_broadcast(64))
        S.act(Aa.v(), Aa.v(), AF.Exp)
        S.ts("dve", Aa.v(), Aa.v(), -1.0, ALU.mult)
        bc32 = lambda t: t.v().rearrange("p (o j) -> p o j", o=1).to_broadcast([64, nch, 32])
        S.tt("dve", DTs.v(), DTs.v(), bc32(dtb), ALU.add)
        S.act(DTs.v(), DTs.v(), AF.Exp)
        S.act(DTs.v(), DTs.v(), AF.Ln, bias=1.0)
        dA = S.sb("sdA", [64, nch, 32], F32)
        S.tt("dve", dA.v(), DTs.v(), bc32(Aa), ALU.mult)
        T2 = S.sb("sT2", [64, nch, 32], F32)
        EOS = S.sb("sEOS", [64, nch, 32], F32)
        XW = S.sb("sXW", [64, nch, 32], F32)
        EOS_tmp = dA_tmp = None
        ETOT = S.sb("sETOT", [128, nch, 32], F32)
        BT = S.sb("sBT", [128, 2, nt], BF16)
        CT = S.sb("sCT", [128, 2, nt], BF16)
        cw = S.sb("scw", [128, 12, 3], F32)
        cbias = S.sb("scb", [128, 12], F32)
        S.push()
        pst = [S.ps("spst%d" % i, [128, 512], F32) for i in range(2)]
        npst = 0
        EOS_tmp = S.sb("sLDT", [64, nch, 32], F32)

        def table(dst, mats, func, npart=64):
            nonlocal npst
            for d in range(2):
                for cb in range(0, nch, 32):
                    ncb = min(32, nch - cb)
                    ps = pst[npst % 2]
                    npst += 1
                    pv = ps[0:npart, 0:ncb * 16].rearrange("p (c j) -> p c j", j=16)
                    S.mm(pv, mats[d], dA[:, cb:cb + ncb, d * 16:(d + 1) * 16])
                    if func is None:
                        S.copy("dve", dst[:, cb:cb + ncb, d * 16:(d + 1) * 16], pv)
                    else:
                        S.act(dst[:, cb:cb + ncb, d * 16:(d + 1) * 16], pv, func)
        table(T2, [nU.v(), SU.v()], None)
        table(EOS, [U.v(), GE.v()], AF.Exp)
        table(XW, [GT.v(), SU.v()], AF.Exp)
        table(ETOT, [one64.v(), one64.v()], AF.Exp, npart=128)
        S.tt("dve", XW.v(), XW.v(), DTs.v(), ALU.mult)
        S.act(EOS_tmp.v(), DTs.v(), AF.Ln)
        S.tt("dve", T2.v(), T2.v(), EOS_tmp.v(), ALU.add)
        S.pop()
        for j in range(3):
            S.dma("sp", cw[:, :, j], I["conv_w"][l, j, :].rearrange("(c p) -> p c", p=128), slow=True)
        S.dma("sp", cbias.v(), I["conv_b"][l, :].rearrange("(c p) -> p c", p=128), slow=True)
        S.push()
        XT = [S.sb("sXT%d" % i, [128, nt], BF16) for i in range(2)]
        xin = S.sb("sxin", [128, nt], F32)
        yv = S.sb("sy", [128, nt], F32)
        ptx = [S.ps("sptx%d" % i, [128, 1024], BF16) for i in range(2)]
        xst = [S.sb("sxst%d" % i, [128, 8, 128], BF16) for i in range(2)]
        nx = 0
        segs = [(0, CTX), (CTX, nt)]
        for ci in range(12):
            S.dma("sp", xin.v(), self.XBCT[ci, :, :])
            S.act(yv.v(), xin.v(), AF.Identity, scale=cw[:, ci, 1:2], bias=cbias[:, ci:ci + 1])
            for (s0, e0) in segs:
                S.stt(yv[:, s0 + 1:e0], xin[:, s0:e0 - 1], cw[:, ci, 0:1], yv[:, s0 + 1:e0], ALU.mult, ALU.add)
                S.stt(yv[:, s0:e0 - 1], xin[:, s0 + 1:e0], cw[:, ci, 2:3], yv[:, s0:e0 - 1], ALU.mult, ALU.add)
            if ci < 8:
                xt_ = XT[ci % 2]
                S.act(xt_.v(), yv.v(), AF.Silu)
                for i0 in range(0, ntile, 8):
                    nti = min(8, ntile - i0)
                    pt = ptx[nx % 2]
                    xs_ = xst[nx % 2]
                    nx += 1
                    for a in range(nti):
                        S.tr(pt[:, a * 128:(a + 1) * 128], xt_[:, (i0 + a) * 128:(i0 + a + 1) * 128], self.ident.v())
                    S.copy("act", xs_[:, 0:nti, :], pt[:, 0:nti * 128].rearrange("p (a c) -> p a c", c=128))
                    S.dma("sp", self.XTM[i0 * 128:(i0 + nti) * 128, ci * 128:(ci + 1) * 128].rearrange("(a p) c -> p a c", p=128),
                          xs_[:, 0:nti, :])
            elif ci < 10:
                S.act(BT[:, ci - 8, :], yv.v(), AF.Silu)
            else:
                S.act(CT[:, ci - 10, :], yv.v(), AF.Silu)
        S.pop()
        Gms = [S.sb("sGms%d" % d, [64, nch, 2, 64], BF16) for d in range(2)]
        Btm = S.sb("sBtm", [64, nch, 2, 128], BF16)
        S.push()
        ptb_ = S.ps("sptb", [128, 1024], BF16)
        ptb = ptb_[0:64, 0:512].rearrange("p (a n) -> p a n", n=128)
        for c in range(0, nch, 2):
            for cc in range(2):
                for g in range(2):
                    S.tr(ptb[:, cc * 2 + g, :], BT[:, g, (c + cc) * 64:(c + cc + 1) * 64], self.ident.v())
            S.copy("act", Btm[:, c:c + 2, :, :].rearrange("p a g n -> p (a g) n"), ptb)
        psg = [S.ps("spg%d" % i, [128, 512], F32) for i in range(2)]
        for c in range(0, nch, 4):
            ps = psg[(c // 4) % 2]
            pv = ps[0:64, :].rearrange("p (a g t) -> p a g t", a=4, g=2)
            for cc in range(4):
                for g in range(2):
                    c0 = (c + cc) * 64
                    S.mm(pv[:, cc, g, :], BT[:, g, c0:c0 + 64], CT[:, g, c0:c0 + 64])
            S.tt("dve", Gms[0][:, c:c + 4, :, :].rearrange("p a g t -> p (a g) t"), ps[0:64, :].rearrange("p (a t) -> p a t", t=64),
                 self.mask_f.v().rearrange("p (o t) -> p o t", o=1).to_broadcast([64, 8, 64]), ALU.mult)
            S.tt("dve", Gms[1][:, c:c + 4, :, :].rearrange("p a g t -> p (a g) t"), ps[0:64, :].rearrange("p (a t) -> p a t", t=64),
                 self.mask_b.v().rearrange("p (o t) -> p o t", o=1).to_broadcast([64, 8, 64]), ALU.mult)
        S.pop()
        ST = [S.sb("sST%d" % d, [128, 512], F32) for d in range(2)]
        STb = [S.sb("sSTb%d" % d, [128, 512], BF16) for d in range(2)]
        Xc = [[S.sb("sXc%d_%d" % (d, i), [64, 512], BF16) for i in range(2)] for d in range(2)]
        Xd = [S.sb("sXd%d" % d, [64, 512], BF16) for d in range(2)]
        Xw = [[S.sb("sXw%d_%d" % (d, i), [64, 512], BF16) for i in range(2)] for d in range(2)]
        RH = [S.sb("sRH%d" % d, [64, 8, 64], F32) for d in range(2)]
        EX = [S.sb("sEX%d" % d, [64, 8, 64], F32) for d in range(2)]
        MT = [[S.sb("sMT%d_%d" % (d, i), [64, 8, 64], BF16) for i in range(2)] for d in range(2)]
        ytmp = [S.sb("sytmp%d" % d, [64, 512], F32) for d in range(2)]
        ysb = [[S.sb("sys%d_%d" % (d, i), [64, 512], F32) for i in range(2)] for d in range(2)]
        ps_T = [S.ps("spT%d" % d, [128, 512], F32)[0:64, :] for d in range(2)]
        ps_Y = [S.ps("spY%d" % d, [128, 512], F32)[0:64, :] for d in range(2)]
        ps_Y2 = [S.ps("spY2%d" % d, [128, 512], F32)[0:64, :] for d in range(2)]
        ps_dS = [S.ps("spdS%d" % d, [128, 512], F32).v() for d in range(2)]
        orders = self._orders()
        outs = [self.YF, self.YB]
        Wd = [U, nSU]
        v8 = lambda v: v.rearrange("p (h q) -> p h q", q=64)
        hb = lambda v, np_: v.rearrange("p (h o) -> p h o", o=1).to_broadcast([np_, 8, 64])
        for g in range(2):
            for d in range(2):
                S.memset("dve", ST[d].v(), 0.0)
                S.memset("pool", STb[d].v(), 0.0)
            def prep(i, d):
                c = orders[d][i]
                c0 = c * 64
                hs = d * 16 + g * 8
                xc = Xc[d][i % 2]
                S.dma("sp", xc.v(), self.XTM[c0:c0 + 64, g * 512:(g + 1) * 512])
                S.tt("pool", RH[d].v(), Wd[d].v().rearrange("p (o t) -> p o t", o=1).to_broadcast([64, 8, 64]),
                     dA[:, c, hs:hs + 8].rearrange("p (h o) -> p h o", o=1).to_broadcast([64, 8, 64]), ALU.mult)
                S.mm(ps_T[d], one64[:, 0:64], RH[d].v().rearrange("p h t -> p (h t)"))
                S.tt("dve", EX[d].v(), v8(ps_T[d]), hb(T2[:, c, hs:hs + 8], 64), ALU.add)
                S.act(EX[d].v(), EX[d].v(), AF.Exp)
                S.stt(MT[d][i % 2].v(), EX[d].v(), 1e30,
                      Gms[d][:, c, g, :].rearrange("p (o t) -> p o t", o=1).to_broadcast([64, 8, 64]), ALU.min, ALU.mult)
                S.tt("pool", v8(Xw[d][i % 2].v()), v8(xc.v()), hb(XW[:, c, hs:hs + 8], 64), ALU.mult)

            def use(i, d):
                c = orders[d][i]
                c0 = c * 64
                hs = d * 16 + g * 8
                xc = Xc[d][i % 2]
                for h in range(8):
                    S.mm(ps_Y[d][:, h * 64:(h + 1) * 64], MT[d][i % 2][:, h, :], xc[:, h * 64:(h + 1) * 64], skip=True)
                S.mm(ps_Y2[d], CT[:, g, c0:c0 + 64], STb[d].v())
                if i < nch - 1:
                    S.mm(ps_dS[d], Btm[:, c, g, :], Xw[d][i % 2].v())
                    S.tt("pool", v8(ST[d].v()), v8(ST[d].v()), hb(ETOT[:, c, hs:hs + 8], 128), ALU.mult)
                    S.tt("dve", ST[d].v(), ST[d].v(), ps_dS[d], ALU.add)
                    S.copy("act", STb[d].v(), ST[d].v())
                y = ysb[d][i % 2]
                S.tt("dve", v8(ytmp[d].v()), v8(ps_Y2[d]), hb(EOS[:, c, hs:hs + 8], 64), ALU.mult)
                S.tt("dve", y.v(), ytmp[d].v(), ps_Y[d], ALU.add)
                S.dma("sp!", outs[d][c0:c0 + 64, g * 512:(g + 1) * 512], y.v())
            for d in range(2):
                prep(0, d)
            for i in range(nch):
                for d in range(2):
                    if i + 1 < nch:
                        prep(i + 1, d)
                    use(i, d)
        self.dbg["yf%d" % l] = (self.YF, [nt, 1024], F32)
        self.dbg["yb%d" % l] = (self.YB, [nt, 1024], F32)
        self.dbg["xtm%d" % l] = (self.XTM, [nt, 1024], BF16)
        S.pop()

    def phase_mixout(self, l):
        S, I, cfg = self.S, self.I, self.cfg
        last = (l == DEPTH - 1)
        S.push()
        hgw = S.sb("ghgw", [128, 128], F32)
        S.dma("sp", hgw.v(), I["hg_norm_w"][l:l + 1, :].partition_broadcast(128))
        ssmw = S.sb("gssmw", [128, 1024], F32)
        S.dma("sp", ssmw.v(), I["ssm_norm_w"][l:l + 1, :].partition_broadcast(128))
        dsk = S.sb("gdsk", [128, 16], F32)
        S.dma("sp", dsk.v(), I["ssm_d"][l:l + 1, :].partition_broadcast(128))
        NB = 2
        of = [S.sb("gof%d" % i, [128, 512], F32) for i in range(NB)]
        ob = [S.sb("gob%d" % i, [128, 512], F32) for i in range(NB)]
        gt = [S.sb("ggt%d" % i, [128, 512], F32) for i in range(NB)]
        yf = [S.sb("gyf%d" % i, [128, 1024], F32) for i in range(NB)]
        yb = [S.sb("gyb%d" % i, [128, 1024], F32) for i in range(NB)]
        sz = [S.sb("gsz%d" % i, [128, 1024], F32) for i in range(NB)]
        xt = [S.sb("gxt%d" % i, [128, 1024], BF16) for i in range(NB)]
        mt = [S.sb("gmt%d" % i, [128, D], BF16) for i in range(NB)]
        junk = S.sb("gjunk", [128, 512], F32)
        ss4 = S.sb("gss4", [128, 4], F32)
        ss2 = S.sb("gss2", [128, 2], F32)
        ptr = S.ps("gptr", [128, D], BF16)
        mTs = [S.sb("gmT%d" % i, [128, 16, 512], BF16) for i in range(2)]
        n = 0
        for g, (t0, gsz) in enumerate(cfg.groups):
            if last and g == 0:
                continue
            mT = mTs[g % 2]
            for j in range(gsz // 128):
                b = n % NB
                n += 1
                n0 = t0 + j * 128
                rows = slice(n0, n0 + 128)
                S.dma("sp", of[b].v(), self.HOF[rows, :])
                S.dma("sp", ob[b].v(), self.HOB[rows, :])
                S.dma("sp", gt[b].v(), self.HG[rows, :])
                S.dma("sp", yf[b].v(), self.YF[rows, :])
                S.dma("sp", yb[b].v(), self.YB[rows, :])
                S.dma("sp", sz[b].v(), self.SZ[rows, :])
                S.dma("sp", xt[b].v(), self.XTM[rows, :])
                S.dma("sp", mt[b][:, 512:1024], self.MDA[rows, :])
                S.tt("dve", of[b].v(), of[b].v(), ob[b].v(), ALU.add)
                for h in range(4):
                    S.act(junk[:, 0:128], of[b][:, h * 128:(h + 1) * 128], AF.Square, accum_out=ss4[:, h:h + 1])
                S.act(ss4.v(), ss4.v(), AF.Ln, scale=1.0 / 128, bias=self.epsc.v())
                S.act(ss4.v(), ss4.v(), AF.Exp, scale=-0.5)
                o3 = of[b].v().rearrange("p (h v) -> p h v", v=128)
                S.tt("dve", o3, o3, ss4.v().rearrange("p (h o) -> p h o", o=1).to_broadcast([128, 4, 128]), ALU.mult)
                S.tt("dve", o3, o3, hgw.v().rearrange("p (o v) -> p o v", o=1).to_broadcast([128, 4, 128]), ALU.mult)
                S.tt("dve", mt[b][:, 0:512], of[b].v(), gt[b].v(), ALU.mult)
                S.tt("dve", yf[b].v(), yf[b].v(), yb[b].v(), ALU.add)
                x3 = xt[b].v().rearrange("p (h q) -> p h q", q=64)
                y3 = yb[b].v().rearrange("p (h q) -> p h q", q=64)
                S.tt("dve", y3, x3, dsk.v().rearrange("p (h o) -> p h o", o=1).to_broadcast([128, 16, 64]), ALU.mult)
                S.tt("dve", yf[b].v(), yf[b].v(), yb[b].v(), ALU.add)
                S.tt("dve", yf[b].v(), yf[b].v(), sz[b].v(), ALU.mult)
                for gg in range(2):
                    S.act(junk.v(), yf[b][:, gg * 512:(gg + 1) * 512], AF.Square, accum_out=ss2[:, gg:gg + 1])
                S.act(ss2.v(), ss2.v(), AF.Ln, scale=1.0 / 512, bias=self.epsc.v())
                S.act(ss2.v(), ss2.v(), AF.Exp, scale=-0.5)
                yg = yf[b].v().rearrange("p (g v) -> p g v", v=512)
                S.tt("dve", yg, yg, ss2.v().rearrange("p (g o) -> p g o", o=1).to_broadcast([128, 2, 512]), ALU.mult)
                S.tt("dve", mt[b][:, 1024:2048], yf[b].v(), ssmw.v(), ALU.mult)
                for k in range(16):
                    S.tr(ptr[:, k * 128:(k + 1) * 128], mt[b][:, k * 128:(k + 1) * 128], self.ident.v())
                S.copy("act", mT[:, :, j * 128:(j + 1) * 128], ptr.v().rearrange("p (k t) -> p k t", t=128))
            S.dma("sp", self.MT[g][:, :, 0:gsz], mT[:, :, 0:gsz])
            self.dbg["mt%d_%d" % (l, g)] = (self.MT[g], [128, 16, 512], BF16)
        S.pop()

    def phase_outproj(self, l):
        S, I, cfg = self.S, self.I, self.cfg
        last = (l == DEPTH - 1)
        for kind in (1, 0):
            if kind == 1 and last:
                continue
            S.push()
            gain, shift = self._gain_shift(l, 1, "norm2_w", kind)
            gate = self._gate(l, 0, kind)
            wo = [S.sb("owo%d" % i, [128, 16, 512], BF16) for i in range(2)]
            mT = S.sb("omT", [128, 16, 512], BF16)
            h1 = S.sb("oh1", [128, 4, D], F32)
            tmp = S.sb("otmp", [128, 512], F32)
            tiles = {"junk": S.sb("ojunk", [128, D], F32), "ss": S.sb("oss", [128, 1], F32),
                     "rstd": S.sb("orstd", [128, 1], F32), "ub": S.sb("oub", [128, D], BF16),
                     "ptr": S.ps("optr", [128, D], BF16)}
            uTs = [S.sb("ouT%d" % i, [128, 16, 512], BF16) for i in range(2)]
            pso = [S.ps("opso%d" % i, [128, 512], F32) for i in range(4)]
            npso = 0
            nwo = 0
            for g, (t0, gsz) in enumerate(cfg.groups):
                if (g == 0) != (kind == 1):
                    continue
                nsub = gsz // 128
                S.dma("sp", mT[:, :, 0:gsz], self.MT[g][:, :, 0:gsz])
                if l == 0:
                    hsrc = I["xin"][t0:t0 + gsz, :]
                else:
                    hsrc = self.H[g][0:gsz, :]
                S.dma("sp", h1[:, 0:nsub, :], hsrc.rearrange("(s p) c -> p s c", p=128))
                for cb in range(4):
                    w = wo[nwo % 2]
                    nwo += 1
                    S.dma("sp", w.v(), self.WO[l][:, cb * 512:(cb + 1) * 512].rearrange("(k p) n -> p k n", p=128))
                    for s in range(nsub):
                        ps = pso[npso % 4]
                        npso += 1
                        for k in range(16):
                            S.mm(ps.v(), mT[:, k, s * 128:(s + 1) * 128], w[:, k, :], start=(k == 0), stop=(k == 15))
                        S.tt("dve", tmp.v(), ps.v(), gate[:, cb * 512:(cb + 1) * 512], ALU.mult)
                        S.tt("dve", h1[:, s, cb * 512:(cb + 1) * 512], h1[:, s, cb * 512:(cb + 1) * 512], tmp.v(), ALU.add)
                uT = uTs[g % 2]
                for s in range(nsub):
                    S.dma("sp", self.H1[g][s * 128:(s + 1) * 128, :], h1[:, s, :])
                    self._rms_mod_transpose(h1[:, s, :], gain, shift, uT, s, tiles)
                S.dma("sp", self.U2T[g][:, :, 0:gsz], uT[:, :, 0:gsz])
            S.pop()

    def phase_ffn(self, l):
        S, I, cfg = self.S, self.I, self.cfg
        last = (l == DEPTH - 1)
        FB = 512
        nfb = DFF // FB
        for kind in (1, 0):
            if kind == 1 and last:
                continue
            S.push()
            gate = self._gate(l, 1, kind)
            if last:
                fnw = S.sb("ffnw", [128, D], F32)
                self._bc_row(fnw, I["final_norm_w"][0:1, :])
            uTs = [S.sb("fuT%d" % i, [128, 16, 512], BF16) for i in range(1)]
            wg = [S.sb("fwg%d" % i, [128, 16, FB], BF16) for i in range(2)]
            wu = [S.sb("fwu%d" % i, [128, 16, FB], BF16) for i in range(2)]
            wd = [S.sb("fwd%d" % i, [128, FB // 128, D], BF16) for i in range(2)]
            actT = [S.sb("fact%d" % i, [128, FB // 128, 512], BF16) for i in range(2)]
            sg = [S.sb("fsg%d" % i, [128, 512], F32) for i in range(2)]
            acc = S.sb("facc", [128, 4, D], F32)
            h1t = [S.sb("fh1%d" % i, [128, D], F32) for i in range(1)]
            tmp = S.sb("ftmp", [128, D], F32)
            ss = S.sb("fss", [128, 1], F32)
            rstd = S.sb("frstd", [128, 1], F32)
            psg = [S.ps("fpsg%d" % i, [128, 512], F32) for i in range(2)]
            psu = [S.ps("fpsu%d" % i, [128, 512], F32) for i in range(2)]
            psd = [S.ps("fpsd%d" % i, [128, 512], F32) for i in range(4)]
            npg = 0
            npd = 0
            nwb = 0
            nh = 0
            for g, (t0, gsz) in enumerate(cfg.groups):
                if (g == 0) != (kind == 1):
                    continue
                nsub = gsz // 128
                uT = uTs[0]
                S.dma("sp", uT[:, :, 0:gsz], self.U2T[g][:, :, 0:gsz])
                def GU(fb):
                    nonlocal npg
                    b = fb % 2
                    f0 = fb * FB
                    S.dma("sp", wg[b].v(), self.WG[l][fb, :, :, :])
                    S.dma("sp", wu[b].v(), self.WU[l][fb, :, :, :])
                    S.dma("sp", wd[b].v(), self.WD[l][f0:f0 + FB, :].rearrange("(c p) n -> p c n", p=128))
                    at = actT[b]
                    for c in range(FB // 128):
                        pg = psg[npg % 2]
                        pu = psu[npg % 2]
                        sgt = sg[npg % 2]
                        npg += 1
                        for k in range(16):
                            S.mm(pg[:, 0:gsz], wg[b][:, k, c * 128:(c + 1) * 128], uT[:, k, 0:gsz], start=(k == 0), stop=(k == 15))
                        for k in range(16):
                            S.mm(pu[:, 0:gsz], wu[b][:, k, c * 128:(c + 1) * 128], uT[:, k, 0:gsz], start=(k == 0), stop=(k == 15))
                        S.act(sgt[:, 0:gsz], pg[:, 0:gsz], AF.Silu)
                        S.tt("dve", at[:, c, 0:gsz], sgt[:, 0:gsz], pu[:, 0:gsz], ALU.mult)

                def DN(fb):
                    nonlocal npd
                    b = fb % 2
                    at = actT[b]
                    for s in range(nsub):
                        for cb in range(4):
                            pd = psd[npd % 4]
                            npd += 1
                            for c in range(FB // 128):
                                S.mm(pd.v(), at[:, c, s * 128:(s + 1) * 128], wd[b][:, c, cb * 512:(cb + 1) * 512],
                                     start=(c == 0), stop=(c == FB // 128 - 1))
                            dst = acc[:, s, cb * 512:(cb + 1) * 512]
                            if fb == 0:
                                S.copy("dve", dst, pd.v())
                            else:
                                S.tt("dve", dst, dst, pd.v(), ALU.add)
                GU(0)
                for fb in range(nfb):
                    if fb + 1 < nfb:
                        GU(fb + 1)
                    DN(fb)
                for s in range(nsub):
                    h1 = h1t[0]
                    nh += 1
                    S.dma("sp", h1.v(), self.H1[g][s * 128:(s + 1) * 128, :])
                    S.tt("dve", tmp.v(), acc[:, s, :], gate.v(), ALU.mult)
                    S.tt("dve", h1.v(), h1.v(), tmp.v(), ALU.add)
                    if not last:
                        S.dma("sp", self.H[g][s * 128:(s + 1) * 128, :], h1.v())
                        self.dbg["h%d_%d" % (l, g)] = (self.H[g], [512, D], F32)
                    else:
                        S.act(tmp.v(), h1.v(), AF.Square, accum_out=ss.v())
                        S.act(rstd.v(), ss.v(), AF.Ln, scale=1.0 / D, bias=self.epsc.v())
                        S.act(rstd.v(), rstd.v(), AF.Exp, scale=-0.5)
                        S.stt(h1.v(), h1.v(), rstd[:, 0:1], fnw.v(), ALU.mult, ALU.mult)
                        r0 = t0 - CTX + s * 128
                        S.dma("sp", self.Y[r0:r0 + 128, :], h1.v())
            S.pop()

def _rope_tables(seq):
    half = 32
    inv = (1.0 / (10000.0 ** (np.arange(0, half, 2, dtype=np.float32) / np.float32(half)))).astype(np.float32)
    rows = seq // GRID_W
    r = np.repeat(np.arange(rows, dtype=np.float32), GRID_W)
    col = np.tile(np.arange(GRID_W, dtype=np.float32), rows)
    ar = (r[:, None] * inv[None, :]).astype(np.float32)
    ac = (col[:, None] * inv[None, :]).astype(np.float32)
    cr, sr, cc_, sc_ = np.cos(ar), np.sin(ar), np.cos(ac), np.sin(ac)
    C64 = np.concatenate([cr, cr, cc_, cc_], axis=1)
    S64 = np.concatenate([-sr, sr, -sc_, sc_], axis=1)
    C = np.concatenate([C64, C64], axis=1).T
    Sg = np.concatenate([S64, S64], axis=1).T
    Cf = np.concatenate([np.ones((128, CTX), np.float32), C.astype(np.float32)], axis=1)
    Sf = np.concatenate([np.zeros((128, CTX), np.float32), Sg.astype(np.float32)], axis=1)
    return np.ascontiguousarray(Cf), np.ascontiguousarray(Sf)


def _w_in_ext(w_in):
    sl = lambda a, b: w_in[:, :, a:b]
    hq, ff, fb, hi, hg = sl(0, 512), sl(512, 1024), sl(1024, 1536), sl(1536, 2048), sl(2048, 2560)
    dq, dk, dv = sl(2560, 3072), sl(3072, 3584), sl(3584, 4096)
    z, xbc, dtf, dtb = sl(4096, 5120), sl(5120, 6656), sl(6656, 6672), sl(6672, 6688)
    perm64 = np.concatenate([np.arange(16, 32), np.arange(0, 16), np.arange(48, 64), np.arange(32, 48)])
    perm = np.concatenate([blk * 64 + perm64 for blk in range(8)])
    dqs, dks = dq[:, :, perm], dk[:, :, perm]
    parts = [hq, ff, fb, xbc[:, :, 0:512], dq, dqs, dk, dks, xbc[:, :, 512:1024], xbc[:, :, 1024:1536],
             hi, hg, dv, z[:, :, 0:512], z[:, :, 512:1024], dtf, dtb]
    out = np.concatenate(parts, axis=2)
    assert out.shape[2] == NCOL
    return np.ascontiguousarray(out)


def host_inputs(cfg, b, inputs, shared):
    x, ctx, c, c_ctx = inputs["x"], inputs["ctx"], inputs["c"], inputs["c_ctx"]
    m = dict(shared)
    if b is None:
        m["xin"] = np.zeros((cfg.nt, D), np.float32)
        m["cc"] = np.zeros((2, D), np.float32)
        m["b_ada"] = np.zeros_like(shared["b_ada"])
        m["conv_b"] = np.zeros_like(shared["conv_b"])
        return m
    m["xin"] = np.ascontiguousarray(np.concatenate([ctx[b], x[b]], axis=0))
    m["cc"] = np.ascontiguousarray(np.stack([c[b], c_ctx], axis=0))
    return m


def shared_inputs(cfg, inputs):
    rc, rs = _rope_tables(cfg.seq)
    f = lambda a: np.ascontiguousarray(np.asarray(a, dtype=np.float32))
    return {
        "w_ada": f(inputs["w_ada"]), "b_ada": f(inputs["b_ada"]),
        "norm1_w": f(inputs["norm1_w"]), "norm2_w": f(inputs["norm2_w"]),
        "final_norm_w": f(inputs["final_norm_w"]).reshape(1, D),
        "w_in_ext": _w_in_ext(f(inputs["w_in"])),
        "hg_lb": f(inputs["hg_lb_logits"]), "hg_norm_w": f(inputs["hg_norm_w"]),
        "da_lambda": f(inputs["da_lambda"]).reshape(DEPTH, 256), "da_subln_w": f(inputs["da_subln_w"]),
        "conv_w": f(inputs["ssm_conv_w"]), "conv_b": f(inputs["ssm_conv_b"]),
        "dt_bias": f(inputs["ssm_dt_bias"]).reshape(DEPTH, 32), "a_log": f(inputs["ssm_a_log"]).reshape(DEPTH, 32),
        "ssm_d": f(inputs["ssm_d"]), "ssm_norm_w": f(inputs["ssm_norm_w"]),
        "w_out": f(inputs["w_out"]), "w_g": f(inputs["w_ffn_gate"]), "w_u": f(inputs["w_ffn_up"]),
        "w_d": f(inputs["w_ffn_down"]),
        "rope_c": rc, "rope_s": rs,
    }


def kernel(**inputs):
    inputs = {k: np.asarray(v) for k, v in inputs.items()}
    B, seq = inputs["x"].shape[0], inputs["x"].shape[1]
    cfg = Cfg(seq=seq)
    nc = build_program(cfg)
    shared = shared_inputs(cfg, inputs)
    owner = {core: i for i, core in enumerate(ACTIVE_CORES[:B])}
    in_maps = [host_inputs(cfg, owner.get(core), inputs, shared) for core in range(N_CORES)]
    res = run_bass_kernel_spmd(nc, in_maps, core_ids=list(range(N_CORES)))
    out = np.stack([res.results[ACTIVE_CORES[b]]["y"] for b in range(B)], axis=0)
    return out.astype(np.float32)
```

```python
import math
from contextlib import ExitStack

import numpy as np
import concourse.bass as bass
import concourse.mybir as mybir
from concourse.bass_utils import run_bass_kernel_spmd

F32 = mybir.dt.float32
BF16 = mybir.dt.bfloat16
AF = mybir.ActivationFunctionType
ALU = mybir.AluOpType

D = 2048
DEPTH = 2
CTX = 256
GRID_W = 64
EPS = 1e-6
DFF = 5632
NCOL = 7712
N_CORES = 8
SAME_ENGINE_SYNC = True
STORES_ON_POOL = True
ACTIVE_CORES = (0, 1, 4, 5)


class Slot:
    __slots__ = ("sem", "cnt")

    def __init__(self, sem):
        self.sem = sem
        self.cnt = 0


class Buf:
    __slots__ = ("name", "lw", "rd", "slot")

    def __init__(self, name):
        self.name = name
        self.lw = {}
        self.rd = {}
        self.slot = None


class View:
    __slots__ = ("ap", "buf")

    def __init__(self, ap, buf):
        self.ap = ap
        self.buf = buf

    def __getitem__(self, k):
        return View(self.ap[k], self.buf)

    def rearrange(self, s, **kw):
        return View(self.ap.rearrange(s, **kw), self.buf)

    def to_broadcast(self, shape):
        return View(self.ap.to_broadcast(list(shape)), self.buf)

    def partition_broadcast(self, n):
        return View(self.ap.partition_broadcast(n), self.buf)

    def bitcast(self, dt):
        return View(self.ap.bitcast(dt), self.buf)


class TT:
    def __init__(self, handle, name, is_ap=False):
        self.h = handle
        self.buf = Buf(name)
        self.is_ap = is_ap

    def __getitem__(self, k):
        return View(self.h[k], self.buf)

    def v(self):
        return View(self.h[:] if not self.is_ap else self.h, self.buf)


def _aps(x):
    return x.ap if isinstance(x, View) else x


def _bufs(*xs):
    out = []
    for x in xs:
        if isinstance(x, View) and x.buf is not None and x.buf not in out:
            out.append(x.buf)
    return out


class Sched:
    ENGS = ("pe", "act", "dve", "pool", "sp")

    def __init__(self, nc, stack, same_engine_sync=True):
        self.nc = nc
        self.stacks = [stack]
        self.prog = {e: [] for e in self.ENGS}
        self.sem = {}
        self.tick = {}
        for e in ("pe", "act", "dve", "pool"):
            self.sem[e] = stack.enter_context(nc.semaphore("s_" + e))
            self.tick[e] = 0
        self.seen = {e: {} for e in self.ENGS}
        self.same = same_engine_sync
        self.slots = []
        self.free_slots = []
        self.scope_bufs = [[]]
        self.ninst = 0
        self.base = stack
        self.uid = 0

    def push(self):
        st = ExitStack()
        self.stacks.append(st)
        self.scope_bufs.append([])
        return st

    def pop(self):
        self.barrier()
        st = self.stacks.pop()
        st.close()
        for b in self.scope_bufs.pop():
            if b.slot is not None:
                self.free_slots.append(b.slot)
                b.slot = None

    def sb(self, name, shape, dtype):
        self.uid += 1
        nm = "%s_%d" % (name, self.uid)
        h = self.stacks[-1].enter_context(self.nc.sbuf_tensor(nm, list(shape), dtype))
        t = TT(h, nm)
        self.scope_bufs[-1].append(t.buf)
        return t

    def ps(self, name, shape, dtype):
        self.uid += 1
        nm = "%s_%d" % (name, self.uid)
        h = self.stacks[-1].enter_context(self.nc.psum_tensor(nm, list(shape), dtype))
        return TT(h, nm)

    def dram(self, name, shape, dtype, kind="Internal"):
        h = self.nc.dram_tensor(name, list(shape), dtype, kind=kind)
        return TT(h.ap(), name, is_ap=True)

    def dslot(self, buf):
        if buf.slot is None:
            if self.free_slots:
                buf.slot = self.free_slots.pop()
            else:
                buf.slot = Slot(self.base.enter_context(self.nc.semaphore("d%d" % len(self.slots))))
                self.slots.append(buf.slot)
        return buf.slot

    def _needs(self, eng, reads, writes, partial, is_dma=False):
        needs = {}

        def add(d):
            for s, v in d.items():
                if needs.get(s, 0) < v:
                    needs[s] = v
        for b in reads:
            add(b.lw)
        for b in writes:
            add(b.rd)
            if not partial:
                add(b.lw)
        waits = []
        seen = self.seen[eng]
        own = self.sem.get(eng)
        for s, v in needs.items():
            if s is own and not is_dma and (eng == "pe" or not self.same):
                continue
            if seen.get(s, 0) < v:
                seen[s] = v
                waits.append((s, v))
        return waits

    def _commit(self, ev, reads, writes, partial):
        s, v = ev
        for b in reads:
            if b.rd.get(s, 0) < v:
                b.rd[s] = v
        for b in writes:
            if partial:
                if b.lw.get(s, 0) < v:
                    b.lw[s] = v
            else:
                b.lw = {s: v}
                b.rd = {}

    def op(self, eng, fn, reads=(), writes=(), partial=False):
        waits = self._needs(eng, reads, writes, partial)
        self.tick[eng] += 1
        ev = (self.sem[eng], self.tick[eng])
        self.prog[eng].append((waits, fn, ev[0], 1))
        self._commit(ev, reads, writes, partial)
        self.ninst += 1
        return ev

    def dma(self, q, out, in_, sembuf=None, partial=None, slow=False):
        ob, ib = out.buf, in_.buf
        o_ap, i_ap = out.ap, in_.ap
        o_is_dram = "DRAM" in str(o_ap.space).upper() or "HBM" in str(o_ap.space).upper()
        i_is_dram = "DRAM" in str(i_ap.space).upper() or "HBM" in str(i_ap.space).upper()
        if q == "sp" and o_is_dram and not i_is_dram and STORES_ON_POOL:
            q = "pool"
        if q == "sp!":
            q = "sp"
        if sembuf is None:
            o_dram = "DRAM" in str(o_ap.space).upper() or "HBM" in str(o_ap.space).upper()
            sembuf = ib if o_dram else ob
        if partial is None:
            partial = "DRAM" in str(o_ap.space).upper() or "HBM" in str(o_ap.space).upper()
        slot = self.dslot(sembuf)
        sem = slot.sem
        reads, writes = [ib], [ob]
        waits = self._needs(q, reads, writes, partial, True)
        prev = 16 * slot.cnt
        if prev and self.seen[q].get(sem, 0) < prev:
            self.seen[q][sem] = prev
            waits.append((sem, prev))
        slot.cnt += 1
        ev = (sem, 16 * slot.cnt)
        kw = {"allow_slow_non_contiguous": True} if slow else {}
        self.prog[q].append((waits, lambda e: e.dma_start(out=o_ap, in_=i_ap, **kw), sem, 16))
        self._commit(ev, reads, writes, partial)
        self.ninst += 1
        return ev

    def barrier(self):
        allv = [(self.sem[e], self.tick[e]) for e in ("pe", "act", "dve", "pool") if self.tick[e]]
        allv += [(sl.sem, 16 * sl.cnt) for sl in self.slots if sl.cnt]
        for e in self.ENGS:
            waits = []
            seen = self.seen[e]
            for s, v in allv:
                if s is self.sem.get(e):
                    continue
                if seen.get(s, 0) < v:
                    seen[s] = v
                    waits.append((s, v))
            if waits:
                self.prog[e].append((waits, None, None, 0))

    def act(self, out, in_, func, scale=None, bias=None, accum_out=None, eng="act"):
        kw = {}
        rd = _bufs(in_)
        wr = _bufs(out)
        if scale is not None:
            kw["scale"] = _aps(scale)
            rd += _bufs(scale)
        if bias is not None:
            kw["bias"] = _aps(bias)
            rd += _bufs(bias)
        if accum_out is not None:
            kw["accum_out"] = _aps(accum_out)
            wr += _bufs(accum_out)
        o, i = out.ap, in_.ap
        return self.op("act", lambda e: e.activation(out=o, in_=i, func=func, **kw), rd, wr)

    def tt(self, eng, out, in0, in1, op):
        o, a, b = out.ap, in0.ap, in1.ap
        return self.op(eng, lambda e: e.tensor_tensor(out=o, in0=a, in1=b, op=op), _bufs(in0, in1), _bufs(out))

    def ts(self, eng, out, in0, s1, op0, s2=None, op1=None):
        o, a = out.ap, in0.ap
        x1, x2 = _aps(s1), _aps(s2)
        kw = {}
        if op1 is not None:
            kw["op1"] = op1
        return self.op(eng, lambda e: e.tensor_scalar(out=o, in0=a, scalar1=x1, scalar2=x2, op0=op0, **kw),
                       _bufs(in0, s1, s2), _bufs(out))

    def stt(self, out, in0, scalar, in1, op0, op1, eng="dve"):
        o, a, b = out.ap, in0.ap, in1.ap
        sc = _aps(scalar)
        return self.op(eng, lambda e: e.scalar_tensor_tensor(out=o, in0=a, scalar=sc, in1=b, op0=op0, op1=op1),
                       _bufs(in0, scalar, in1), _bufs(out))

    def copy(self, eng, out, in_):
        o, i = out.ap, in_.ap
        if eng == "act":
            return self.op("act", lambda e: e.activation(out=o, in_=i, func=AF.Copy), _bufs(in_), _bufs(out))
        return self.op(eng, lambda e: e.tensor_copy(out=o, in_=i), _bufs(in_), _bufs(out))

    def memset(self, eng, out, val):
        o = out.ap
        return self.op(eng, lambda e: e.memset(o, val), [], _bufs(out))

    def recip(self, out, in_, eng="dve"):
        o, i = out.ap, in_.ap
        return self.op(eng, lambda e: e.reciprocal(out=o, in_=i), _bufs(in_), _bufs(out))

    def ttr(self, out, in0, in1, accum_out, op0=ALU.mult, op1=ALU.add, scale=1.0, scalar=0.0):
        o, a, b, acc = out.ap, in0.ap, in1.ap, accum_out.ap
        return self.op("dve", lambda e: e.tensor_tensor_reduce(out=o, in0=a, in1=b, scale=scale, scalar=scalar,
                                                               op0=op0, op1=op1, accum_out=acc),
                       _bufs(in0, in1), _bufs(out, accum_out))

    def scan(self, out, d0, d1, initial=0.0, op0=ALU.mult, op1=ALU.add):
        o, a, b = out.ap, d0.ap, d1.ap
        ini = _aps(initial)
        return self.op("dve", lambda e: e.tensor_tensor_scan(out=o, data0=a, data1=b, initial=ini, op0=op0, op1=op1),
                       _bufs(d0, d1, initial), _bufs(out))

    def reduce(self, out, in_, op=ALU.add, eng="dve"):
        o, i = out.ap, in_.ap
        return self.op(eng, lambda e: e.tensor_reduce(out=o, in_=i, axis=mybir.AxisListType.X, op=op),
                       _bufs(in_), _bufs(out))

    def mm(self, out, lhsT, rhs, start=True, stop=True, skip=False):
        o, a, b = out.ap, lhsT.ap, rhs.ap
        kw = {"skip_group_check": True} if skip else {}
        return self.op("pe", lambda e: e.matmul(o, a, b, start=start, stop=stop, **kw), _bufs(lhsT, rhs), _bufs(out))

    def tr(self, out, in_, ident):
        o, a, b = out.ap, in_.ap, ident.ap
        return self.op("pe", lambda e: e.transpose(out=o, in_=a, identity=b), _bufs(in_, ident), _bufs(out))

    def aselect(self, out, in_, pattern, cmp, fill, base, cm):
        o, i = out.ap, in_.ap
        return self.op("pool", lambda e: e.affine_select(out=o, in_=i, pattern=pattern, compare_op=cmp, fill=fill,
                                                         base=base, channel_multiplier=cm), _bufs(in_), _bufs(out))

    def final_wait(self, eng, bufs):
        needs = {}
        for b in bufs:
            for s, v in b.lw.items():
                if needs.get(s, 0) < v:
                    needs[s] = v
        self.prog[eng].append((list(needs.items()), None, None, 0))

    def emit(self):
        nc = self.nc
        prog = self.prog

        def run(engine, items):
            for waits, fn, sem, inc in items:
                for s, v in waits:
                    engine.wait_ge(s, v)
                if fn is not None:
                    fn(engine).then_inc(sem, inc)

        with nc.Block() as block:
            @block.tensor
            def _(e):
                run(e, prog["pe"])

            @block.scalar
            def _(e):
                run(e, prog["act"])

            @block.vector
            def _(e):
                run(e, prog["dve"])

            @block.gpsimd
            def _(e):
                run(e, prog["pool"])

            @block.sync
            def _(e):
                run(e, prog["sp"])


class Cfg:
    def __init__(self, seq=4096, debug=(), stop_after=None, depth=DEPTH):
        self.seq = seq
        self.nt = CTX + seq
        self.debug = tuple(debug)
        self.stop_after = stop_after
        self.depth = depth
        self.groups = [(0, CTX)] + [(CTX + i * 512, 512) for i in range(seq // 512)]
        self.nch = self.nt // 64
        self.ntile = self.nt // 128


INPUT_SPECS = None


def input_shapes(cfg):
    nt = cfg.nt
    return {
        "xin": ([nt, D], F32),
        "cc": ([2, D], F32),
        "w_ada": ([DEPTH, D, 6 * D], F32),
        "b_ada": ([DEPTH, 6 * D], F32),
        "norm1_w": ([DEPTH, D], F32),
        "norm2_w": ([DEPTH, D], F32),
        "final_norm_w": ([1, D], F32),
        "w_in_ext": ([DEPTH, D, NCOL], F32),
        "hg_lb": ([DEPTH, 512], F32),
        "hg_norm_w": ([DEPTH, 128], F32),
        "da_lambda": ([DEPTH, 256], F32),
        "da_subln_w": ([DEPTH, 128], F32),
        "conv_w": ([DEPTH, 3, 1536], F32),
        "conv_b": ([DEPTH, 1536], F32),
        "dt_bias": ([DEPTH, 32], F32),
        "a_log": ([DEPTH, 32], F32),
        "ssm_d": ([DEPTH, 16], F32),
        "ssm_norm_w": ([DEPTH, 1024], F32),
        "w_out": ([DEPTH, D, D], F32),
        "w_g": ([DEPTH, D, DFF], F32),
        "w_u": ([DEPTH, D, DFF], F32),
        "w_d": ([DEPTH, DFF, D], F32),
        "rope_c": ([128, nt], F32),
        "rope_s": ([128, nt], F32),
    }


def build_program(cfg):
    nc = bass.Bass("TRN2", target_bir_lowering=False)
    nt, seq = cfg.nt, cfg.seq
    with ExitStack() as st:
        S = Sched(nc, st, same_engine_sync=SAME_ENGINE_SYNC)
        I = {}
        for name, (shape, dt) in input_shapes(cfg).items():
            I[name] = S.dram(name, shape, dt, kind="ExternalInput")
        Y = S.dram("y", [seq, D], F32, kind="ExternalOutput")
        P = Prog(S, cfg, I, Y)
        P.build()
        S.emit()
    return nc


class Prog:
    def __init__(self, S, cfg, I, Y):
        self.S, self.cfg, self.I, self.Y = S, cfg, I, Y
        self.dbg = {}

    def dbg_out(self, name, src, shape, dtype):
        S = self.S
        o = S.dram("dbg_" + name, shape, dtype, kind="ExternalOutput")
        S.dma("sp", o.v(), src.v(), sembuf=o.buf)
        self.outs.append(o)

    def dbg_sb(self, name, view, shape, dtype):
        if name not in self.cfg.debug:
            return
        o = self.S.dram("dbg_" + name, shape, dtype, kind="ExternalOutput")
        self.S.dma("sp", o.v(), view)
        self.outs.append(o)

    def build(self):
        S, cfg, I = self.S, self.cfg, self.I
        nt = cfg.nt
        self.outs = [self.Y]
        self.WIN = [S.dram("WIN%d" % l, [D, NCOL], BF16) for l in range(DEPTH)]
        self.WO = [S.dram("WO%d" % l, [D, D], BF16) for l in range(DEPTH)]
        self.WG = [S.dram("WG%d" % l, [DFF // 512, 128, 16, 512], BF16) for l in range(DEPTH)]
        self.WU = [S.dram("WU%d" % l, [DFF // 512, 128, 16, 512], BF16) for l in range(DEPTH)]
        self.WD = [S.dram("WD%d" % l, [DFF, D], BF16) for l in range(DEPTH)]
        self.MOD = [S.dram("MOD%d" % l, [2, 6 * D], F32) for l in range(DEPTH)]
        ng = len(cfg.groups)
        self.UT = [S.dram("UT%d" % g, [128, 16, 512], BF16) for g in range(ng)]
        self.MT = [S.dram("MT%d" % g, [128, 16, 512], BF16) for g in range(ng)]
        self.U2T = [S.dram("U2T%d" % g, [128, 16, 512], BF16) for g in range(ng)]
        self.H1 = [S.dram("H1_%d" % g, [512, D], F32) for g in range(ng)]
        self.H = [S.dram("H_%d" % g, [512, D], F32) for g in range(ng)]
        self.HQT = S.dram("HQT", [4, 128, nt], F32)
        self.SFT = S.dram("SFT", [4, 128, nt], F32)
        self.SBT = S.dram("SBT", [4, 128, nt], F32)
        self.QT = S.dram("QT", [4, 128, nt], BF16)
        self.KT = S.dram("KT", [4, 128, nt], BF16)
        self.XBCT = S.dram("XBCT", [12, 128, nt], F32)
        self.HV = S.dram("HV", [nt, 512], BF16)
        self.HG = S.dram("HG", [nt, 512], F32)
        self.DV = S.dram("DV", [nt, 512], BF16)
        self.SZ = S.dram("SZ", [nt, 1024], F32)
        self.DT = S.dram("DT", [nt, 32], F32)
        self.HOF = S.dram("HOF", [nt, 512], F32)
        self.HOB = S.dram("HOB", [nt, 512], F32)
        self.MDA = S.dram("MDA", [nt, 512], BF16)
        self.XTM = S.dram("XTM", [nt, 1024], BF16)
        self.YF = S.dram("YF", [nt, 1024], F32)
        self.YB = S.dram("YB", [nt, 1024], F32)

        self.ident = S.sb("ident", [128, 128], BF16)
        identf = S.sb("identf", [128, 128], F32)
        self.identf = identf
        self.epsc = S.sb("epsc", [128, 1], F32)
        S.memset("pool", self.epsc.v(), EPS)
        self.onescol = S.sb("onescol", [128, 1], F32)
        S.memset("pool", self.onescol.v(), 1.0)
        S.memset("pool", identf.v(), 0.0)
        S.aselect(identf.v(), identf.v(), [[-1, 128]], ALU.not_equal, 1.0, 0, 1)
        S.copy("dve", self.ident.v(), identf.v())
        onesf = S.sb("onesf", [64, 64], F32)
        self.mask_f = S.sb("mask_f", [64, 64], BF16)
        self.mask_b = S.sb("mask_b", [64, 64], BF16)
        tmpm = S.sb("tmpm", [64, 64], F32)
        S.memset("pool", onesf.v(), 1.0)
        S.aselect(tmpm.v(), onesf.v(), [[1, 64]], ALU.is_ge, 0.0, 0, -1)
        S.copy("dve", self.mask_f.v(), tmpm.v())
        tmpm2 = S.sb("tmpm2", [64, 64], F32)
        S.aselect(tmpm2.v(), onesf.v(), [[-1, 64]], ALU.is_ge, 0.0, 0, 1)
        S.copy("dve", self.mask_b.v(), tmpm2.v())

        self.convert_weights(0, ["in"])
        for l in range(cfg.depth):
            last = (l == DEPTH - 1)
            self.phase_mod(l)
            if l == 0:
                self.convert_weights(0, ["o", "g", "u", "d"])
            if cfg.stop_after == ("mod", l):
                break
            self.phase_norm1(l)
            if cfg.stop_after == ("norm1", l):
                break
            self.phase_inproj(l)
            if cfg.stop_after == ("inproj", l):
                break
            if l + 1 < cfg.depth:
                self.convert_weights(l + 1, ["in", "o", "g", "u", "d"])
            self.phase_hgrn(l)
            if cfg.stop_after == ("hgrn", l):
                break
            self.phase_da(l)
            if cfg.stop_after == ("da", l):
                break
            self.phase_ssd(l)
            if cfg.stop_after == ("ssd", l):
                break
            self.phase_mixout(l)
            if cfg.stop_after == ("mixout", l):
                break
            self.phase_outproj(l)
            if cfg.stop_after == ("outproj", l):
                break
            self.phase_ffn(l)
            if cfg.stop_after == ("ffn", l):
                break
        for name, (src, shape, dt) in self.dbg.items():
            if name in cfg.debug:
                self.dbg_out(name, src, shape, dt)
        S.final_wait("sp", [o.buf for o in self.outs])

    def convert_weights(self, l, which):
        S, I = self.S, self.I
        if not hasattr(self, "_cvsems"):
            self._cvsems = [Buf("cv%d" % i) for i in range(4)]
            self._cvk = 0
        for name in which:
            if name in ("g", "u"):
                src, dst = (I["w_g"], self.WG[l]) if name == "g" else (I["w_u"], self.WU[l])
                for kc in range(16):
                    S.dma("pool", dst[:, :, kc, :].rearrange("f p n -> p f n"),
                          src[l, kc * 128:(kc + 1) * 128, :].rearrange("p (f n) -> p f n", n=512),
                          sembuf=self._cvsems[self._cvk % 4])
                    self._cvk += 1
                continue
            src, dst, rows = {"in": (I["w_in_ext"], self.WIN[l], D), "o": (I["w_out"], self.WO[l], D),
                              "d": (I["w_d"], self.WD[l], DFF)}[name]
            for r0 in range(0, rows, 128):
                S.dma("pool", dst[r0:r0 + 128, :], src[l, r0:r0 + 128, :], sembuf=self._cvsems[self._cvk % 4])
                self._cvk += 1

    def phase_mod(self, l):
        S, I = self.S, self.I
        S.push()
        cT = S.sb("cT", [128, 16, 2], F32)
        scT = S.sb("scT", [128, 16, 2], BF16)
        bada = S.sb("bada", [2, 6 * D], F32)
        modsb = S.sb("modsb", [2, 6 * D], F32)
        wt = [S.sb("wada%d" % i, [128, 16, 512], BF16) for i in range(2)]
        pm = [S.ps("pmod%d" % i, [128, 512], F32) for i in range(2)]
        for t in range(2):
            S.dma("sp", cT[:, :, t], I["cc"][t, :].rearrange("(k p) -> p k", p=128), slow=True)
        S.dma("sp", bada.v(), I["b_ada"][l:l + 1, :].partition_broadcast(2))
        S.act(scT.v(), cT.v(), AF.Silu)
        for j in range(24):
            w = wt[j % 2]
            S.dma("pool", w.v(), I["w_ada"][l, :, j * 512:(j + 1) * 512].rearrange("(k p) n -> p k n", p=128))
            p = pm[j % 2]
            for k in range(16):
                S.mm(p[0:2, :], scT[:, k, :], w[:, k, :], start=(k == 0), stop=(k == 15))
            S.tt("dve", modsb[:, j * 512:(j + 1) * 512], p[0:2, :], bada[:, j * 512:(j + 1) * 512], ALU.add)
        S.dma("sp", self.MOD[l].v(), modsb.v())
        self.dbg["mod%d" % l] = (self.MOD[l], [2, 6 * D], F32)
        S.pop()

    def _bc_row(self, dst, src_row):
        self.S.dma("sp", dst.v(), src_row.partition_broadcast(128))

    def _gain_shift(self, l, which, norm_key, t):
        S, I = self.S, self.I
        nw = S.sb("nw", [128, D], F32)
        self._bc_row(nw, I[norm_key][l:l + 1, :])
        gain = S.sb("gain", [128, D], F32)
        shift = S.sb("shift", [128, D], F32)
        base = which * 3 * D
        self._bc_row(shift, self.MOD[l][t:t + 1, base:base + D])
        self._bc_row(gain, self.MOD[l][t:t + 1, base + D:base + 2 * D])
        S.stt(gain.v(), gain.v(), 1.0, nw.v(), ALU.add, ALU.mult)
        return gain, shift

    def _gate(self, l, which, t):
        gate = self.S.sb("gate", [128, D], F32)
        base = which * 3 * D
        self._bc_row(gate, self.MOD[l][t:t + 1, base + 2 * D:base + 3 * D])
        return gate

    def _rms_mod_transpose(self, src_view, gain, shift, uT, j, tiles):
        S = self.S
        junk, ss, rstd, ub, ptr = tiles["junk"], tiles["ss"], tiles["rstd"], tiles["ub"], tiles["ptr"]
        S.act(junk.v(), src_view, AF.Square, accum_out=ss.v())
        S.act(rstd.v(), ss.v(), AF.Ln, scale=1.0 / D, bias=self.epsc.v())
        S.act(rstd.v(), rstd.v(), AF.Exp, scale=-0.5)
        S.stt(junk.v(), src_view, rstd[:, 0:1], gain.v(), ALU.mult, ALU.mult)
        S.tt("dve", ub.v(), junk.v(), shift.v(), ALU.add)
        for k in range(16):
            S.tr(ptr[:, k * 128:(k + 1) * 128], ub[:, k * 128:(k + 1) * 128], self.ident.v())
        S.copy("act", uT[:, :, j * 128:(j + 1) * 128], ptr.v().rearrange("p (k t) -> p k t", t=128))

    def phase_norm1(self, l):
        S, I, cfg = self.S, self.I, self.cfg
        S.push()
        g_lat, s_lat = self._gain_shift(l, 0, "norm1_w", 0)
        g_ctx, s_ctx = self._gain_shift(l, 0, "norm1_w", 1)
        ht = [S.sb("ht%d" % i, [128, D], F32) for i in range(2)]
        tiles = {"junk": S.sb("junk", [128, D], F32), "ss": S.sb("ss", [128, 1], F32),
                 "rstd": S.sb("rstd", [128, 1], F32), "ub": S.sb("ub", [128, D], BF16),
                 "ptr": S.ps("ptr", [128, D], BF16)}
        uTs = [S.sb("uT%d" % i, [128, 16, 512], BF16) for i in range(2)]
        n = 0
        for g, (t0, gsz) in enumerate(cfg.groups):
            uT = uTs[g % 2]
            gain, shift = (g_ctx, s_ctx) if g == 0 else (g_lat, s_lat)
            for j in range(gsz // 128):
                h = ht[n % 2]
                n += 1
                if l == 0:
                    src = I["xin"][t0 + j * 128:t0 + (j + 1) * 128, :]
                else:
                    src = self.H[g][j * 128:(j + 1) * 128, :]
                S.dma("sp", h.v(), src)
                self._rms_mod_transpose(h.v(), gain, shift, uT, j, tiles)
            S.dma("sp", self.UT[g][:, :, 0:gsz], uT[:, :, 0:gsz])
        S.pop()

    def phase_inproj(self, l):
        S, I, cfg = self.S, self.I, self.cfg
        nt = cfg.nt
        S.push()
        WIN = self.WIN[l]
        uTs = [S.sb("uTi%d" % i, [128, 16, 512], BF16) for i in range(2)]
        wts = [S.sb("wi%d" % i, [128, 16, 1024], BF16) for i in range(2)]
        rc = [S.sb("rc%d" % i, [128, 512], F32) for i in range(2)]
        rs = [S.sb("rs%d" % i, [128, 512], F32) for i in range(2)]
        stg = [S.sb("stg%d" % i, [128, 4, 512], F32) for i in range(2)]
        stgb = [S.sb("stgb%d" % i, [128, 4, 512], BF16) for i in range(2)]
        t1 = S.sb("ropet1", [128, 512], F32)
        t2 = S.sb("ropet2", [128, 512], F32)
        psA = [S.ps("psA%d" % i, [128, 512], F32) for i in range(4)]
        nstg = 0
        npsum = 0
        nw = 0
        for g, (t0, gsz) in enumerate(cfg.groups):
            uT = uTs[g % 2]
            S.dma("sp", uT[:, :, 0:gsz], self.UT[g][:, :, 0:gsz])
            S.dma("sp", rc[g % 2][:, 0:gsz], I["rope_c"][:, t0:t0 + gsz])
            S.dma("sp", rs[g % 2][:, 0:gsz], I["rope_s"][:, t0:t0 + gsz])
            RC, RS = rc[g % 2], rs[g % 2]
            nsub = gsz // 128
            for pair in range(8):
                w = wts[nw % 2]
                nw += 1
                c0 = pair * 1024
                ncols = min(1024, NCOL - c0)
                S.dma("sp", w[:, :, 0:ncols], WIN[:, c0:c0 + ncols].rearrange("(k p) n -> p k n", p=128))

                def fm_chunk(colofs, ps):
                    for k in range(16):
                        S.mm(ps[:, 0:gsz], w[:, k, colofs:colofs + 128], uT[:, k, 0:gsz], start=(k == 0), stop=(k == 15))

                def fm_block(half, func, dst, dst_c0, bf=False):
                    nonlocal nstg, npsum
                    sg = (stgb if bf else stg)[nstg % 2]
                    nstg += 1
                    for c in range(4):
                        ps = psA[npsum % 4]
                        npsum += 1
                        fm_chunk(half * 512 + c * 128, ps)
                        if func is None:
                            S.copy("dve", sg[:, c, 0:gsz], ps[:, 0:gsz])
                        else:
                            S.act(sg[:, c, 0:gsz], ps[:, 0:gsz], func)
                    S.dma("sp", dst[dst_c0:dst_c0 + 4, :, t0:t0 + gsz].rearrange("c p t -> p c t"), sg[:, :, 0:gsz])

                def rope_block(dst):
                    nonlocal nstg, npsum
                    sg = stgb[nstg % 2]
                    nstg += 1
                    for c in range(4):
                        ps1 = psA[npsum % 4]
                        ps2 = psA[(npsum + 1) % 4]
                        npsum += 2
                        fm_chunk(c * 128, ps1)
                        fm_chunk(512 + c * 128, ps2)
                        S.tt("dve", t1[:, 0:gsz], ps1[:, 0:gsz], RC[:, 0:gsz], ALU.mult)
                        S.tt("dve", t2[:, 0:gsz], ps2[:, 0:gsz], RS[:, 0:gsz], ALU.mult)
                        S.tt("dve", sg[:, c, 0:gsz], t1[:, 0:gsz], t2[:, 0:gsz], ALU.add)
                    S.dma("sp", dst[:, :, t0:t0 + gsz].rearrange("c p t -> p c t"), sg[:, :, 0:gsz])

                def tm_block(half, width, func, dst, dst_c0, bf):
                    nonlocal nstg, npsum
                    sg = (stgb if bf else stg)[nstg % 2]
                    nstg += 1
                    for s in range(nsub):
                        ps = psA[npsum % 4]
                        npsum += 1
                        for k in range(16):
                            S.mm(ps[:, 0:width], uT[:, k, s * 128:(s + 1) * 128], w[:, k, half * 512:half * 512 + width],
                                 start=(k == 0), stop=(k == 15))
                        if func is None:
                            S.copy("dve", sg[:, s, 0:width], ps[:, 0:width])
                        else:
                            S.act(sg[:, s, 0:width], ps[:, 0:width], func)
                    S.dma("sp", dst[t0:t0 + gsz, dst_c0:dst_c0 + width].rearrange("(s p) c -> p s c", p=128),
                          sg[:, 0:nsub, 0:width])

                if pair == 0:
                    fm_block(0, AF.Silu, self.HQT, 0)
                    fm_block(1, AF.Sigmoid, self.SFT, 0)
                elif pair == 1:
                    fm_block(0, AF.Sigmoid, self.SBT, 0)
                    fm_block(1, None, self.XBCT, 0)
                elif pair == 2:
                    rope_block(self.QT)
                elif pair == 3:
                    rope_block(self.KT)
                elif pair == 4:
                    fm_block(0, None, self.XBCT, 4)
                    fm_block(1, None, self.XBCT, 8)
                elif pair == 5:
                    tm_block(0, 512, None, self.HV, 0, True)
                    tm_block(1, 512, AF.Silu, self.HG, 0, False)
                elif pair == 6:
                    tm_block(0, 512, None, self.DV, 0, True)
                    tm_block(1, 512, AF.Silu, self.SZ, 0, False)
                elif pair == 7:
                    tm_block(0, 512, AF.Silu, self.SZ, 512, False)
                    tm_block(1, 32, None, self.DT, 0, False)
        for nm, t, shape, dt in (("hqt", self.HQT, [4, 128, nt], F32), ("sft", self.SFT, [4, 128, nt], F32),
                                 ("sbt", self.SBT, [4, 128, nt], F32), ("qt", self.QT, [4, 128, nt], BF16),
                                 ("kt", self.KT, [4, 128, nt], BF16), ("xbct", self.XBCT, [12, 128, nt], F32),
                                 ("hv", self.HV, [nt, 512], BF16), ("hg", self.HG, [nt, 512], F32),
                                 ("dv", self.DV, [nt, 512], BF16), ("sz", self.SZ, [nt, 1024], F32),
                                 ("dt", self.DT, [nt, 32], F32)):
            self.dbg["%s%d" % (nm, l)] = (t, shape, dt)
        S.pop()


    def _orders(self, L=64):
        nch = self.cfg.nt // L
        nctx = CTX // L
        fwd = list(range(nch))
        bwd = list(range(nctx - 1, -1, -1)) + list(range(nch - 1, nctx - 1, -1))
        return fwd, bwd

    def phase_hgrn(self, l):
        S, I, cfg = self.S, self.I, self.cfg
        LH = 32
        nt, nch = cfg.nt, cfg.nt // LH
        S.push()
        lbraw = S.sb("lbraw", [128, 2, 4], F32)
        for ll in range(2):
            S.dma("sp", lbraw[:, ll, :], I["hg_lb"][ll, :].rearrange("(h k) -> k h", k=128), slow=True)
        lb = S.sb("lb", [128, 4], F32)
        omlb = S.sb("omlb", [128, 4], F32)
        if l == 0:
            S.memset("dve", lb.v(), 0.0)
        else:
            S.tt("dve", lb.v(), lbraw[:, 1, :], lbraw[:, 0, :], ALU.subtract)
            S.act(lb.v(), lb.v(), AF.Sigmoid)
        S.ts("dve", omlb.v(), lb.v(), -1.0, ALU.mult, 1.0, ALU.add)
        ones = S.sb("ones", [128, nt], BF16)
        S.memset("pool", ones.v(), 1.0)
        Q = S.sb("hQ", [128, nt], F32)
        X1 = S.sb("hX1", [128, nt], F32)
        X2 = S.sb("hX2", [128, nt], F32)
        X3 = S.sb("hX3", [128, nt], F32)
        qt = [S.sb("hqt%d" % d, [128, nt], BF16) for d in range(2)]
        kt = [S.sb("hkt%d" % d, [128, nt], BF16) for d in range(2)]
        V = S.sb("hV", [LH, nch, 128], BF16)
        gg = [S.sb("hgg%d" % d, [128, nch], F32) for d in range(2)]
        rt = S.sb("hrt", [128, nch], F32)
        din = S.sb("hdin", [128, nch], F32)
        dout = S.sb("hdout", [128, nch], F32)
        Sf = [S.sb("hS%d" % d, [128, 128], F32) for d in range(2)]
        St = [S.sb("hSt%d" % d, [128, 128], F32) for d in range(2)]
        Sb = [S.sb("hSb%d" % d, [128, 128], BF16) for d in range(2)]
        attm = [[S.sb("hattm%d_%d" % (d, i), [LH, LH], BF16) for i in range(2)] for d in range(2)]
        ktT = [[S.sb("hktT%d_%d" % (d, i), [LH, 128], BF16) for i in range(2)] for d in range(2)]
        osb = [[S.sb("hosb%d_%d" % (d, i), [LH, 512], F32) for i in range(2)] for d in range(2)]
        ps_att_ = [S.ps("hpsatt%d" % d, [128, 512], F32) for d in range(2)]
        ps_att = [t[0:LH, 0:LH] for t in ps_att_]
        ps_kT_ = [S.ps("hpskT%d" % d, [128, 1024], BF16) for d in range(2)]
        ps_kT = [t[0:LH, 0:128] for t in ps_kT_]
        ps_o_ = [S.ps("hpso%d" % d, [128, 512], F32) for d in range(2)]
        ps_o = [t[0:LH, 0:128] for t in ps_o_]
        ps_dS_ = [S.ps("hpsdS%d" % d, [128, 512], F32) for d in range(2)]
        ps_dS = [t[:, 0:128] for t in ps_dS_]
        masks = [self.mask_f[0:LH, 0:LH], self.mask_b[0:LH, 0:LH]]
        orders = self._orders(LH)
        outs = [self.HOF, self.HOB]
        v3 = lambda t: t.v().rearrange("p (c l) -> p c l", l=LH)
        for h in range(4):
            S.dma("sp", Q.v(), self.HQT[h, :, :])
            S.dma("sp", V.v(), self.HV[:, h * 128:(h + 1) * 128].rearrange("(c p) v -> p c v", p=LH))
            for d in range(2):
                src = self.SFT if d == 0 else self.SBT
                S.dma("sp", X1.v(), src[h, :, :])
                S.ts("dve", X1.v(), X1.v(), omlb[:, h:h + 1], ALU.mult, lb[:, h:h + 1], ALU.add)
                S.ts("dve", X2.v(), X1.v(), -1.0, ALU.mult, 1.0, ALU.add)
                S.act(X1.v(), X1.v(), AF.Ln)
                S.scan(X3.v(), ones.v(), X1.v())
                if d == 1:
                    S.tt("dve", X3.v(), X1.v(), X3.v(), ALU.subtract)
                R3, L3 = v3(X3), v3(X1)
                i_en, i_ex = (0, LH - 1) if d == 0 else (LH - 1, 0)
                S.copy("dve", rt.v(), R3[:, :, LH // 2])
                S.tt("dve", din.v(), rt.v(), R3[:, :, i_en], ALU.subtract)
                S.tt("dve", din.v(), din.v(), L3[:, :, i_en], ALU.add)
                S.act(din.v(), din.v(), AF.Exp)
                S.tt("dve", dout.v(), R3[:, :, i_ex], rt.v(), ALU.subtract)
                S.act(dout.v(), dout.v(), AF.Exp)
                if d == 0:
                    S.tt("dve", gg[d][:, 0:nch - 1], dout[:, 0:nch - 1], din[:, 1:nch], ALU.mult)
                else:
                    S.tt("dve", gg[d][:, 1:nch], dout[:, 1:nch], din[:, 0:nch - 1], ALU.mult)
                    S.tt("dve", gg[d][:, 0:1], dout[:, 0:1], din[:, nch - 1:nch], ALU.mult)
                S.tt("dve", v3(X1), R3, rt.v().rearrange("p (c o) -> p c o", o=1).to_broadcast([128, nch, LH]), ALU.subtract)
                S.act(X3.v(), X1.v(), AF.Exp)
                S.tt("dve", qt[d].v(), Q.v(), X3.v(), ALU.mult)
                S.act(X3.v(), X1.v(), AF.Exp, scale=-1.0)
                S.tt("dve", kt[d].v(), X2.v(), X3.v(), ALU.mult)
                S.memset("dve", Sf[d].v(), 0.0)
                S.memset("pool", Sb[d].v(), 0.0)
            def prep(i, d):
                c = orders[d][i]
                c0 = c * LH
                S.mm(ps_att[d], kt[d][:, c0:c0 + LH], qt[d][:, c0:c0 + LH])
                S.tr(ps_kT[d], kt[d][:, c0:c0 + LH], self.ident.v())
                S.tt("dve", attm[d][i % 2].v(), ps_att[d], masks[d], ALU.mult)
                S.copy("act", ktT[d][i % 2].v(), ps_kT[d])

            def use(i, d):
                c = orders[d][i]
                c0 = c * LH
                j4 = i % 4
                slot = j4 if d == 0 else 3 - j4
                po = ps_o_[d][0:LH, slot * 128:(slot + 1) * 128]
                S.mm(po, attm[d][i % 2].v(), V[:, c, :], start=(j4 == 0), stop=False, skip=True)
                S.mm(po, qt[d][:, c0:c0 + LH], Sb[d].v(), start=False, stop=True, skip=True)
                if i < nch - 1:
                    S.mm(ps_dS[d], ktT[d][i % 2].v(), V[:, c, :])
                    S.tt("dve", St[d].v(), Sf[d].v(), ps_dS[d], ALU.add)
                    S.act(Sb[d].v(), St[d].v(), AF.Identity, scale=gg[d][:, c:c + 1])
                    S.ts("dve", Sf[d].v(), St[d].v(), gg[d][:, c:c + 1], ALU.mult)
                if j4 == 3:
                    o = osb[d][(i // 4) % 2]
                    S.copy("act", o.v(), ps_o_[d][0:LH, :])
                    base = min(orders[d][i - 3 + jj] for jj in range(4)) * LH
                    dst = outs[d][base:base + 4 * LH, h * 128:(h + 1) * 128].rearrange("(j p) v -> p j v", p=LH)
                    S.dma("sp", dst, o.v().rearrange("p (j v) -> p j v", v=128))
            for d in range(2):
                prep(0, d)
            for i in range(nch):
                for d in range(2):
                    if i + 1 < nch:
                        prep(i + 1, d)
                    use(i, d)
        self.dbg["hof%d" % l] = (self.HOF, [nt, 512], F32)
        self.dbg["hob%d" % l] = (self.HOB, [nt, 512], F32)
        S.pop()

    def phase_da(self, l):
        S, I, cfg = self.S, self.I, self.cfg
        nt, ntile, seq = cfg.nt, cfg.ntile, cfg.seq
        lam_init = 0.8 - 0.6 * math.exp(-0.3 * l)
        S.push()
        lamraw = S.sb("lamraw", [128, 256], F32)
        S.dma("sp", lamraw.v(), I["da_lambda"][l:l + 1, :].partition_broadcast(128))
        prod = S.sb("lamprod", [128, 2, 64], F32)
        lr = lamraw.v().rearrange("p (a b k) -> p a b k", a=2, b=2)
        S.tt("dve", prod.v(), lr[:, :, 0, :], lr[:, :, 1, :], ALU.mult)
        lsum = S.sb("lsum", [128, 2], F32)
        S.reduce(lsum.v(), prod.v())
        S.act(lsum.v(), lsum.v(), AF.Exp)
        neglam = S.sb("neglam", [128, 1], F32)
        S.tt("dve", neglam.v(), lsum[:, 1:2], lsum[:, 0:1], ALU.subtract)
        S.ts("dve", neglam.v(), neglam.v(), -lam_init, ALU.add)
        self.dbg_sb("neglam%d" % l, neglam.v(), [128, 1], F32)
        self.dbg_sb("lsum%d" % l, lsum.v(), [128, 2], F32)
        sw = S.sb("sublnw", [128, 128], F32)
        S.dma("sp", sw.v(), I["da_subln_w"][l:l + 1, :].partition_broadcast(128))
        S.ts("dve", sw.v(), sw.v(), 1.0 - lam_init, ALU.mult)
        QTh = S.sb("dQT", [128, nt], BF16)
        KTh = S.sb("dKT", [128, nt], BF16)
        Vh = S.sb("dV", [128, ntile, 128], BF16)
        NS = 3
        NP = 4
        ps_s = [S.ps("dps%d" % i, [128, 2, 512], F32) for i in range(NS)]
        Pb = [S.sb("dP%d" % i, [128, 2, 512], BF16) for i in range(NP)]
        ps_oT = [S.ps("dpoT%d" % c, [128, 512], F32) for c in range(2)]
        ps_tr = ps_s[0].v().rearrange("p a (b v) -> p a b v", v=128)
        ps_l = ps_s[1][:, 0, :]
        Pacc = [S.sb("dPacc%d" % c, [128, 512], F32) for c in range(2)]
        Pacc2 = [S.sb("dPacc2_%d" % c, [128, 512], F32) for c in range(2)]
        oT = [S.sb("doT%d" % c, [128, 512], F32) for c in range(2)]
        rc8 = S.sb("drc8", [128, 8], F32)
        t04 = S.sb("dt04", [128, 4, 128], F32)
        a4 = S.sb("da4", [128, 4, 128], F32)
        ss4d = S.sb("dss4", [128, 4], F32)
        stg = [S.sb("dstg%d" % i, [128, 4, 128], BF16) for i in range(2)]
        n = 0
        nst = 0
        qblocks = [(0, CTX, [0, 1])] + [(CTX + i * 512, 512, list(range(ntile))) for i in range(seq // 512)]
        for h in range(4):
            S.dma("sp", QTh.v(), self.QT[h, :, :])
            S.dma("sp", KTh.v(), self.KT[h, :, :])
            S.dma("sp", Vh.v(), self.DV[:, h * 128:(h + 1) * 128].rearrange("(kb p) v -> p kb v", p=128))
            for (q0, nq, kbs) in qblocks:
                nqs = nq // 128
                npair = len(kbs) // 2
                its = [(c, pi) for c in range(2) for pi in range(npair)]

                def score(it, m):
                    c, pi = it
                    for e in range(2):
                        kb = kbs[2 * pi + e]
                        S.mm(ps_s[m % NS][:, e, 0:nq], KTh[c * 64:(c + 1) * 64, kb * 128:(kb + 1) * 128],
                             QTh[c * 64:(c + 1) * 64, q0:q0 + nq])
                    S.act(Pb[m % NP][:, :, 0:nq], ps_s[m % NS][:, :, 0:nq], AF.Exp, scale=0.125)

                def pv(it, m):
                    c, pi = it
                    P = Pb[m % NP]
                    for e in range(2):
                        kb = kbs[2 * pi + e]
                        S.mm(ps_oT[c][:, 0:nq], Vh[:, kb, :], P[:, e, 0:nq],
                             start=(pi == 0 and e == 0), stop=(pi == npair - 1 and e == 1))
                    if pi == 0:
                        S.copy("dve", Pacc[c][:, 0:nq], P[:, 0, 0:nq])
                        S.copy("pool", Pacc2[c][:, 0:nq], P[:, 1, 0:nq])
                    else:
                        S.tt("dve", Pacc[c][:, 0:nq], Pacc[c][:, 0:nq], P[:, 0, 0:nq], ALU.add)
                        S.tt("pool", Pacc2[c][:, 0:nq], Pacc2[c][:, 0:nq], P[:, 1, 0:nq], ALU.add)
                    if pi == npair - 1:
                        S.tt("dve", Pacc[c][:, 0:nq], Pacc[c][:, 0:nq], Pacc2[c][:, 0:nq], ALU.add)
                LOOK = 2
                for j in range(min(LOOK, len(its))):
                    score(its[j], n + j)
                for j in range(len(its)):
                    if j + LOOK < len(its):
                        score(its[j + LOOK], n + j + LOOK)
                    pv(its[j], n + j)
                n += len(its)
                for c in range(2):
                    S.copy("act", oT[c][:, 0:nq], ps_oT[c][:, 0:nq])
                for c in range(2):
                    for qs in range(nqs):
                        S.tr(ps_tr[:, c, qs, :], oT[c][:, qs * 128:(qs + 1) * 128], self.identf.v())
                        S.mm(ps_l[:, c * 4 + qs:c * 4 + qs + 1], Pacc[c][:, qs * 128:(qs + 1) * 128], self.onescol.v(), skip=True)
                sg = stg[nst % 2]
                nst += 1
                bq = lambda v: v.rearrange("p (a o) -> p a o", o=1).to_broadcast([128, nqs, 128])
                S.recip(rc8.v(), ps_l[:, 0:8])
                S.ts("dve", rc8[:, 4:8], rc8[:, 4:8], neglam[:, 0:1], ALU.mult)
                S.tt("dve", t04[:, 0:nqs, :], ps_tr[:, 0, 0:nqs, :], bq(rc8[:, 0:nqs]), ALU.mult)
                S.tt("dve", a4[:, 0:nqs, :], ps_tr[:, 1, 0:nqs, :], bq(rc8[:, 4:4 + nqs]), ALU.mult)
                S.tt("dve", a4[:, 0:nqs, :], a4[:, 0:nqs, :], t04[:, 0:nqs, :], ALU.add)
                S.tt("dve", t04[:, 0:nqs, :], a4[:, 0:nqs, :], a4[:, 0:nqs, :], ALU.mult)
                S.reduce(ss4d[:, 0:nqs], t04[:, 0:nqs, :])
                S.act(ss4d[:, 0:nqs], ss4d[:, 0:nqs], AF.Ln, scale=1.0 / 128, bias=self.epsc.v())
                S.act(ss4d[:, 0:nqs], ss4d[:, 0:nqs], AF.Exp, scale=-0.5)
                S.tt("dve", a4[:, 0:nqs, :], a4[:, 0:nqs, :], bq(ss4d[:, 0:nqs]), ALU.mult)
                S.tt("dve", sg[:, 0:nqs, :], a4[:, 0:nqs, :],
                     sw.v().rearrange("p (o v) -> p o v", o=1).to_broadcast([128, nqs, 128]), ALU.mult)
                S.dma("sp", self.MDA[q0:q0 + nq, h * 128:(h + 1) * 128].rearrange("(s p) v -> p s v", p=128), sg[:, 0:nqs, :])
        self.dbg["mda%d" % l] = (self.MDA, [nt, 512], BF16)
        S.pop()

    def phase_ssd(self, l):
        S, I, cfg = self.S, self.I, self.cfg
        nt, nch, ntile = cfg.nt, cfg.nch, cfg.ntile
        S.push()
        one64 = S.sb("one64", [64, 128], F32)
        S.memset("pool", one64.v(), 1.0)
        U = S.sb("sU", [64, 64], F32)
        SU = S.sb("sSU", [64, 64], F32)
        S.aselect(U.v(), one64[:, 0:64], [[1, 64]], ALU.is_ge, 0.0, 0, -1)
        S.aselect(SU.v(), one64[:, 0:64], [[1, 64]], ALU.is_gt, 0.0, 0, -1)
        nU = S.sb("snU", [64, 64], F32)
        nSU = S.sb("snSU", [64, 64], F32)
        GT = S.sb("sGT", [64, 64], F32)
        GE = S.sb("sGE", [64, 64], F32)
        S.ts("dve", nU.v(), U.v(), -1.0, ALU.mult)
        S.ts("dve", nSU.v(), SU.v(), -1.0, ALU.mult)
        S.ts("dve", GT.v(), U.v(), -1.0, ALU.mult, 1.0, ALU.add)
        S.ts("dve", GE.v(), SU.v(), -1.0, ALU.mult, 1.0, ALU.add)
        dA = S.sb("sdA", [64, nch, 32], F32)
        T2 = S.sb("sT2", [64, nch, 32], F32)
        EOS = S.sb("sEOS", [64, nch, 32], F32)
        XW = S.sb("sXW", [64, nch, 32], F32)
        EOS_tmp = dA_tmp = None
        ETOT = S.sb("sETOT", [128, nch, 32], F32)
        BT = S.sb("sBT", [128, 2, nt], BF16)
        CT = S.sb("sCT", [128, 2, nt], BF16)
        cw = S.sb("scw", [128, 12, 3], F32)
        cbias = S.sb("scb", [128, 12], F32)
        S.push()
        pst = [S.ps("spst%d" % i, [128, 512], F32) for i in range(2)]
        npst = 0
        EOS_tmp = S.sb("sLDT", [64, nch, 32], F32)
        DTs = S.sb("sDT", [64, nch, 32], F32)
        S.dma("sp", DTs.v(), self.DT.v().rearrange("(c p) j -> p c j", p=64))
        dtb = S.sb("sdtb", [64, 32], F32)
        Aa = S.sb("sA", [64, 32], F32)
        S.dma("sp", dtb.v(), I["dt_bias"][l:l + 1, :].partition_broadcast(64))
        S.dma("sp", Aa.v(), I["a_log"][l:l + 1, :].partition_broadcast(64))
        S.act(Aa.v(), Aa.v(), AF.Exp)
        S.ts("dve", Aa.v(), Aa.v(), -1.0, ALU.mult)
        bc32 = lambda t: t.v().rearrange("p (o j) -> p o j", o=1).to_broadcast([64, nch, 32])
        S.tt("dve", DTs.v(), DTs.v(), bc32(dtb), ALU.add)
        S.act(DTs.v(), DTs.v(), AF.Exp)
        S.act(DTs.v(), DTs.v(), AF.Ln, bias=1.0)
        S.tt("dve", dA.v(), DTs.v(), bc32(Aa), ALU.mult)

        def table(dst, mats, func, npart=64):
            nonlocal npst
            for d in range(2):
                for cb in range(0, nch, 32):
                    ncb = min(32, nch - cb)
                    ps = pst[npst % 2]
                    npst += 1
                    pv = ps[0:npart, 0:ncb * 16].rearrange("p (c j) -> p c j", j=16)
                    S.mm(pv, mats[d], dA[:, cb:cb + ncb, d * 16:(d + 1) * 16])
                    if func is None:
                        S.copy("dve", dst[:, cb:cb + ncb, d * 16:(d + 1) * 16], pv)
                    else:
                        S.act(dst[:, cb:cb + ncb, d * 16:(d + 1) * 16], pv, func)
        table(T2, [nU.v(), SU.v()], None)
        table(EOS, [U.v(), GE.v()], AF.Exp)
        table(XW, [GT.v(), SU.v()], AF.Exp)
        table(ETOT, [one64.v(), one64.v()], AF.Exp, npart=128)
        S.tt("dve", XW.v(), XW.v(), DTs.v(), ALU.mult)
        S.act(EOS_tmp.v(), DTs.v(), AF.Ln)
        S.tt("dve", T2.v(), T2.v(), EOS_tmp.v(), ALU.add)
        S.pop()
        for j in range(3):
            S.dma("sp", cw[:, :, j], I["conv_w"][l, j, :].rearrange("(c p) -> p c", p=128), slow=True)
        S.dma("sp", cbias.v(), I["conv_b"][l, :].rearrange("(c p) -> p c", p=128), slow=True)
        S.push()
        XT = [S.sb("sXT%d" % i, [128, nt], BF16) for i in range(2)]
        xin = S.sb("sxin", [128, nt], F32)
        yv = S.sb("sy", [128, nt], F32)
        ptx = [S.ps("sptx%d" % i, [128, 1024], BF16) for i in range(2)]
        xst = [S.sb("sxst%d" % i, [128, 8, 128], BF16) for i in range(2)]
        nx = 0
        segs = [(0, CTX), (CTX, nt)]
        for ci in range(12):
            S.dma("sp", xin.v(), self.XBCT[ci, :, :])
            S.act(yv.v(), xin.v(), AF.Identity, scale=cw[:, ci, 1:2], bias=cbias[:, ci:ci + 1])
            for (s0, e0) in segs:
                S.stt(yv[:, s0 + 1:e0], xin[:, s0:e0 - 1], cw[:, ci, 0:1], yv[:, s0 + 1:e0], ALU.mult, ALU.add)
                S.stt(yv[:, s0:e0 - 1], xin[:, s0 + 1:e0], cw[:, ci, 2:3], yv[:, s0:e0 - 1], ALU.mult, ALU.add)
            if ci < 8:
                xt_ = XT[ci % 2]
                S.act(xt_.v(), yv.v(), AF.Silu)
                for i0 in range(0, ntile, 8):
                    nti = min(8, ntile - i0)
                    pt = ptx[nx % 2]
                    xs_ = xst[nx % 2]
                    nx += 1
                    for a in range(nti):
                        S.tr(pt[:, a * 128:(a + 1) * 128], xt_[:, (i0 + a) * 128:(i0 + a + 1) * 128], self.ident.v())
                    S.copy("act", xs_[:, 0:nti, :], pt[:, 0:nti * 128].rearrange("p (a c) -> p a c", c=128))
                    S.dma("sp", self.XTM[i0 * 128:(i0 + nti) * 128, ci * 128:(ci + 1) * 128].rearrange("(a p) c -> p a c", p=128),
                          xs_[:, 0:nti, :])
            elif ci < 10:
                S.act(BT[:, ci - 8, :], yv.v(), AF.Silu)
            else:
                S.act(CT[:, ci - 10, :], yv.v(), AF.Silu)
        S.pop()
        Gms = [S.sb("sGms%d" % d, [64, nch, 2, 64], BF16) for d in range(2)]
        Btm = S.sb("sBtm", [64, nch, 2, 128], BF16)
        S.push()
        ptb_ = S.ps("sptb", [128, 1024], BF16)
        ptb = ptb_[0:64, 0:512].rearrange("p (a n) -> p a n", n=128)
        for c in range(0, nch, 2):
            for cc in range(2):
                for g in range(2):
                    S.tr(ptb[:, cc * 2 + g, :], BT[:, g, (c + cc) * 64:(c + cc + 1) * 64], self.ident.v())
            S.copy("act", Btm[:, c:c + 2, :, :].rearrange("p a g n -> p (a g) n"), ptb)
        psg = [S.ps("spg%d" % i, [128, 512], F32) for i in range(2)]
        for c in range(0, nch, 4):
            ps = psg[(c // 4) % 2]
            pv = ps[0:64, :].rearrange("p (a g t) -> p a g t", a=4, g=2)
            for cc in range(4):
                for g in range(2):
                    c0 = (c + cc) * 64
                    S.mm(pv[:, cc, g, :], BT[:, g, c0:c0 + 64], CT[:, g, c0:c0 + 64])
            S.tt("dve", Gms[0][:, c:c + 4, :, :].rearrange("p a g t -> p (a g) t"), ps[0:64, :].rearrange("p (a t) -> p a t", t=64),
                 self.mask_f.v().rearrange("p (o t) -> p o t", o=1).to_broadcast([64, 8, 64]), ALU.mult)
            S.tt("dve", Gms[1][:, c:c + 4, :, :].rearrange("p a g t -> p (a g) t"), ps[0:64, :].rearrange("p (a t) -> p a t", t=64),
                 self.mask_b.v().rearrange("p (o t) -> p o t", o=1).to_broadcast([64, 8, 64]), ALU.mult)
        S.pop()
        NK = 4
        ST = [S.sb("sST%d" % k, [128, 512], F32) for k in range(NK)]
        STb = [S.sb("sSTb%d" % k, [128, 512], BF16) for k in range(NK)]
        Xc = [[S.sb("sXc%d_%d" % (k, i), [64, 512], BF16) for i in range(2)] for k in range(NK)]
        Xw = [[S.sb("sXw%d_%d" % (k, i), [64, 512], BF16) for i in range(2)] for k in range(NK)]
        RH = [S.sb("sRH%d" % k, [64, 8, 64], F32) for k in range(2)] * 2
        EX = [S.sb("sEX%d" % k, [64, 8, 64], F32) for k in range(2)] * 2
        MT = [[S.sb("sMT%d_%d" % (k, i), [64, 8, 64], BF16) for i in range(2)] for k in range(NK)]
        ytmp = [S.sb("sytmp%d" % k, [64, 512], F32) for k in range(2)] * 2
        ysb = [[S.sb("sys%d_%d" % (k, i), [64, 512], F32) for i in range(1)] * 2 for k in range(NK)]
        ps_T = [S.ps("spT%d" % d, [128, 512], F32)[0:64, :] for d in range(2)]
        ps_Y = [S.ps("spY%d" % d, [128, 512], F32)[0:64, :] for d in range(2)]
        ps_Y2 = [S.ps("spY2%d" % d, [128, 512], F32)[0:64, :] for d in range(2)]
        ps_dS = [S.ps("spdS%d" % d, [128, 512], F32).v() for d in range(2)]
        orders = self._orders()
        outs = [self.YF, self.YB]
        Wd = [U, nSU]
        v8 = lambda v: v.rearrange("p (h q) -> p h q", q=64)
        hb = lambda v, np_: v.rearrange("p (h o) -> p h o", o=1).to_broadcast([np_, 8, 64])

        def prep(i, g, d):
            k = g * 2 + d
            c = orders[d][i]
            c0 = c * 64
            hs = d * 16 + g * 8
            xc = Xc[k][i % 2]
            S.dma("sp", xc.v(), self.XTM[c0:c0 + 64, g * 512:(g + 1) * 512])
            S.tt("pool", RH[k].v(), Wd[d].v().rearrange("p (o t) -> p o t", o=1).to_broadcast([64, 8, 64]),
                 dA[:, c, hs:hs + 8].rearrange("p (h o) -> p h o", o=1).to_broadcast([64, 8, 64]), ALU.mult)
            S.mm(ps_T[d], one64[:, 0:64], RH[k].v().rearrange("p h t -> p (h t)"))
            S.tt("dve", EX[k].v(), v8(ps_T[d]), hb(T2[:, c, hs:hs + 8], 64), ALU.add)
            S.act(EX[k].v(), EX[k].v(), AF.Exp)
            S.stt(MT[k][i % 2].v(), EX[k].v(), 1e30,
                  Gms[d][:, c, g, :].rearrange("p (o t) -> p o t", o=1).to_broadcast([64, 8, 64]), ALU.min, ALU.mult)
            S.tt("pool", v8(Xw[k][i % 2].v()), v8(xc.v()), hb(XW[:, c, hs:hs + 8], 64), ALU.mult)

        def use(i, g, d):
            k = g * 2 + d
            c = orders[d][i]
            c0 = c * 64
            hs = d * 16 + g * 8
            xc = Xc[k][i % 2]
            for h in range(8):
                S.mm(ps_Y[d][:, h * 64:(h + 1) * 64], MT[k][i % 2][:, h, :], xc[:, h * 64:(h + 1) * 64], skip=True)
            S.mm(ps_Y2[d], CT[:, g, c0:c0 + 64], STb[k].v())
            if i < nch - 1:
                S.mm(ps_dS[d], Btm[:, c, g, :], Xw[k][i % 2].v())
                S.tt("pool", v8(ST[k].v()), v8(ST[k].v()), hb(ETOT[:, c, hs:hs + 8], 128), ALU.mult)
                S.tt("dve", ST[k].v(), ST[k].v(), ps_dS[d], ALU.add)
                S.copy("act", STb[k].v(), ST[k].v())
            y = ysb[k][i % 2]
            S.tt("dve", v8(ytmp[k].v()), v8(ps_Y2[d]), hb(EOS[:, c, hs:hs + 8], 64), ALU.mult)
            S.tt("dve", y.v(), ytmp[k].v(), ps_Y[d], ALU.add)
            S.dma("sp!", outs[d][c0:c0 + 64, g * 512:(g + 1) * 512], y.v())
        for k in range(NK):
            S.memset("dve", ST[k].v(), 0.0)
            S.memset("pool", STb[k].v(), 0.0)
        for g in range(2):
            for d in range(2):
                prep(0, g, d)
        for i in range(nch):
            for g in range(2):
                for d in range(2):
                    if i + 1 < nch:
                        prep(i + 1, g, d)
                    use(i, g, d)
        self.dbg["yf%d" % l] = (self.YF, [nt, 1024], F32)
        self.dbg["yb%d" % l] = (self.YB, [nt, 1024], F32)
        self.dbg["xtm%d" % l] = (self.XTM, [nt, 1024], BF16)
        S.pop()

    def phase_mixout(self, l):
        S, I, cfg = self.S, self.I, self.cfg
        last = (l == DEPTH - 1)
        S.push()
        hgw = S.sb("ghgw", [128, 128], F32)
        S.dma("sp", hgw.v(), I["hg_norm_w"][l:l + 1, :].partition_broadcast(128))
        ssmw = S.sb("gssmw", [128, 1024], F32)
        S.dma("sp", ssmw.v(), I["ssm_norm_w"][l:l + 1, :].partition_broadcast(128))
        dsk = S.sb("gdsk", [128, 16], F32)
        S.dma("sp", dsk.v(), I["ssm_d"][l:l + 1, :].partition_broadcast(128))
        NB = 2
        of = [S.sb("gof%d" % i, [128, 512], F32) for i in range(NB)]
        ob = [S.sb("gob%d" % i, [128, 512], F32) for i in range(NB)]
        gt = [S.sb("ggt%d" % i, [128, 512], F32) for i in range(NB)]
        yf = [S.sb("gyf%d" % i, [128, 1024], F32) for i in range(NB)]
        yb = [S.sb("gyb%d" % i, [128, 1024], F32) for i in range(NB)]
        sz = [S.sb("gsz%d" % i, [128, 1024], F32) for i in range(NB)]
        xt = [S.sb("gxt%d" % i, [128, 1024], BF16) for i in range(NB)]
        mt = [S.sb("gmt%d" % i, [128, D], BF16) for i in range(NB)]
        junk = S.sb("gjunk", [128, 512], F32)
        ss4 = S.sb("gss4", [128, 4], F32)
        ss2 = S.sb("gss2", [128, 2], F32)
        ptr = S.ps("gptr", [128, D], BF16)
        mTs = [S.sb("gmT%d" % i, [128, 16, 512], BF16) for i in range(2)]
        n = 0
        for g, (t0, gsz) in enumerate(cfg.groups):
            if last and g == 0:
                continue
            mT = mTs[g % 2]
            for j in range(gsz // 128):
                b = n % NB
                n += 1
                n0 = t0 + j * 128
                rows = slice(n0, n0 + 128)
                S.dma("sp", of[b].v(), self.HOF[rows, :])
                S.dma("sp", ob[b].v(), self.HOB[rows, :])
                S.dma("sp", gt[b].v(), self.HG[rows, :])
                S.dma("sp", yf[b].v(), self.YF[rows, :])
                S.dma("sp", yb[b].v(), self.YB[rows, :])
                S.dma("sp", sz[b].v(), self.SZ[rows, :])
                S.dma("sp", xt[b].v(), self.XTM[rows, :])
                S.dma("sp", mt[b][:, 512:1024], self.MDA[rows, :])
                S.tt("dve", of[b].v(), of[b].v(), ob[b].v(), ALU.add)
                for h in range(4):
                    S.act(junk[:, 0:128], of[b][:, h * 128:(h + 1) * 128], AF.Square, accum_out=ss4[:, h:h + 1])
                S.act(ss4.v(), ss4.v(), AF.Ln, scale=1.0 / 128, bias=self.epsc.v())
                S.act(ss4.v(), ss4.v(), AF.Exp, scale=-0.5)
                o3 = of[b].v().rearrange("p (h v) -> p h v", v=128)
                S.tt("dve", o3, o3, ss4.v().rearrange("p (h o) -> p h o", o=1).to_broadcast([128, 4, 128]), ALU.mult)
                S.tt("dve", o3, o3, hgw.v().rearrange("p (o v) -> p o v", o=1).to_broadcast([128, 4, 128]), ALU.mult)
                S.tt("dve", mt[b][:, 0:512], of[b].v(), gt[b].v(), ALU.mult)
                S.tt("dve", yf[b].v(), yf[b].v(), yb[b].v(), ALU.add)
                x3 = xt[b].v().rearrange("p (h q) -> p h q", q=64)
                y3 = yb[b].v().rearrange("p (h q) -> p h q", q=64)
                S.tt("dve", y3, x3, dsk.v().rearrange("p (h o) -> p h o", o=1).to_broadcast([128, 16, 64]), ALU.mult)
                S.tt("dve", yf[b].v(), yf[b].v(), yb[b].v(), ALU.add)
                S.tt("dve", yf[b].v(), yf[b].v(), sz[b].v(), ALU.mult)
                for gg in range(2):
                    S.act(junk.v(), yf[b][:, gg * 512:(gg + 1) * 512], AF.Square, accum_out=ss2[:, gg:gg + 1])
                S.act(ss2.v(), ss2.v(), AF.Ln, scale=1.0 / 512, bias=self.epsc.v())
                S.act(ss2.v(), ss2.v(), AF.Exp, scale=-0.5)
                yg = yf[b].v().rearrange("p (g v) -> p g v", v=512)
                S.tt("dve", yg, yg, ss2.v().rearrange("p (g o) -> p g o", o=1).to_broadcast([128, 2, 512]), ALU.mult)
                S.tt("dve", mt[b][:, 1024:2048], yf[b].v(), ssmw.v(), ALU.mult)
                for k in range(16):
                    S.tr(ptr[:, k * 128:(k + 1) * 128], mt[b][:, k * 128:(k + 1) * 128], self.ident.v())
                S.copy("act", mT[:, :, j * 128:(j + 1) * 128], ptr.v().rearrange("p (k t) -> p k t", t=128))
            S.dma("sp", self.MT[g][:, :, 0:gsz], mT[:, :, 0:gsz])
            self.dbg["mt%d_%d" % (l, g)] = (self.MT[g], [128, 16, 512], BF16)
        S.pop()

    def phase_outproj(self, l):
        S, I, cfg = self.S, self.I, self.cfg
        last = (l == DEPTH - 1)
        for kind in (1, 0):
            if kind == 1 and last:
                continue
            S.push()
            gain, shift = self._gain_shift(l, 1, "norm2_w", kind)
            gate = self._gate(l, 0, kind)
            wo = [S.sb("owo%d" % i, [128, 16, 512], BF16) for i in range(2)]
            mT = S.sb("omT", [128, 16, 512], BF16)
            h1 = S.sb("oh1", [128, 4, D], F32)
            tmp = S.sb("otmp", [128, 512], F32)
            tiles = {"junk": S.sb("ojunk", [128, D], F32), "ss": S.sb("oss", [128, 1], F32),
                     "rstd": S.sb("orstd", [128, 1], F32), "ub": S.sb("oub", [128, D], BF16),
                     "ptr": S.ps("optr", [128, D], BF16)}
            uTs = [S.sb("ouT%d" % i, [128, 16, 512], BF16) for i in range(2)]
            pso = [S.ps("opso%d" % i, [128, 512], F32) for i in range(4)]
            npso = 0
            nwo = 0
            for g, (t0, gsz) in enumerate(cfg.groups):
                if (g == 0) != (kind == 1):
                    continue
                nsub = gsz // 128
                S.dma("sp", mT[:, :, 0:gsz], self.MT[g][:, :, 0:gsz])
                if l == 0:
                    hsrc = I["xin"][t0:t0 + gsz, :]
                else:
                    hsrc = self.H[g][0:gsz, :]
                S.dma("sp", h1[:, 0:nsub, :], hsrc.rearrange("(s p) c -> p s c", p=128))
                for cb in range(4):
                    w = wo[nwo % 2]
                    nwo += 1
                    S.dma("sp", w.v(), self.WO[l][:, cb * 512:(cb + 1) * 512].rearrange("(k p) n -> p k n", p=128))
                    for s in range(nsub):
                        ps = pso[npso % 4]
                        npso += 1
                        for k in range(16):
                            S.mm(ps.v(), mT[:, k, s * 128:(s + 1) * 128], w[:, k, :], start=(k == 0), stop=(k == 15))
                        S.tt("dve", tmp.v(), ps.v(), gate[:, cb * 512:(cb + 1) * 512], ALU.mult)
                        S.tt("dve", h1[:, s, cb * 512:(cb + 1) * 512], h1[:, s, cb * 512:(cb + 1) * 512], tmp.v(), ALU.add)
                uT = uTs[g % 2]
                for s in range(nsub):
                    S.dma("sp", self.H1[g][s * 128:(s + 1) * 128, :], h1[:, s, :])
                    self._rms_mod_transpose(h1[:, s, :], gain, shift, uT, s, tiles)
                S.dma("sp", self.U2T[g][:, :, 0:gsz], uT[:, :, 0:gsz])
            S.pop()

    def phase_ffn(self, l):
        S, I, cfg = self.S, self.I, self.cfg
        last = (l == DEPTH - 1)
        FB = 512
        nfb = DFF // FB
        for kind in (1, 0):
            if kind == 1 and last:
                continue
            S.push()
            gate = self._gate(l, 1, kind)
            if last:
                fnw = S.sb("ffnw", [128, D], F32)
                self._bc_row(fnw, I["final_norm_w"][0:1, :])
            uTs = [S.sb("fuT%d" % i, [128, 16, 512], BF16) for i in range(1)]
            wg = [S.sb("fwg%d" % i, [128, 16, FB], BF16) for i in range(2)]
            wu = [S.sb("fwu%d" % i, [128, 16, FB], BF16) for i in range(2)]
            wd = [S.sb("fwd%d" % i, [128, FB // 128, D], BF16) for i in range(2)]
            actT = [S.sb("fact%d" % i, [128, FB // 128, 512], BF16) for i in range(2)]
            sg = [S.sb("fsg%d" % i, [128, 512], F32) for i in range(2)]
            acc = S.sb("facc", [128, 4, D], F32)
            h1t = [S.sb("fh1%d" % i, [128, D], F32) for i in range(1)]
            tmp = S.sb("ftmp", [128, D], F32)
            ss = S.sb("fss", [128, 1], F32)
            rstd = S.sb("frstd", [128, 1], F32)
            psg = [S.ps("fpsg%d" % i, [128, 512], F32) for i in range(2)]
            psu = [S.ps("fpsu%d" % i, [128, 512], F32) for i in range(2)]
            psd = [S.ps("fpsd%d" % i, [128, 512], F32) for i in range(4)]
            npg = 0
            npd = 0
            nwb = 0
            nh = 0
            for g, (t0, gsz) in enumerate(cfg.groups):
                if (g == 0) != (kind == 1):
                    continue
                nsub = gsz // 128
                uT = uTs[0]
                S.dma("sp", uT[:, :, 0:gsz], self.U2T[g][:, :, 0:gsz])
                def GU(fb):
                    nonlocal npg
                    b = fb % 2
                    f0 = fb * FB
                    S.dma("sp", wg[b].v(), self.WG[l][fb, :, :, :])
                    S.dma("sp", wu[b].v(), self.WU[l][fb, :, :, :])
                    S.dma("sp", wd[b].v(), self.WD[l][f0:f0 + FB, :].rearrange("(c p) n -> p c n", p=128))
                    at = actT[b]
                    for c in range(FB // 128):
                        pg = psg[npg % 2]
                        pu = psu[npg % 2]
                        sgt = sg[npg % 2]
                        npg += 1
                        for k in range(16):
                            S.mm(pg[:, 0:gsz], wg[b][:, k, c * 128:(c + 1) * 128], uT[:, k, 0:gsz], start=(k == 0), stop=(k == 15))
                        for k in range(16):
                            S.mm(pu[:, 0:gsz], wu[b][:, k, c * 128:(c + 1) * 128], uT[:, k, 0:gsz], start=(k == 0), stop=(k == 15))
                        S.act(sgt[:, 0:gsz], pg[:, 0:gsz], AF.Silu)
                        S.tt("dve", at[:, c, 0:gsz], sgt[:, 0:gsz], pu[:, 0:gsz], ALU.mult)

                def DN(fb):
                    nonlocal npd
                    b = fb % 2
                    at = actT[b]
                    for s in range(nsub):
                        for cb in range(4):
                            pd = psd[npd % 4]
                            npd += 1
                            for c in range(FB // 128):
                                S.mm(pd.v(), at[:, c, s * 128:(s + 1) * 128], wd[b][:, c, cb * 512:(cb + 1) * 512],
                                     start=(c == 0), stop=(c == FB // 128 - 1))
                            dst = acc[:, s, cb * 512:(cb + 1) * 512]
                            if fb == 0:
                                S.copy("dve", dst, pd.v())
                            else:
                                S.tt("dve", dst, dst, pd.v(), ALU.add)
                GU(0)
                for fb in range(nfb):
                    if fb + 1 < nfb:
                        GU(fb + 1)
                    DN(fb)
                for s in range(nsub):
                    h1 = h1t[0]
                    nh += 1
                    S.dma("sp", h1.v(), self.H1[g][s * 128:(s + 1) * 128, :])
                    S.tt("dve", tmp.v(), acc[:, s, :], gate.v(), ALU.mult)
                    S.tt("dve", h1.v(), h1.v(), tmp.v(), ALU.add)
                    if not last:
                        S.dma("sp", self.H[g][s * 128:(s + 1) * 128, :], h1.v())
                        self.dbg["h%d_%d" % (l, g)] = (self.H[g], [512, D], F32)
                    else:
                        S.act(tmp.v(), h1.v(), AF.Square, accum_out=ss.v())
                        S.act(rstd.v(), ss.v(), AF.Ln, scale=1.0 / D, bias=self.epsc.v())
                        S.act(rstd.v(), rstd.v(), AF.Exp, scale=-0.5)
                        S.stt(h1.v(), h1.v(), rstd[:, 0:1], fnw.v(), ALU.mult, ALU.mult)
                        r0 = t0 - CTX + s * 128
                        S.dma("sp", self.Y[r0:r0 + 128, :], h1.v())
            S.pop()

def _rope_tables(seq):
    half = 32
    inv = (1.0 / (10000.0 ** (np.arange(0, half, 2, dtype=np.float32) / np.float32(half)))).astype(np.float32)
    rows = seq // GRID_W
    r = np.repeat(np.arange(rows, dtype=np.float32), GRID_W)
    col = np.tile(np.arange(GRID_W, dtype=np.float32), rows)
    ar = (r[:, None] * inv[None, :]).astype(np.float32)
    ac = (col[:, None] * inv[None, :]).astype(np.float32)
    cr, sr, cc_, sc_ = np.cos(ar), np.sin(ar), np.cos(ac), np.sin(ac)
    C64 = np.concatenate([cr, cr, cc_, cc_], axis=1)
    S64 = np.concatenate([-sr, sr, -sc_, sc_], axis=1)
    C = np.concatenate([C64, C64], axis=1).T
    Sg = np.concatenate([S64, S64], axis=1).T
    Cf = np.concatenate([np.ones((128, CTX), np.float32), C.astype(np.float32)], axis=1)
    Sf = np.concatenate([np.zeros((128, CTX), np.float32), Sg.astype(np.float32)], axis=1)
    return np.ascontiguousarray(Cf), np.ascontiguousarray(Sf)


def _w_in_ext(w_in):
    sl = lambda a, b: w_in[:, :, a:b]
    hq, ff, fb, hi, hg = sl(0, 512), sl(512, 1024), sl(1024, 1536), sl(1536, 2048), sl(2048, 2560)
    dq, dk, dv = sl(2560, 3072), sl(3072, 3584), sl(3584, 4096)
    z, xbc, dtf, dtb = sl(4096, 5120), sl(5120, 6656), sl(6656, 6672), sl(6672, 6688)
    perm64 = np.concatenate([np.arange(16, 32), np.arange(0, 16), np.arange(48, 64), np.arange(32, 48)])
    perm = np.concatenate([blk * 64 + perm64 for blk in range(8)])
    dqs, dks = dq[:, :, perm], dk[:, :, perm]
    parts = [hq, ff, fb, xbc[:, :, 0:512], dq, dqs, dk, dks, xbc[:, :, 512:1024], xbc[:, :, 1024:1536],
             hi, hg, dv, z[:, :, 0:512], z[:, :, 512:1024], dtf, dtb]
    out = np.concatenate(parts, axis=2)
    assert out.shape[2] == NCOL
    return np.ascontiguousarray(out)


def host_inputs(cfg, b, inputs, shared):
    x, ctx, c, c_ctx = inputs["x"], inputs["ctx"], inputs["c"], inputs["c_ctx"]
    m = dict(shared)
    if b is None:
        m["xin"] = np.zeros((cfg.nt, D), np.float32)
        m["cc"] = np.zeros((2, D), np.float32)
        m["b_ada"] = np.zeros_like(shared["b_ada"])
        m["conv_b"] = np.zeros_like(shared["conv_b"])
        return m
    m["xin"] = np.ascontiguousarray(np.concatenate([ctx[b], x[b]], axis=0))
    m["cc"] = np.ascontiguousarray(np.stack([c[b], c_ctx], axis=0))
    return m


def shared_inputs(cfg, inputs):
    rc, rs = _rope_tables(cfg.seq)
    f = lambda a: np.ascontiguousarray(np.asarray(a, dtype=np.float32))
    return {
        "w_ada": f(inputs["w_ada"]), "b_ada": f(inputs["b_ada"]),
        "norm1_w": f(inputs["norm1_w"]), "norm2_w": f(inputs["norm2_w"]),
        "final_norm_w": f(inputs["final_norm_w"]).reshape(1, D),
        "w_in_ext": _w_in_ext(f(inputs["w_in"])),
        "hg_lb": f(inputs["hg_lb_logits"]), "hg_norm_w": f(inputs["hg_norm_w"]),
        "da_lambda": f(inputs["da_lambda"]).reshape(DEPTH, 256), "da_subln_w": f(inputs["da_subln_w"]),
        "conv_w": f(inputs["ssm_conv_w"]), "conv_b": f(inputs["ssm_conv_b"]),
        "dt_bias": f(inputs["ssm_dt_bias"]).reshape(DEPTH, 32), "a_log": f(inputs["ssm_a_log"]).reshape(DEPTH, 32),
        "ssm_d": f(inputs["ssm_d"]), "ssm_norm_w": f(inputs["ssm_norm_w"]),
        "w_out": f(inputs["w_out"]), "w_g": f(inputs["w_ffn_gate"]), "w_u": f(inputs["w_ffn_up"]),
        "w_d": f(inputs["w_ffn_down"]),
        "rope_c": rc, "rope_s": rs,
    }


def kernel(**inputs):
    inputs = {k: np.asarray(v) for k, v in inputs.items()}
    B, seq = inputs["x"].shape[0], inputs["x"].shape[1]
    cfg = Cfg(seq=seq)
    nc = build_program(cfg)
    shared = shared_inputs(cfg, inputs)
    owner = {core: i for i, core in enumerate(ACTIVE_CORES[:B])}
    in_maps = [host_inputs(cfg, owner.get(core), inputs, shared) for core in range(N_CORES)]
    res = run_bass_kernel_spmd(nc, in_maps, core_ids=list(range(N_CORES)))
    out = np.stack([res.results[ACTIVE_CORES[b]]["y"] for b in range(B)], axis=0)
    return out.astype(np.float32)
```

```python
import math
from contextlib import ExitStack

import numpy as np
import concourse.bass as bass
import concourse.mybir as mybir
from concourse.bass_utils import run_bass_kernel_spmd

F32 = mybir.dt.float32
BF16 = mybir.dt.bfloat16
AF = mybir.ActivationFunctionType
ALU = mybir.AluOpType

D = 2048
DEPTH = 2
CTX = 256
GRID_W = 64
EPS = 1e-6
DFF = 5632
NCOL = 7712
N_CORES = 8
SAME_ENGINE_SYNC = True
STORES_ON_POOL = True
ACTIVE_CORES = (0, 1, 4, 5)


class Slot:
    __slots__ = ("sem", "cnt")

    def __init__(self, sem):
        self.sem = sem
        self.cnt = 0


class Buf:
    __slots__ = ("name", "lw", "rd", "slot")

    def __init__(self, name):
        self.name = name
        self.lw = {}
        self.rd = {}
        self.slot = None


class View:
    __slots__ = ("ap", "buf")

    def __init__(self, ap, buf):
        self.ap = ap
        self.buf = buf

    def __getitem__(self, k):
        return View(self.ap[k], self.buf)

    def rearrange(self, s, **kw):
        return View(self.ap.rearrange(s, **kw), self.buf)

    def to_broadcast(self, shape):
        return View(self.ap.to_broadcast(list(shape)), self.buf)

    def partition_broadcast(self, n):
        return View(self.ap.partition_broadcast(n), self.buf)

    def bitcast(self, dt):
        return View(self.ap.bitcast(dt), self.buf)


class TT:
    def __init__(self, handle, name, is_ap=False):
        self.h = handle
        self.buf = Buf(name)
        self.is_ap = is_ap

    def __getitem__(self, k):
        return View(self.h[k], self.buf)

    def v(self):
        return View(self.h[:] if not self.is_ap else self.h, self.buf)


def _aps(x):
    return x.ap if isinstance(x, View) else x


def _bufs(*xs):
    out = []
    for x in xs:
        if isinstance(x, View) and x.buf is not None and x.buf not in out:
            out.append(x.buf)
    return out


class Sched:
    ENGS = ("pe", "act", "dve", "pool", "sp")

    def __init__(self, nc, stack, same_engine_sync=True):
        self.nc = nc
        self.stacks = [stack]
        self.prog = {e: [] for e in self.ENGS}
        self.sem = {}
        self.tick = {}
        for e in ("pe", "act", "dve", "pool"):
            self.sem[e] = stack.enter_context(nc.semaphore("s_" + e))
            self.tick[e] = 0
        self.seen = {e: {} for e in self.ENGS}
        self.same = same_engine_sync
        self.slots = []
        self.free_slots = []
        self.scope_bufs = [[]]
        self.ninst = 0
        self.base = stack
        self.uid = 0

    def push(self):
        st = ExitStack()
        self.stacks.append(st)
        self.scope_bufs.append([])
        return st

    def pop(self):
        self.barrier()
        st = self.stacks.pop()
        st.close()
        for b in self.scope_bufs.pop():
            if b.slot is not None:
                self.free_slots.append(b.slot)
                b.slot = None

    def sb(self, name, shape, dtype):
        self.uid += 1
        nm = "%s_%d" % (name, self.uid)
        h = self.stacks[-1].enter_context(self.nc.sbuf_tensor(nm, list(shape), dtype))
        t = TT(h, nm)
        self.scope_bufs[-1].append(t.buf)
        return t

    def ps(self, name, shape, dtype):
        self.uid += 1
        nm = "%s_%d" % (name, self.uid)
        h = self.stacks[-1].enter_context(self.nc.psum_tensor(nm, list(shape), dtype))
        return TT(h, nm)

    def dram(self, name, shape, dtype, kind="Internal"):
        h = self.nc.dram_tensor(name, list(shape), dtype, kind=kind)
        return TT(h.ap(), name, is_ap=True)

    def dslot(self, buf):
        if buf.slot is None:
            if self.free_slots:
                buf.slot = self.free_slots.pop()
            else:
                buf.slot = Slot(self.base.enter_context(self.nc.semaphore("d%d" % len(self.slots))))
                self.slots.append(buf.slot)
        return buf.slot

    def _needs(self, eng, reads, writes, partial, is_dma=False):
        needs = {}

        def add(d):
            for s, v in d.items():
                if needs.get(s, 0) < v:
                    needs[s] = v
        for b in reads:
            add(b.lw)
        for b in writes:
            add(b.rd)
            if not partial:
                add(b.lw)
        waits = []
        seen = self.seen[eng]
        own = self.sem.get(eng)
        for s, v in needs.items():
            if s is own and not is_dma and (eng == "pe" or not self.same):
                continue
            if seen.get(s, 0) < v:
                seen[s] = v
                waits.append((s, v))
        return waits

    def _commit(self, ev, reads, writes, partial):
        s, v = ev
        for b in reads:
            if b.rd.get(s, 0) < v:
                b.rd[s] = v
        for b in writes:
            if partial:
                if b.lw.get(s, 0) < v:
                    b.lw[s] = v
            else:
                b.lw = {s: v}
                b.rd = {}

    def op(self, eng, fn, reads=(), writes=(), partial=False):
        waits = self._needs(eng, reads, writes, partial)
        self.tick[eng] += 1
        ev = (self.sem[eng], self.tick[eng])
        self.prog[eng].append((waits, fn, ev[0], 1))
        self._commit(ev, reads, writes, partial)
        self.ninst += 1
        return ev

    def dma(self, q, out, in_, sembuf=None, partial=None, slow=False):
        ob, ib = out.buf, in_.buf
        o_ap, i_ap = out.ap, in_.ap
        o_is_dram = "DRAM" in str(o_ap.space).upper() or "HBM" in str(o_ap.space).upper()
        i_is_dram = "DRAM" in str(i_ap.space).upper() or "HBM" in str(i_ap.space).upper()
        if q == "sp" and o_is_dram and not i_is_dram and STORES_ON_POOL:
            q = "pool"
        if q == "sp!":
            q = "sp"
        if sembuf is None:
            o_dram = "DRAM" in str(o_ap.space).upper() or "HBM" in str(o_ap.space).upper()
            sembuf = ib if o_dram else ob
        if partial is None:
            partial = "DRAM" in str(o_ap.space).upper() or "HBM" in str(o_ap.space).upper()
        slot = self.dslot(sembuf)
        sem = slot.sem
        reads, writes = [ib], [ob]
        waits = self._needs(q, reads, writes, partial, True)
        prev = 16 * slot.cnt
        if prev and self.seen[q].get(sem, 0) < prev:
            self.seen[q][sem] = prev
            waits.append((sem, prev))
        slot.cnt += 1
        ev = (sem, 16 * slot.cnt)
        kw = {"allow_slow_non_contiguous": True} if slow else {}
        self.prog[q].append((waits, lambda e: e.dma_start(out=o_ap, in_=i_ap, **kw), sem, 16))
        self._commit(ev, reads, writes, partial)
        self.ninst += 1
        return ev

    def barrier(self):
        allv = [(self.sem[e], self.tick[e]) for e in ("pe", "act", "dve", "pool") if self.tick[e]]
        allv += [(sl.sem, 16 * sl.cnt) for sl in self.slots if sl.cnt]
        for e in self.ENGS:
            waits = []
            seen = self.seen[e]
            for s, v in allv:
                if s is self.sem.get(e):
                    continue
                if seen.get(s, 0) < v:
                    seen[s] = v
                    waits.append((s, v))
            if waits:
                self.prog[e].append((waits, None, None, 0))

    def act(self, out, in_, func, scale=None, bias=None, accum_out=None, eng="act"):
        kw = {}
        rd = _bufs(in_)
        wr = _bufs(out)
        if scale is not None:
            kw["scale"] = _aps(scale)
            rd += _bufs(scale)
        if bias is not None:
            kw["bias"] = _aps(bias)
            rd += _bufs(bias)
        if accum_out is not None:
            kw["accum_out"] = _aps(accum_out)
            wr += _bufs(accum_out)
        o, i = out.ap, in_.ap
        return self.op("act", lambda e: e.activation(out=o, in_=i, func=func, **kw), rd, wr)

    def tt(self, eng, out, in0, in1, op):
        o, a, b = out.ap, in0.ap, in1.ap
        return self.op(eng, lambda e: e.tensor_tensor(out=o, in0=a, in1=b, op=op), _bufs(in0, in1), _bufs(out))

    def ts(self, eng, out, in0, s1, op0, s2=None, op1=None):
        o, a = out.ap, in0.ap
        x1, x2 = _aps(s1), _aps(s2)
        kw = {}
        if op1 is not None:
            kw["op1"] = op1
        return self.op(eng, lambda e: e.tensor_scalar(out=o, in0=a, scalar1=x1, scalar2=x2, op0=op0, **kw),
                       _bufs(in0, s1, s2), _bufs(out))

    def stt(self, out, in0, scalar, in1, op0, op1, eng="dve"):
        o, a, b = out.ap, in0.ap, in1.ap
        sc = _aps(scalar)
        return self.op(eng, lambda e: e.scalar_tensor_tensor(out=o, in0=a, scalar=sc, in1=b, op0=op0, op1=op1),
                       _bufs(in0, scalar, in1), _bufs(out))

    def copy(self, eng, out, in_):
        o, i = out.ap, in_.ap
        if eng == "act":
            return self.op("act", lambda e: e.activation(out=o, in_=i, func=AF.Copy), _bufs(in_), _bufs(out))
        return self.op(eng, lambda e: e.tensor_copy(out=o, in_=i), _bufs(in_), _bufs(out))

    def memset(self, eng, out, val):
        o = out.ap
        return self.op(eng, lambda e: e.memset(o, val), [], _bufs(out))

    def recip(self, out, in_, eng="dve"):
        o, i = out.ap, in_.ap
        return self.op(eng, lambda e: e.reciprocal(out=o, in_=i), _bufs(in_), _bufs(out))

    def ttr(self, out, in0, in1, accum_out, op0=ALU.mult, op1=ALU.add, scale=1.0, scalar=0.0):
        o, a, b, acc = out.ap, in0.ap, in1.ap, accum_out.ap
        return self.op("dve", lambda e: e.tensor_tensor_reduce(out=o, in0=a, in1=b, scale=scale, scalar=scalar,
                                                               op0=op0, op1=op1, accum_out=acc),
                       _bufs(in0, in1), _bufs(out, accum_out))

    def scan(self, out, d0, d1, initial=0.0, op0=ALU.mult, op1=ALU.add):
        o, a, b = out.ap, d0.ap, d1.ap
        ini = _aps(initial)
        return self.op("dve", lambda e: e.tensor_tensor_scan(out=o, data0=a, data1=b, initial=ini, op0=op0, op1=op1),
                       _bufs(d0, d1, initial), _bufs(out))

    def reduce(self, out, in_, op=ALU.add, eng="dve"):
        o, i = out.ap, in_.ap
        return self.op(eng, lambda e: e.tensor_reduce(out=o, in_=i, axis=mybir.AxisListType.X, op=op),
                       _bufs(in_), _bufs(out))

    def mm(self, out, lhsT, rhs, start=True, stop=True, skip=False):
        o, a, b = out.ap, lhsT.ap, rhs.ap
        kw = {"skip_group_check": True} if skip else {}
        return self.op("pe", lambda e: e.matmul(o, a, b, start=start, stop=stop, **kw), _bufs(lhsT, rhs), _bufs(out))

    def tr(self, out, in_, ident):
        o, a, b = out.ap, in_.ap, ident.ap
        return self.op("pe", lambda e: e.transpose(out=o, in_=a, identity=b), _bufs(in_, ident), _bufs(out))

    def aselect(self, out, in_, pattern, cmp, fill, base, cm):
        o, i = out.ap, in_.ap
        return self.op("pool", lambda e: e.affine_select(out=o, in_=i, pattern=pattern, compare_op=cmp, fill=fill,
                                                         base=base, channel_multiplier=cm), _bufs(in_), _bufs(out))

    def final_wait(self, eng, bufs):
        needs = {}
        for b in bufs:
            for s, v in b.lw.items():
                if needs.get(s, 0) < v:
                    needs[s] = v
        self.prog[eng].append((list(needs.items()), None, None, 0))

    def emit(self):
        nc = self.nc
        prog = self.prog

        def run(engine, items):
            for waits, fn, sem, inc in items:
                for s, v in waits:
                    engine.wait_ge(s, v)
                if fn is not None:
                    fn(engine).then_inc(sem, inc)

        with nc.Block() as block:
            @block.tensor
            def _(e):
                run(e, prog["pe"])

            @block.scalar
            def _(e):
                run(e, prog["act"])

            @block.vector
            def _(e):
                run(e, prog["dve"])

            @block.gpsimd
            def _(e):
                run(e, prog["pool"])

            @block.sync
            def _(e):
                run(e, prog["sp"])


class Cfg:
    def __init__(self, seq=4096, debug=(), stop_after=None, depth=DEPTH):
        self.seq = seq
        self.nt = CTX + seq
        self.debug = tuple(debug)
        self.stop_after = stop_after
        self.depth = depth
        self.groups = [(0, CTX)] + [(CTX + i * 512, 512) for i in range(seq // 512)]
        self.nch = self.nt // 64
        self.ntile = self.nt // 128


INPUT_SPECS = None


def input_shapes(cfg):
    nt = cfg.nt
    return {
        "xin": ([nt, D], F32),
        "cc": ([2, D], F32),
        "w_ada": ([DEPTH, D, 6 * D], F32),
        "b_ada": ([DEPTH, 6 * D], F32),
        "norm1_w": ([DEPTH, D], F32),
        "norm2_w": ([DEPTH, D], F32),
        "final_norm_w": ([1, D], F32),
        "w_in_ext": ([DEPTH, D, NCOL], F32),
        "hg_lb": ([DEPTH, 512], F32),
        "hg_norm_w": ([DEPTH, 128], F32),
        "da_lambda": ([DEPTH, 256], F32),
        "da_subln_w": ([DEPTH, 128], F32),
        "conv_w": ([DEPTH, 3, 1536], F32),
        "conv_b": ([DEPTH, 1536], F32),
        "dt_bias": ([DEPTH, 32], F32),
        "a_log": ([DEPTH, 32], F32),
        "ssm_d": ([DEPTH, 16], F32),
        "ssm_norm_w": ([DEPTH, 1024], F32),
        "w_out": ([DEPTH, D, D], F32),
        "w_g": ([DEPTH, D, DFF], F32),
        "w_u": ([DEPTH, D, DFF], F32),
        "w_d": ([DEPTH, DFF, D], F32),
        "rope_c": ([128, nt], F32),
        "rope_s": ([128, nt], F32),
    }


def build_program(cfg):
    nc = bass.Bass("TRN2", target_bir_lowering=False)
    nt, seq = cfg.nt, cfg.seq
    with ExitStack() as st:
        S = Sched(nc, st, same_engine_sync=SAME_ENGINE_SYNC)
        I = {}
        for name, (shape, dt) in input_shapes(cfg).items():
            I[name] = S.dram(name, shape, dt, kind="ExternalInput")
        Y = S.dram("y", [seq, D], F32, kind="ExternalOutput")
        P = Prog(S, cfg, I, Y)
        P.build()
        S.emit()
    return nc


class Prog:
    def __init__(self, S, cfg, I, Y):
        self.S, self.cfg, self.I, self.Y = S, cfg, I, Y
        self.dbg = {}

    def dbg_out(self, name, src, shape, dtype):
        S = self.S
        o = S.dram("dbg_" + name, shape, dtype, kind="ExternalOutput")
        S.dma("sp", o.v(), src.v(), sembuf=o.buf)
        self.outs.append(o)

    def dbg_sb(self, name, view, shape, dtype):
        if name not in self.cfg.debug:
            return
        o = self.S.dram("dbg_" + name, shape, dtype, kind="ExternalOutput")
        self.S.dma("sp", o.v(), view)
        self.outs.append(o)

    def build(self):
        S, cfg, I = self.S, self.cfg, self.I
        nt = cfg.nt
        self.outs = [self.Y]
        self.WIN = [S.dram("WIN%d" % l, [D, NCOL], BF16) for l in range(DEPTH)]
        self.WO = [S.dram("WO%d" % l, [D, D], BF16) for l in range(DEPTH)]
        self.WG = [S.dram("WG%d" % l, [DFF // 512, 128, 16, 512], BF16) for l in range(DEPTH)]
        self.WU = [S.dram("WU%d" % l, [DFF // 512, 128, 16, 512], BF16) for l in range(DEPTH)]
        self.WD = [S.dram("WD%d" % l, [DFF, D], BF16) for l in range(DEPTH)]
        self.MOD = [S.dram("MOD%d" % l, [2, 6 * D], F32) for l in range(DEPTH)]
        ng = len(cfg.groups)
        self.UT = [S.dram("UT%d" % g, [128, 16, 512], BF16) for g in range(ng)]
        self.MT = [S.dram("MT%d" % g, [128, 16, 512], BF16) for g in range(ng)]
        self.U2T = [S.dram("U2T%d" % g, [128, 16, 512], BF16) for g in range(ng)]
        self.H1 = [S.dram("H1_%d" % g, [512, D], F32) for g in range(ng)]
        self.H = [S.dram("H_%d" % g, [512, D], F32) for g in range(ng)]
        self.HQT = S.dram("HQT", [4, 128, nt], F32)
        self.SFT = S.dram("SFT", [4, 128, nt], F32)
        self.SBT = S.dram("SBT", [4, 128, nt], F32)
        self.QT = S.dram("QT", [4, 128, nt], BF16)
        self.KT = S.dram("KT", [4, 128, nt], BF16)
        self.XBCT = S.dram("XBCT", [12, 128, nt], F32)
        self.HV = S.dram("HV", [nt, 512], BF16)
        self.HG = S.dram("HG", [nt, 512], F32)
        self.DV = S.dram("DV", [nt, 512], BF16)
        self.SZ = S.dram("SZ", [nt, 1024], F32)
        self.DT = S.dram("DT", [nt, 32], F32)
        self.HOF = S.dram("HOF", [nt, 512], F32)
        self.HOB = S.dram("HOB", [nt, 512], F32)
        self.MDA = S.dram("MDA", [nt, 512], BF16)
        self.XTM = S.dram("XTM", [nt, 1024], BF16)
        self.YF = S.dram("YF", [nt, 1024], F32)
        self.YB = S.dram("YB", [nt, 1024], F32)

        self.ident = S.sb("ident", [128, 128], BF16)
        identf = S.sb("identf", [128, 128], F32)
        self.identf = identf
        self.epsc = S.sb("epsc", [128, 1], F32)
        S.memset("pool", self.epsc.v(), EPS)
        self.onescol = S.sb("onescol", [128, 1], F32)
        S.memset("pool", self.onescol.v(), 1.0)
        S.memset("pool", identf.v(), 0.0)
        S.aselect(identf.v(), identf.v(), [[-1, 128]], ALU.not_equal, 1.0, 0, 1)
        S.copy("dve", self.ident.v(), identf.v())
        onesf = S.sb("onesf", [64, 64], F32)
        self.mask_f = S.sb("mask_f", [64, 64], BF16)
        self.mask_b = S.sb("mask_b", [64, 64], BF16)
        tmpm = S.sb("tmpm", [64, 64], F32)
        S.memset("pool", onesf.v(), 1.0)
        S.aselect(tmpm.v(), onesf.v(), [[1, 64]], ALU.is_ge, 0.0, 0, -1)
        S.copy("dve", self.mask_f.v(), tmpm.v())
        tmpm2 = S.sb("tmpm2", [64, 64], F32)
        S.aselect(tmpm2.v(), onesf.v(), [[-1, 64]], ALU.is_ge, 0.0, 0, 1)
        S.copy("dve", self.mask_b.v(), tmpm2.v())

        for l in range(cfg.depth):
            last = (l == DEPTH - 1)
            self.phase_mod(l)
            if l == 0:
                self.convert_weights(0, ["in", "o", "g", "u", "d"])
            if cfg.stop_after == ("mod", l):
                break
            self.phase_norm1(l)
            if cfg.stop_after == ("norm1", l):
                break
            self.phase_inproj(l)
            if cfg.stop_after == ("inproj", l):
                break
            if l + 1 < cfg.depth:
                self.convert_weights(l + 1, ["in", "o", "g", "u", "d"])
            self.phase_hgrn(l)
            if cfg.stop_after == ("hgrn", l):
                break
            self.phase_da(l)
            if cfg.stop_after == ("da", l):
                break
            self.phase_ssd(l)
            if cfg.stop_after == ("ssd", l):
                break
            self.phase_mixout(l)
            if cfg.stop_after == ("mixout", l):
                break
            self.phase_outproj(l)
            if cfg.stop_after == ("outproj", l):
                break
            self.phase_ffn(l)
            if cfg.stop_after == ("ffn", l):
                break
        for name, (src, shape, dt) in self.dbg.items():
            if name in cfg.debug:
                self.dbg_out(name, src, shape, dt)
        S.final_wait("sp", [o.buf for o in self.outs])

    def convert_weights(self, l, which):
        S, I = self.S, self.I
        if not hasattr(self, "_cvsems"):
            self._cvsems = [Buf("cv%d" % i) for i in range(4)]
            self._cvk = 0
        for name in which:
            if name in ("g", "u"):
                src, dst = (I["w_g"], self.WG[l]) if name == "g" else (I["w_u"], self.WU[l])
                for kc in range(16):
                    S.dma("pool", dst[:, :, kc, :].rearrange("f p n -> p f n"),
                          src[l, kc * 128:(kc + 1) * 128, :].rearrange("p (f n) -> p f n", n=512),
                          sembuf=self._cvsems[self._cvk % 4])
                    self._cvk += 1
                continue
            src, dst, rows = {"in": (I["w_in_ext"], self.WIN[l], D), "o": (I["w_out"], self.WO[l], D),
                              "d": (I["w_d"], self.WD[l], DFF)}[name]
            for r0 in range(0, rows, 128):
                S.dma("pool", dst[r0:r0 + 128, :], src[l, r0:r0 + 128, :], sembuf=self._cvsems[self._cvk % 4])
                self._cvk += 1

    def phase_mod(self, l):
        S, I = self.S, self.I
        S.push()
        cT = S.sb("cT", [128, 16, 2], F32)
        scT = S.sb("scT", [128, 16, 2], BF16)
        bada = S.sb("bada", [2, 6 * D], F32)
        modsb = S.sb("modsb", [2, 6 * D], F32)
        wt = [S.sb("wada%d" % i, [128, 16, 512], BF16) for i in range(2)]
        pm = [S.ps("pmod%d" % i, [128, 512], F32) for i in range(2)]
        for t in range(2):
            S.dma("sp", cT[:, :, t], I["cc"][t, :].rearrange("(k p) -> p k", p=128), slow=True)
        S.dma("sp", bada.v(), I["b_ada"][l:l + 1, :].partition_broadcast(2))
        S.act(scT.v(), cT.v(), AF.Silu)
        for j in range(24):
            w = wt[j % 2]
            S.dma("pool", w.v(), I["w_ada"][l, :, j * 512:(j + 1) * 512].rearrange("(k p) n -> p k n", p=128))
            p = pm[j % 2]
            for k in range(16):
                S.mm(p[0:2, :], scT[:, k, :], w[:, k, :], start=(k == 0), stop=(k == 15))
            S.tt("dve", modsb[:, j * 512:(j + 1) * 512], p[0:2, :], bada[:, j * 512:(j + 1) * 512], ALU.add)
        S.dma("sp", self.MOD[l].v(), modsb.v())
        self.dbg["mod%d" % l] = (self.MOD[l], [2, 6 * D], F32)
        S.pop()

    def _bc_row(self, dst, src_row):
        self.S.dma("sp", dst.v(), src_row.partition_broadcast(128))

    def _gain_shift(self, l, which, norm_key, t):
        S, I = self.S, self.I
        nw = S.sb("nw", [128, D], F32)
        self._bc_row(nw, I[norm_key][l:l + 1, :])
        gain = S.sb("gain", [128, D], F32)
        shift = S.sb("shift", [128, D], F32)
        base = which * 3 * D
        self._bc_row(shift, self.MOD[l][t:t + 1, base:base + D])
        self._bc_row(gain, self.MOD[l][t:t + 1, base + D:base + 2 * D])
        S.stt(gain.v(), gain.v(), 1.0, nw.v(), ALU.add, ALU.mult)
        return gain, shift

    def _gate(self, l, which, t):
        gate = self.S.sb("gate", [128, D], F32)
        base = which * 3 * D
        self._bc_row(gate, self.MOD[l][t:t + 1, base + 2 * D:base + 3 * D])
        return gate

    def _rms_mod_transpose(self, src_view, gain, shift, uT, j, tiles):
        S = self.S
        junk, ss, rstd, ub, ptr = tiles["junk"], tiles["ss"], tiles["rstd"], tiles["ub"], tiles["ptr"]
        S.act(junk.v(), src_view, AF.Square, accum_out=ss.v())
        S.act(rstd.v(), ss.v(), AF.Ln, scale=1.0 / D, bias=self.epsc.v())
        S.act(rstd.v(), rstd.v(), AF.Exp, scale=-0.5)
        S.stt(junk.v(), src_view, rstd[:, 0:1], gain.v(), ALU.mult, ALU.mult)
        S.tt("dve", ub.v(), junk.v(), shift.v(), ALU.add)
        for k in range(16):
            S.tr(ptr[:, k * 128:(k + 1) * 128], ub[:, k * 128:(k + 1) * 128], self.ident.v())
        S.copy("act", uT[:, :, j * 128:(j + 1) * 128], ptr.v().rearrange("p (k t) -> p k t", t=128))

    def phase_norm1(self, l):
        S, I, cfg = self.S, self.I, self.cfg
        S.push()
        g_lat, s_lat = self._gain_shift(l, 0, "norm1_w", 0)
        g_ctx, s_ctx = self._gain_shift(l, 0, "norm1_w", 1)
        ht = [S.sb("ht%d" % i, [128, D], F32) for i in range(2)]
        tiles = {"junk": S.sb("junk", [128, D], F32), "ss": S.sb("ss", [128, 1], F32),
                 "rstd": S.sb("rstd", [128, 1], F32), "ub": S.sb("ub", [128, D], BF16),
                 "ptr": S.ps("ptr", [128, D], BF16)}
        uTs = [S.sb("uT%d" % i, [128, 16, 512], BF16) for i in range(2)]
        n = 0
        for g, (t0, gsz) in enumerate(cfg.groups):
            uT = uTs[g % 2]
            gain, shift = (g_ctx, s_ctx) if g == 0 else (g_lat, s_lat)
            for j in range(gsz // 128):
                h = ht[n % 2]
                n += 1
                if l == 0:
                    src = I["xin"][t0 + j * 128:t0 + (j + 1) * 128, :]
                else:
                    src = self.H[g][j * 128:(j + 1) * 128, :]
                S.dma("sp", h.v(), src)
                self._rms_mod_transpose(h.v(), gain, shift, uT, j, tiles)
            S.dma("sp", self.UT[g][:, :, 0:gsz], uT[:, :, 0:gsz])
        S.pop()

    def phase_inproj(self, l):
        S, I, cfg = self.S, self.I, self.cfg
        nt = cfg.nt
        S.push()
        WIN = self.WIN[l]
        uTs = [S.sb("uTi%d" % i, [128, 16, 512], BF16) for i in range(2)]
        wts = [S.sb("wi%d" % i, [128, 16, 1024], BF16) for i in range(2)]
        rc = [S.sb("rc%d" % i, [128, 512], F32) for i in range(2)]
        rs = [S.sb("rs%d" % i, [128, 512], F32) for i in range(2)]
        stg = [S.sb("stg%d" % i, [128, 4, 512], F32) for i in range(2)]
        stgb = [S.sb("stgb%d" % i, [128, 4, 512], BF16) for i in range(2)]
        t1 = S.sb("ropet1", [128, 512], F32)
        t2 = S.sb("ropet2", [128, 512], F32)
        psA = [S.ps("psA%d" % i, [128, 512], F32) for i in range(4)]
        nstg = 0
        npsum = 0
        nw = 0
        for g, (t0, gsz) in enumerate(cfg.groups):
            uT = uTs[g % 2]
            S.dma("sp", uT[:, :, 0:gsz], self.UT[g][:, :, 0:gsz])
            S.dma("sp", rc[g % 2][:, 0:gsz], I["rope_c"][:, t0:t0 + gsz])
            S.dma("sp", rs[g % 2][:, 0:gsz], I["rope_s"][:, t0:t0 + gsz])
            RC, RS = rc[g % 2], rs[g % 2]
            nsub = gsz // 128
            for pair in range(8):
                w = wts[nw % 2]
                nw += 1
                c0 = pair * 1024
                ncols = min(1024, NCOL - c0)
                S.dma("sp", w[:, :, 0:ncols], WIN[:, c0:c0 + ncols].rearrange("(k p) n -> p k n", p=128))

                def fm_chunk(colofs, ps):
                    for k in range(16):
                        S.mm(ps[:, 0:gsz], w[:, k, colofs:colofs + 128], uT[:, k, 0:gsz], start=(k == 0), stop=(k == 15))

                def fm_block(half, func, dst, dst_c0, bf=False):
                    nonlocal nstg, npsum
                    sg = (stgb if bf else stg)[nstg % 2]
                    nstg += 1
                    for c in range(4):
                        ps = psA[npsum % 4]
                        npsum += 1
                        fm_chunk(half * 512 + c * 128, ps)
                        if func is None:
                            S.copy("dve", sg[:, c, 0:gsz], ps[:, 0:gsz])
                        else:
                            S.act(sg[:, c, 0:gsz], ps[:, 0:gsz], func)
                    S.dma("sp", dst[dst_c0:dst_c0 + 4, :, t0:t0 + gsz].rearrange("c p t -> p c t"), sg[:, :, 0:gsz])

                def rope_block(dst):
                    nonlocal nstg, npsum
                    sg = stgb[nstg % 2]
                    nstg += 1
                    for c in range(4):
                        ps1 = psA[npsum % 4]
                        ps2 = psA[(npsum + 1) % 4]
                        npsum += 2
                        fm_chunk(c * 128, ps1)
                        fm_chunk(512 + c * 128, ps2)
                        S.tt("dve", t1[:, 0:gsz], ps1[:, 0:gsz], RC[:, 0:gsz], ALU.mult)
                        S.tt("dve", t2[:, 0:gsz], ps2[:, 0:gsz], RS[:, 0:gsz], ALU.mult)
                        S.tt("dve", sg[:, c, 0:gsz], t1[:, 0:gsz], t2[:, 0:gsz], ALU.add)
                    S.dma("sp", dst[:, :, t0:t0 + gsz].rearrange("c p t -> p c t"), sg[:, :, 0:gsz])

                def tm_block(half, width, func, dst, dst_c0, bf):
                    nonlocal nstg, npsum
                    sg = (stgb if bf else stg)[nstg % 2]
                    nstg += 1
                    for s in range(nsub):
                        ps = psA[npsum % 4]
                        npsum += 1
                        for k in range(16):
                            S.mm(ps[:, 0:width], uT[:, k, s * 128:(s + 1) * 128], w[:, k, half * 512:half * 512 + width],
                                 start=(k == 0), stop=(k == 15))
                        if func is None:
                            S.copy("dve", sg[:, s, 0:width], ps[:, 0:width])
                        else:
                            S.act(sg[:, s, 0:width], ps[:, 0:width], func)
                    S.dma("sp", dst[t0:t0 + gsz, dst_c0:dst_c0 + width].rearrange("(s p) c -> p s c", p=128),
                          sg[:, 0:nsub, 0:width])

                if pair == 0:
                    fm_block(0, AF.Silu, self.HQT, 0)
                    fm_block(1, AF.Sigmoid, self.SFT, 0)
                elif pair == 1:
                    fm_block(0, AF.Sigmoid, self.SBT, 0)
                    fm_block(1, None, self.XBCT, 0)
                elif pair == 2:
                    rope_block(self.QT)
                elif pair == 3:
                    rope_block(self.KT)
                elif pair == 4:
                    fm_block(0, None, self.XBCT, 4)
                    fm_block(1, None, self.XBCT, 8)
                elif pair == 5:
                    tm_block(0, 512, None, self.HV, 0, True)
                    tm_block(1, 512, AF.Silu, self.HG, 0, False)
                elif pair == 6:
                    tm_block(0, 512, None, self.DV, 0, True)
                    tm_block(1, 512, AF.Silu, self.SZ, 0, False)
                elif pair == 7:
                    tm_block(0, 512, AF.Silu, self.SZ, 512, False)
                    tm_block(1, 32, None, self.DT, 0, False)
        for nm, t, shape, dt in (("hqt", self.HQT, [4, 128, nt], F32), ("sft", self.SFT, [4, 128, nt], F32),
                                 ("sbt", self.SBT, [4, 128, nt], F32), ("qt", self.QT, [4, 128, nt], BF16),
                                 ("kt", self.KT, [4, 128, nt], BF16), ("xbct", self.XBCT, [12, 128, nt], F32),
                                 ("hv", self.HV, [nt, 512], BF16), ("hg", self.HG, [nt, 512], F32),
                                 ("dv", self.DV, [nt, 512], BF16), ("sz", self.SZ, [nt, 1024], F32),
                                 ("dt", self.DT, [nt, 32], F32)):
            self.dbg["%s%d" % (nm, l)] = (t, shape, dt)
        S.pop()


    def _orders(self, L=64):
        nch = self.cfg.nt // L
        nctx = CTX // L
        fwd = list(range(nch))
        bwd = list(range(nctx - 1, -1, -1)) + list(range(nch - 1, nctx - 1, -1))
        return fwd, bwd

    def phase_hgrn(self, l):
        S, I, cfg = self.S, self.I, self.cfg
        LH = 32
        nt, nch = cfg.nt, cfg.nt // LH
        S.push()
        lbraw = S.sb("lbraw", [128, 2, 4], F32)
        for ll in range(2):
            S.dma("sp", lbraw[:, ll, :], I["hg_lb"][ll, :].rearrange("(h k) -> k h", k=128), slow=True)
        lb = S.sb("lb", [128, 4], F32)
        omlb = S.sb("omlb", [128, 4], F32)
        if l == 0:
            S.memset("dve", lb.v(), 0.0)
        else:
            S.tt("dve", lb.v(), lbraw[:, 1, :], lbraw[:, 0, :], ALU.subtract)
            S.act(lb.v(), lb.v(), AF.Sigmoid)
        S.ts("dve", omlb.v(), lb.v(), -1.0, ALU.mult, 1.0, ALU.add)
        ones = S.sb("ones", [128, nt], BF16)
        S.memset("pool", ones.v(), 1.0)
        Q = S.sb("hQ", [128, nt], F32)
        X1 = S.sb("hX1", [128, nt], F32)
        X2 = S.sb("hX2", [128, nt], F32)
        X3 = S.sb("hX3", [128, nt], F32)
        qt = [S.sb("hqt%d" % d, [128, nt], BF16) for d in range(2)]
        kt = [S.sb("hkt%d" % d, [128, nt], BF16) for d in range(2)]
        V = S.sb("hV", [LH, nch, 128], BF16)
        gg = [S.sb("hgg%d" % d, [128, nch], F32) for d in range(2)]
        rt = S.sb("hrt", [128, nch], F32)
        din = S.sb("hdin", [128, nch], F32)
        dout = S.sb("hdout", [128, nch], F32)
        Sf = [S.sb("hS%d" % d, [128, 128], F32) for d in range(2)]
        St = [S.sb("hSt%d" % d, [128, 128], F32) for d in range(2)]
        Sb = [S.sb("hSb%d" % d, [128, 128], BF16) for d in range(2)]
        attm = [[S.sb("hattm%d_%d" % (d, i), [LH, LH], BF16) for i in range(2)] for d in range(2)]
        ktT = [[S.sb("hktT%d_%d" % (d, i), [LH, 128], BF16) for i in range(2)] for d in range(2)]
        osb = [[S.sb("hosb%d_%d" % (d, i), [LH, 512], F32) for i in range(2)] for d in range(2)]
        ps_att_ = [S.ps("hpsatt%d" % d, [128, 512], F32) for d in range(2)]
        ps_att = [t[0:LH, 0:LH] for t in ps_att_]
        ps_kT_ = [S.ps("hpskT%d" % d, [128, 1024], BF16) for d in range(2)]
        ps_kT = [t[0:LH, 0:128] for t in ps_kT_]
        ps_o_ = [S.ps("hpso%d" % d, [128, 512], F32) for d in range(2)]
        ps_o = [t[0:LH, 0:128] for t in ps_o_]
        ps_dS_ = [S.ps("hpsdS%d" % d, [128, 512], F32) for d in range(2)]
        ps_dS = [t[:, 0:128] for t in ps_dS_]
        masks = [self.mask_f[0:LH, 0:LH], self.mask_b[0:LH, 0:LH]]
        orders = self._orders(LH)
        outs = [self.HOF, self.HOB]
        v3 = lambda t: t.v().rearrange("p (c l) -> p c l", l=LH)
        for h in range(4):
            S.dma("sp", Q.v(), self.HQT[h, :, :])
            S.dma("sp", V.v(), self.HV[:, h * 128:(h + 1) * 128].rearrange("(c p) v -> p c v", p=LH))
            for d in range(2):
                src = self.SFT if d == 0 else self.SBT
                S.dma("sp", X1.v(), src[h, :, :])
                S.ts("dve", X1.v(), X1.v(), omlb[:, h:h + 1], ALU.mult, lb[:, h:h + 1], ALU.add)
                S.ts("dve", X2.v(), X1.v(), -1.0, ALU.mult, 1.0, ALU.add)
                S.act(X1.v(), X1.v(), AF.Ln)
                S.scan(X3.v(), ones.v(), X1.v())
                if d == 1:
                    S.tt("dve", X3.v(), X1.v(), X3.v(), ALU.subtract)
                R3, L3 = v3(X3), v3(X1)
                i_en, i_ex = (0, LH - 1) if d == 0 else (LH - 1, 0)
                S.copy("dve", rt.v(), R3[:, :, LH // 2])
                S.tt("dve", din.v(), rt.v(), R3[:, :, i_en], ALU.subtract)
                S.tt("dve", din.v(), din.v(), L3[:, :, i_en], ALU.add)
                S.act(din.v(), din.v(), AF.Exp)
                S.tt("dve", dout.v(), R3[:, :, i_ex], rt.v(), ALU.subtract)
                S.act(dout.v(), dout.v(), AF.Exp)
                if d == 0:
                    S.tt("dve", gg[d][:, 0:nch - 1], dout[:, 0:nch - 1], din[:, 1:nch], ALU.mult)
                else:
                    S.tt("dve", gg[d][:, 1:nch], dout[:, 1:nch], din[:, 0:nch - 1], ALU.mult)
                    S.tt("dve", gg[d][:, 0:1], dout[:, 0:1], din[:, nch - 1:nch], ALU.mult)
                S.tt("dve", v3(X1), R3, rt.v().rearrange("p (c o) -> p c o", o=1).to_broadcast([128, nch, LH]), ALU.subtract)
                S.act(X3.v(), X1.v(), AF.Exp)
                S.tt("dve", qt[d].v(), Q.v(), X3.v(), ALU.mult)
                S.act(X3.v(), X1.v(), AF.Exp, scale=-1.0)
                S.tt("dve", kt[d].v(), X2.v(), X3.v(), ALU.mult)
                S.memset("dve", Sf[d].v(), 0.0)
                S.memset("pool", Sb[d].v(), 0.0)
            def prep(i, d):
                c = orders[d][i]
                c0 = c * LH
                S.mm(ps_att[d], kt[d][:, c0:c0 + LH], qt[d][:, c0:c0 + LH])
                S.tr(ps_kT[d], kt[d][:, c0:c0 + LH], self.ident.v())
                S.tt("dve", attm[d][i % 2].v(), ps_att[d], masks[d], ALU.mult)
                S.copy("act", ktT[d][i % 2].v(), ps_kT[d])

            def use(i, d):
                c = orders[d][i]
                c0 = c * LH
                j4 = i % 4
                slot = j4 if d == 0 else 3 - j4
                po = ps_o_[d][0:LH, slot * 128:(slot + 1) * 128]
                S.mm(po, attm[d][i % 2].v(), V[:, c, :], start=(j4 == 0), stop=False, skip=True)
                S.mm(po, qt[d][:, c0:c0 + LH], Sb[d].v(), start=False, stop=True, skip=True)
                if i < nch - 1:
                    S.mm(ps_dS[d], ktT[d][i % 2].v(), V[:, c, :])
                    S.tt("dve", St[d].v(), Sf[d].v(), ps_dS[d], ALU.add)
                    S.act(Sb[d].v(), St[d].v(), AF.Identity, scale=gg[d][:, c:c + 1])
                    S.ts("dve", Sf[d].v(), St[d].v(), gg[d][:, c:c + 1], ALU.mult)
                if j4 == 3:
                    o = osb[d][(i // 4) % 2]
                    S.copy("act", o.v(), ps_o_[d][0:LH, :])
                    base = min(orders[d][i - 3 + jj] for jj in range(4)) * LH
                    dst = outs[d][base:base + 4 * LH, h * 128:(h + 1) * 128].rearrange("(j p) v -> p j v", p=LH)
                    S.dma("sp", dst, o.v().rearrange("p (j v) -> p j v", v=128))
            for d in range(2):
                prep(0, d)
            for i in range(nch):
                for d in range(2):
                    if i + 1 < nch:
                        prep(i + 1, d)
                    use(i, d)
        self.dbg["hof%d" % l] = (self.HOF, [nt, 512], F32)
        self.dbg["hob%d" % l] = (self.HOB, [nt, 512], F32)
        S.pop()

    def phase_da(self, l):
        S, I, cfg = self.S, self.I, self.cfg
        nt, ntile, seq = cfg.nt, cfg.ntile, cfg.seq
        lam_init = 0.8 - 0.6 * math.exp(-0.3 * l)
        S.push()
        lamraw = S.sb("lamraw", [128, 256], F32)
        S.dma("sp", lamraw.v(), I["da_lambda"][l:l + 1, :].partition_broadcast(128))
        prod = S.sb("lamprod", [128, 2, 64], F32)
        lr = lamraw.v().rearrange("p (a b k) -> p a b k", a=2, b=2)
        S.tt("dve", prod.v(), lr[:, :, 0, :], lr[:, :, 1, :], ALU.mult)
        lsum = S.sb("lsum", [128, 2], F32)
        S.reduce(lsum.v(), prod.v())
        S.act(lsum.v(), lsum.v(), AF.Exp)
        neglam = S.sb("neglam", [128, 1], F32)
        S.tt("dve", neglam.v(), lsum[:, 1:2], lsum[:, 0:1], ALU.subtract)
        S.ts("dve", neglam.v(), neglam.v(), -lam_init, ALU.add)
        self.dbg_sb("neglam%d" % l, neglam.v(), [128, 1], F32)
        self.dbg_sb("lsum%d" % l, lsum.v(), [128, 2], F32)
        sw = S.sb("sublnw", [128, 128], F32)
        S.dma("sp", sw.v(), I["da_subln_w"][l:l + 1, :].partition_broadcast(128))
        S.ts("dve", sw.v(), sw.v(), 1.0 - lam_init, ALU.mult)
        QTh = S.sb("dQT", [128, nt], BF16)
        KTh = S.sb("dKT", [128, nt], BF16)
        Vh = S.sb("dV", [128, ntile, 128], BF16)
        NS = 3
        NP = 4
        ps_s = [S.ps("dps%d" % i, [128, 2, 512], F32) for i in range(NS)]
        Pb = [S.sb("dP%d" % i, [128, 2, 512], BF16) for i in range(NP)]
        ps_oT = [S.ps("dpoT%d" % c, [128, 512], F32) for c in range(2)]
        ps_tr = ps_s[0].v().rearrange("p a (b v) -> p a b v", v=128)
        ps_l = ps_s[1][:, 0, :]
        Pacc = [S.sb("dPacc%d" % c, [128, 512], F32) for c in range(2)]
        Pacc2 = [S.sb("dPacc2_%d" % c, [128, 512], F32) for c in range(2)]
        oT = [S.sb("doT%d" % c, [128, 512], F32) for c in range(2)]
        rc8 = S.sb("drc8", [128, 8], F32)
        t04 = S.sb("dt04", [128, 4, 128], F32)
        a4 = S.sb("da4", [128, 4, 128], F32)
        ss4d = S.sb("dss4", [128, 4], F32)
        stg = [S.sb("dstg%d" % i, [128, 4, 128], BF16) for i in range(2)]
        n = 0
        nst = 0
        qblocks = [(0, CTX, [0, 1])] + [(CTX + i * 512, 512, list(range(ntile))) for i in range(seq // 512)]
        for h in range(4):
            S.dma("sp", QTh.v(), self.QT[h, :, :])
            S.dma("sp", KTh.v(), self.KT[h, :, :])
            S.dma("sp", Vh.v(), self.DV[:, h * 128:(h + 1) * 128].rearrange("(kb p) v -> p kb v", p=128))
            for (q0, nq, kbs) in qblocks:
                nqs = nq // 128
                npair = len(kbs) // 2
                its = [(c, pi) for c in range(2) for pi in range(npair)]

                def score(it, m):
                    c, pi = it
                    for e in range(2):
                        kb = kbs[2 * pi + e]
                        S.mm(ps_s[m % NS][:, e, 0:nq], KTh[c * 64:(c + 1) * 64, kb * 128:(kb + 1) * 128],
                             QTh[c * 64:(c + 1) * 64, q0:q0 + nq])
                    S.act(Pb[m % NP][:, :, 0:nq], ps_s[m % NS][:, :, 0:nq], AF.Exp, scale=0.125)

                def pv(it, m):
                    c, pi = it
                    P = Pb[m % NP]
                    for e in range(2):
                        kb = kbs[2 * pi + e]
                        S.mm(ps_oT[c][:, 0:nq], Vh[:, kb, :], P[:, e, 0:nq],
                             start=(pi == 0 and e == 0), stop=(pi == npair - 1 and e == 1))
                    if pi == 0:
                        S.copy("dve", Pacc[c][:, 0:nq], P[:, 0, 0:nq])
                        S.copy("pool", Pacc2[c][:, 0:nq], P[:, 1, 0:nq])
                    else:
                        S.tt("dve", Pacc[c][:, 0:nq], Pacc[c][:, 0:nq], P[:, 0, 0:nq], ALU.add)
                        S.tt("pool", Pacc2[c][:, 0:nq], Pacc2[c][:, 0:nq], P[:, 1, 0:nq], ALU.add)
                    if pi == npair - 1:
                        S.tt("dve", Pacc[c][:, 0:nq], Pacc[c][:, 0:nq], Pacc2[c][:, 0:nq], ALU.add)
                LOOK = 2
                for j in range(min(LOOK, len(its))):
                    score(its[j], n + j)
                for j in range(len(its)):
                    if j + LOOK < len(its):
                        score(its[j + LOOK], n + j + LOOK)
                    pv(its[j], n + j)
                n += len(its)
                for c in range(2):
                    S.copy("act", oT[c][:, 0:nq], ps_oT[c][:, 0:nq])
                for c in range(2):
                    for qs in range(nqs):
                        S.tr(ps_tr[:, c, qs, :], oT[c][:, qs * 128:(qs + 1) * 128], self.identf.v())
                        S.mm(ps_l[:, c * 4 + qs:c * 4 + qs + 1], Pacc[c][:, qs * 128:(qs + 1) * 128], self.onescol.v(), skip=True)
                sg = stg[nst % 2]
                nst += 1
                bq = lambda v: v.rearrange("p (a o) -> p a o", o=1).to_broadcast([128, nqs, 128])
                S.recip(rc8.v(), ps_l[:, 0:8])
                S.ts("dve", rc8[:, 4:8], rc8[:, 4:8], neglam[:, 0:1], ALU.mult)
                S.tt("dve", t04[:, 0:nqs, :], ps_tr[:, 0, 0:nqs, :], bq(rc8[:, 0:nqs]), ALU.mult)
                S.tt("dve", a4[:, 0:nqs, :], ps_tr[:, 1, 0:nqs, :], bq(rc8[:, 4:4 + nqs]), ALU.mult)
                S.tt("dve", a4[:, 0:nqs, :], a4[:, 0:nqs, :], t04[:, 0:nqs, :], ALU.add)
                S.tt("dve", t04[:, 0:nqs, :], a4[:, 0:nqs, :], a4[:, 0:nqs, :], ALU.mult)
                S.reduce(ss4d[:, 0:nqs], t04[:, 0:nqs, :])
                S.act(ss4d[:, 0:nqs], ss4d[:, 0:nqs], AF.Ln, scale=1.0 / 128, bias=self.epsc.v())
                S.act(ss4d[:, 0:nqs], ss4d[:, 0:nqs], AF.Exp, scale=-0.5)
                S.tt("dve", a4[:, 0:nqs, :], a4[:, 0:nqs, :], bq(ss4d[:, 0:nqs]), ALU.mult)
                S.tt("dve", sg[:, 0:nqs, :], a4[:, 0:nqs, :],
                     sw.v().rearrange("p (o v) -> p o v", o=1).to_broadcast([128, nqs, 128]), ALU.mult)
                S.dma("sp", self.MDA[q0:q0 + nq, h * 128:(h + 1) * 128].rearrange("(s p) v -> p s v", p=128), sg[:, 0:nqs, :])
        self.dbg["mda%d" % l] = (self.MDA, [nt, 512], BF16)
        S.pop()

    def phase_ssd(self, l):
        S, I, cfg = self.S, self.I, self.cfg
        nt, nch, ntile = cfg.nt, cfg.nch, cfg.ntile
        S.push()
        one64 = S.sb("one64", [64, 128], F32)
        S.memset("pool", one64.v(), 1.0)
        U = S.sb("sU", [64, 64], F32)
        SU = S.sb("sSU", [64, 64], F32)
        S.aselect(U.v(), one64[:, 0:64], [[1, 64]], ALU.is_ge, 0.0, 0, -1)
        S.aselect(SU.v(), one64[:, 0:64], [[1, 64]], ALU.is_gt, 0.0, 0, -1)
        nU = S.sb("snU", [64, 64], F32)
        nSU = S.sb("snSU", [64, 64], F32)
        GT = S.sb("sGT", [64, 64], F32)
        GE = S.sb("sGE", [64, 64], F32)
        S.ts("dve", nU.v(), U.v(), -1.0, ALU.mult)
        S.ts("dve", nSU.v(), SU.v(), -1.0, ALU.mult)
        S.ts("dve", GT.v(), U.v(), -1.0, ALU.mult, 1.0, ALU.add)
        S.ts("dve", GE.v(), SU.v(), -1.0, ALU.mult, 1.0, ALU.add)
        dA = S.sb("sdA", [64, nch, 32], F32)
        T2 = S.sb("sT2", [64, nch, 32], F32)
        EOS = S.sb("sEOS", [64, nch, 32], F32)
        XW = S.sb("sXW", [64, nch, 32], F32)
        EOS_tmp = dA_tmp = None
        ETOT = S.sb("sETOT", [128, nch, 32], F32)
        BT = S.sb("sBT", [128, 2, nt], BF16)
        CT = S.sb("sCT", [128, 2, nt], BF16)
        cw = S.sb("scw", [128, 12, 3], F32)
        cbias = S.sb("scb", [128, 12], F32)
        S.push()
        pst = [S.ps("spst%d" % i, [128, 512], F32) for i in range(2)]
        npst = 0
        EOS_tmp = S.sb("sLDT", [64, nch, 32], F32)
        DTs = S.sb("sDT", [64, nch, 32], F32)
        S.dma("sp", DTs.v(), self.DT.v().rearrange("(c p) j -> p c j", p=64))
        dtb = S.sb("sdtb", [64, 32], F32)
        Aa = S.sb("sA", [64, 32], F32)
        S.dma("sp", dtb.v(), I["dt_bias"][l:l + 1, :].partition_broadcast(64))
        S.dma("sp", Aa.v(), I["a_log"][l:l + 1, :].partition_broadcast(64))
        S.act(Aa.v(), Aa.v(), AF.Exp)
        S.ts("dve", Aa.v(), Aa.v(), -1.0, ALU.mult)
        bc32 = lambda t: t.v().rearrange("p (o j) -> p o j", o=1).to_broadcast([64, nch, 32])
        S.tt("dve", DTs.v(), DTs.v(), bc32(dtb), ALU.add)
        S.act(DTs.v(), DTs.v(), AF.Exp)
        S.act(DTs.v(), DTs.v(), AF.Ln, bias=1.0)
        S.tt("dve", dA.v(), DTs.v(), bc32(Aa), ALU.mult)

        def table(dst, mats, func, npart=64):
            nonlocal npst
            for d in range(2):
                for cb in range(0, nch, 32):
                    ncb = min(32, nch - cb)
                    ps = pst[npst % 2]
                    npst += 1
                    pv = ps[0:npart, 0:ncb * 16].rearrange("p (c j) -> p c j", j=16)
                    S.mm(pv, mats[d], dA[:, cb:cb + ncb, d * 16:(d + 1) * 16])
                    if func is None:
                        S.copy("dve", dst[:, cb:cb + ncb, d * 16:(d + 1) * 16], pv)
                    else:
                        S.act(dst[:, cb:cb + ncb, d * 16:(d + 1) * 16], pv, func)
        table(T2, [nU.v(), SU.v()], None)
        table(EOS, [U.v(), GE.v()], AF.Exp)
        table(XW, [GT.v(), SU.v()], AF.Exp)
        table(ETOT, [one64.v(), one64.v()], AF.Exp, npart=128)
        S.tt("dve", XW.v(), XW.v(), DTs.v(), ALU.mult)
        S.act(EOS_tmp.v(), DTs.v(), AF.Ln)
        S.tt("dve", T2.v(), T2.v(), EOS_tmp.v(), ALU.add)
        S.pop()
        for j in range(3):
            S.dma("sp", cw[:, :, j], I["conv_w"][l, j, :].rearrange("(c p) -> p c", p=128), slow=True)
        S.dma("sp", cbias.v(), I["conv_b"][l, :].rearrange("(c p) -> p c", p=128), slow=True)
        S.push()
        XT = [S.sb("sXT%d" % i, [128, nt], BF16) for i in range(2)]
        xin = S.sb("sxin", [128, nt], F32)
        yv = S.sb("sy", [128, nt], F32)
        ptx = [S.ps("sptx%d" % i, [128, 1024], BF16) for i in range(2)]
        xst = [S.sb("sxst%d" % i, [128, 8, 128], BF16) for i in range(2)]
        nx = 0
        segs = [(0, CTX), (CTX, nt)]
        for ci in range(12):
            S.dma("sp", xin.v(), self.XBCT[ci, :, :])
            S.act(yv.v(), xin.v(), AF.Identity, scale=cw[:, ci, 1:2], bias=cbias[:, ci:ci + 1])
            for (s0, e0) in segs:
                S.stt(yv[:, s0 + 1:e0], xin[:, s0:e0 - 1], cw[:, ci, 0:1], yv[:, s0 + 1:e0], ALU.mult, ALU.add)
                S.stt(yv[:, s0:e0 - 1], xin[:, s0 + 1:e0], cw[:, ci, 2:3], yv[:, s0:e0 - 1], ALU.mult, ALU.add)
            if ci < 8:
                xt_ = XT[ci % 2]
                S.act(xt_.v(), yv.v(), AF.Silu)
                for i0 in range(0, ntile, 8):
                    nti = min(8, ntile - i0)
                    pt = ptx[nx % 2]
                    xs_ = xst[nx % 2]
                    nx += 1
                    for a in range(nti):
                        S.tr(pt[:, a * 128:(a + 1) * 128], xt_[:, (i0 + a) * 128:(i0 + a + 1) * 128], self.ident.v())
                    S.copy("act", xs_[:, 0:nti, :], pt[:, 0:nti * 128].rearrange("p (a c) -> p a c", c=128))
                    S.dma("sp", self.XTM[i0 * 128:(i0 + nti) * 128, ci * 128:(ci + 1) * 128].rearrange("(a p) c -> p a c", p=128),
                          xs_[:, 0:nti, :])
            elif ci < 10:
                S.act(BT[:, ci - 8, :], yv.v(), AF.Silu)
            else:
                S.act(CT[:, ci - 10, :], yv.v(), AF.Silu)
        S.pop()
        Gms = [S.sb("sGms%d" % d, [64, nch, 2, 64], BF16) for d in range(2)]
        Btm = S.sb("sBtm", [64, nch, 2, 128], BF16)
        S.push()
        ptb_ = S.ps("sptb", [128, 1024], BF16)
        ptb = ptb_[0:64, 0:512].rearrange("p (a n) -> p a n", n=128)
        for c in range(0, nch, 2):
            for cc in range(2):
                for g in range(2):
                    S.tr(ptb[:, cc * 2 + g, :], BT[:, g, (c + cc) * 64:(c + cc + 1) * 64], self.ident.v())
            S.copy("act", Btm[:, c:c + 2, :, :].rearrange("p a g n -> p (a g) n"), ptb)
        psg = [S.ps("spg%d" % i, [128, 512], F32) for i in range(2)]
        for c in range(0, nch, 4):
            ps = psg[(c // 4) % 2]
            pv = ps[0:64, :].rearrange("p (a g t) -> p a g t", a=4, g=2)
            for cc in range(4):
                for g in range(2):
                    c0 = (c + cc) * 64
                    S.mm(pv[:, cc, g, :], BT[:, g, c0:c0 + 64], CT[:, g, c0:c0 + 64])
            S.tt("dve", Gms[0][:, c:c + 4, :, :].rearrange("p a g t -> p (a g) t"), ps[0:64, :].rearrange("p (a t) -> p a t", t=64),
                 self.mask_f.v().rearrange("p (o t) -> p o t", o=1).to_broadcast([64, 8, 64]), ALU.mult)
            S.tt("dve", Gms[1][:, c:c + 4, :, :].rearrange("p a g t -> p (a g) t"), ps[0:64, :].rearrange("p (a t) -> p a t", t=64),
                 self.mask_b.v().rearrange("p (o t) -> p o t", o=1).to_broadcast([64, 8, 64]), ALU.mult)
        S.pop()
        NK = 4
        ST = [S.sb("sST%d" % k, [128, 512], F32) for k in range(NK)]
        STb = [S.sb("sSTb%d" % k, [128, 512], BF16) for k in range(NK)]
        Xc = [[S.sb("sXc%d_%d" % (k, i), [64, 512], BF16) for i in range(2)] for k in range(NK)]
        Xw = [[S.sb("sXw%d_%d" % (k, i), [64, 512], BF16) for i in range(2)] for k in range(NK)]
        RH = [S.sb("sRH%d" % k, [64, 8, 64], F32) for k in range(2)] * 2
        EX = [S.sb("sEX%d" % k, [64, 8, 64], F32) for k in range(2)] * 2
        MT = [[S.sb("sMT%d_%d" % (k, i), [64, 8, 64], BF16) for i in range(2)] for k in range(NK)]
        ytmp = [S.sb("sytmp%d" % k, [64, 512], F32) for k in range(2)] * 2
        ysb = [[S.sb("sys%d_%d" % (k, i), [64, 512], F32) for i in range(1)] * 2 for k in range(NK)]
        ps_T = [S.ps("spT%d" % d, [128, 512], F32)[0:64, :] for d in range(2)]
        ps_Y = [S.ps("spY%d" % d, [128, 512], F32)[0:64, :] for d in range(2)]
        ps_Y2 = [S.ps("spY2%d" % d, [128, 512], F32)[0:64, :] for d in range(2)]
        ps_dS = [S.ps("spdS%d" % d, [128, 512], F32).v() for d in range(2)]
        orders = self._orders()
        outs = [self.YF, self.YB]
        Wd = [U, nSU]
        v8 = lambda v: v.rearrange("p (h q) -> p h q", q=64)
        hb = lambda v, np_: v.rearrange("p (h o) -> p h o", o=1).to_broadcast([np_, 8, 64])

        def prep(i, g, d):
            k = g * 2 + d
            c = orders[d][i]
            c0 = c * 64
            hs = d * 16 + g * 8
            xc = Xc[k][i % 2]
            S.dma("sp", xc.v(), self.XTM[c0:c0 + 64, g * 512:(g + 1) * 512])
            S.tt("pool", RH[k].v(), Wd[d].v().rearrange("p (o t) -> p o t", o=1).to_broadcast([64, 8, 64]),
                 dA[:, c, hs:hs + 8].rearrange("p (h o) -> p h o", o=1).to_broadcast([64, 8, 64]), ALU.mult)
            S.mm(ps_T[d], one64[:, 0:64], RH[k].v().rearrange("p h t -> p (h t)"))
            S.tt("dve", EX[k].v(), v8(ps_T[d]), hb(T2[:, c, hs:hs + 8], 64), ALU.add)
            S.act(EX[k].v(), EX[k].v(), AF.Exp)
            S.stt(MT[k][i % 2].v(), EX[k].v(), 1e30,
                  Gms[d][:, c, g, :].rearrange("p (o t) -> p o t", o=1).to_broadcast([64, 8, 64]), ALU.min, ALU.mult)
            S.tt("pool", v8(Xw[k][i % 2].v()), v8(xc.v()), hb(XW[:, c, hs:hs + 8], 64), ALU.mult)

        def use(i, g, d):
            k = g * 2 + d
            c = orders[d][i]
            c0 = c * 64
            hs = d * 16 + g * 8
            xc = Xc[k][i % 2]
            for h in range(8):
                S.mm(ps_Y[d][:, h * 64:(h + 1) * 64], MT[k][i % 2][:, h, :], xc[:, h * 64:(h + 1) * 64], skip=True)
            S.mm(ps_Y2[d], CT[:, g, c0:c0 + 64], STb[k].v())
            if i < nch - 1:
                S.mm(ps_dS[d], Btm[:, c, g, :], Xw[k][i % 2].v())
                S.tt("pool", v8(ST[k].v()), v8(ST[k].v()), hb(ETOT[:, c, hs:hs + 8], 128), ALU.mult)
                S.tt("dve", ST[k].v(), ST[k].v(), ps_dS[d], ALU.add)
                S.copy("act", STb[k].v(), ST[k].v())
            y = ysb[k][i % 2]
            S.tt("dve", v8(ytmp[k].v()), v8(ps_Y2[d]), hb(EOS[:, c, hs:hs + 8], 64), ALU.mult)
            S.tt("dve", y.v(), ytmp[k].v(), ps_Y[d], ALU.add)
            S.dma("sp!", outs[d][c0:c0 + 64, g * 512:(g + 1) * 512], y.v())
        for k in range(NK):
            S.memset("dve", ST[k].v(), 0.0)
            S.memset("pool", STb[k].v(), 0.0)
        for g in range(2):
            for d in range(2):
                prep(0, g, d)
        for i in range(nch):
            for g in range(2):
                for d in range(2):
                    if i + 1 < nch:
                        prep(i + 1, g, d)
                    use(i, g, d)
        self.dbg["yf%d" % l] = (self.YF, [nt, 1024], F32)
        self.dbg["yb%d" % l] = (self.YB, [nt, 1024], F32)
        self.dbg["xtm%d" % l] = (self.XTM, [nt, 1024], BF16)
        S.pop()

    def phase_mixout(self, l):
        S, I, cfg = self.S, self.I, self.cfg
        last = (l == DEPTH - 1)
        S.push()
        hgw = S.sb("ghgw", [128, 128], F32)
        S.dma("sp", hgw.v(), I["hg_norm_w"][l:l + 1, :].partition_broadcast(128))
        ssmw = S.sb("gssmw", [128, 1024], F32)
        S.dma("sp", ssmw.v(), I["ssm_norm_w"][l:l + 1, :].partition_broadcast(128))
        dsk = S.sb("gdsk", [128, 16], F32)
        S.dma("sp", dsk.v(), I["ssm_d"][l:l + 1, :].partition_broadcast(128))
        NB = 2
        of = [S.sb("gof%d" % i, [128, 512], F32) for i in range(NB)]
        ob = [S.sb("gob%d" % i, [128, 512], F32) for i in range(NB)]
        gt = [S.sb("ggt%d" % i, [128, 512], F32) for i in range(NB)]
        yf = [S.sb("gyf%d" % i, [128, 1024], F32) for i in range(NB)]
        yb = [S.sb("gyb%d" % i, [128, 1024], F32) for i in range(NB)]
        sz = [S.sb("gsz%d" % i, [128, 1024], F32) for i in range(NB)]
        xt = [S.sb("gxt%d" % i, [128, 1024], BF16) for i in range(NB)]
        mt = [S.sb("gmt%d" % i, [128, D], BF16) for i in range(NB)]
        junk = S.sb("gjunk", [128, 512], F32)
        ss4 = S.sb("gss4", [128, 4], F32)
        ss2 = S.sb("gss2", [128, 2], F32)
        ptr = S.ps("gptr", [128, D], BF16)
        mTs = [S.sb("gmT%d" % i, [128, 16, 512], BF16) for i in range(2)]
        n = 0
        for g, (t0, gsz) in enumerate(cfg.groups):
            if last and g == 0:
                continue
            mT = mTs[g % 2]
            for j in range(gsz // 128):
                b = n % NB
                n += 1
                n0 = t0 + j * 128
                rows = slice(n0, n0 + 128)
                S.dma("sp", of[b].v(), self.HOF[rows, :])
                S.dma("sp", ob[b].v(), self.HOB[rows, :])
                S.dma("sp", gt[b].v(), self.HG[rows, :])
                S.dma("sp", yf[b].v(), self.YF[rows, :])
                S.dma("sp", yb[b].v(), self.YB[rows, :])
                S.dma("sp", sz[b].v(), self.SZ[rows, :])
                S.dma("sp", xt[b].v(), self.XTM[rows, :])
                S.dma("sp", mt[b][:, 512:1024], self.MDA[rows, :])
                S.tt("dve", of[b].v(), of[b].v(), ob[b].v(), ALU.add)
                for h in range(4):
                    S.act(junk[:, 0:128], of[b][:, h * 128:(h + 1) * 128], AF.Square, accum_out=ss4[:, h:h + 1])
                S.act(ss4.v(), ss4.v(), AF.Ln, scale=1.0 / 128, bias=self.epsc.v())
                S.act(ss4.v(), ss4.v(), AF.Exp, scale=-0.5)
                o3 = of[b].v().rearrange("p (h v) -> p h v", v=128)
                S.tt("dve", o3, o3, ss4.v().rearrange("p (h o) -> p h o", o=1).to_broadcast([128, 4, 128]), ALU.mult)
                S.tt("dve", o3, o3, hgw.v().rearrange("p (o v) -> p o v", o=1).to_broadcast([128, 4, 128]), ALU.mult)
                S.tt("dve", mt[b][:, 0:512], of[b].v(), gt[b].v(), ALU.mult)
                S.tt("dve", yf[b].v(), yf[b].v(), yb[b].v(), ALU.add)
                x3 = xt[b].v().rearrange("p (h q) -> p h q", q=64)
                y3 = yb[b].v().rearrange("p (h q) -> p h q", q=64)
                S.tt("dve", y3, x3, dsk.v().rearrange("p (h o) -> p h o", o=1).to_broadcast([128, 16, 64]), ALU.mult)
                S.tt("dve", yf[b].v(), yf[b].v(), yb[b].v(), ALU.add)
                S.tt("dve", yf[b].v(), yf[b].v(), sz[b].v(), ALU.mult)
                for gg in range(2):
                    S.act(junk.v(), yf[b][:, gg * 512:(gg + 1) * 512], AF.Square, accum_out=ss2[:, gg:gg + 1])
                S.act(ss2.v(), ss2.v(), AF.Ln, scale=1.0 / 512, bias=self.epsc.v())
                S.act(ss2.v(), ss2.v(), AF.Exp, scale=-0.5)
                yg = yf[b].v().rearrange("p (g v) -> p g v", v=512)
                S.tt("dve", yg, yg, ss2.v().rearrange("p (g o) -> p g o", o=1).to_broadcast([128, 2, 512]), ALU.mult)
                S.tt("dve", mt[b][:, 1024:2048], yf[b].v(), ssmw.v(), ALU.mult)
                for k in range(16):
                    S.tr(ptr[:, k * 128:(k + 1) * 128], mt[b][:, k * 128:(k + 1) * 128], self.ident.v())
                S.copy("act", mT[:, :, j * 128:(j + 1) * 128], ptr.v().rearrange("p (k t) -> p k t", t=128))
            S.dma("sp", self.MT[g][:, :, 0:gsz], mT[:, :, 0:gsz])
            self.dbg["mt%d_%d" % (l, g)] = (self.MT[g], [128, 16, 512], BF16)
        S.pop()

    def phase_outproj(self, l):
        S, I, cfg = self.S, self.I, self.cfg
        last = (l == DEPTH - 1)
        for kind in (1, 0):
            if kind == 1 and last:
                continue
            S.push()
            gain, shift = self._gain_shift(l, 1, "norm2_w", kind)
            gate = self._gate(l, 0, kind)
            wo = [S.sb("owo%d" % i, [128, 16, 512], BF16) for i in range(2)]
            mT = S.sb("omT", [128, 16, 512], BF16)
            h1 = S.sb("oh1", [128, 4, D], F32)
            tmp = S.sb("otmp", [128, 512], F32)
            tiles = {"junk": S.sb("ojunk", [128, D], F32), "ss": S.sb("oss", [128, 1], F32),
                     "rstd": S.sb("orstd", [128, 1], F32), "ub": S.sb("oub", [128, D], BF16),
                     "ptr": S.ps("optr", [128, D], BF16)}
            uTs = [S.sb("ouT%d" % i, [128, 16, 512], BF16) for i in range(2)]
            pso = [S.ps("opso%d" % i, [128, 512], F32) for i in range(4)]
            npso = 0
            nwo = 0
            for g, (t0, gsz) in enumerate(cfg.groups):
                if (g == 0) != (kind == 1):
                    continue
                nsub = gsz // 128
                S.dma("sp", mT[:, :, 0:gsz], self.MT[g][:, :, 0:gsz])
                if l == 0:
                    hsrc = I["xin"][t0:t0 + gsz, :]
                else:
                    hsrc = self.H[g][0:gsz, :]
                S.dma("sp", h1[:, 0:nsub, :], hsrc.rearrange("(s p) c -> p s c", p=128))
                for cb in range(4):
                    w = wo[nwo % 2]
                    nwo += 1
                    S.dma("sp", w.v(), self.WO[l][:, cb * 512:(cb + 1) * 512].rearrange("(k p) n -> p k n", p=128))
                    for s in range(nsub):
                        ps = pso[npso % 4]
                        npso += 1
                        for k in range(16):
                            S.mm(ps.v(), mT[:, k, s * 128:(s + 1) * 128], w[:, k, :], start=(k == 0), stop=(k == 15))
                        S.tt("dve", tmp.v(), ps.v(), gate[:, cb * 512:(cb + 1) * 512], ALU.mult)
                        S.tt("dve", h1[:, s, cb * 512:(cb + 1) * 512], h1[:, s, cb * 512:(cb + 1) * 512], tmp.v(), ALU.add)
                uT = uTs[g % 2]
                for s in range(nsub):
                    S.dma("sp", self.H1[g][s * 128:(s + 1) * 128, :], h1[:, s, :])
                    self._rms_mod_transpose(h1[:, s, :], gain, shift, uT, s, tiles)
                S.dma("sp", self.U2T[g][:, :, 0:gsz], uT[:, :, 0:gsz])
            S.pop()

    def phase_ffn(self, l):
        S, I, cfg = self.S, self.I, self.cfg
        last = (l == DEPTH - 1)
        FB = 512
        nfb = DFF // FB
        for kind in (1, 0):
            if kind == 1 and last:
                continue
            S.push()
            gate = self._gate(l, 1, kind)
            if last:
                fnw = S.sb("ffnw", [128, D], F32)
                self._bc_row(fnw, I["final_norm_w"][0:1, :])
            uTs = [S.sb("fuT%d" % i, [128, 16, 512], BF16) for i in range(1)]
            wg = [S.sb("fwg%d" % i, [128, 16, FB], BF16) for i in range(2)]
            wu = [S.sb("fwu%d" % i, [128, 16, FB], BF16) for i in range(2)]
            wd = [S.sb("fwd%d" % i, [128, FB // 128, D], BF16) for i in range(2)]
            actT = [S.sb("fact%d" % i, [128, FB // 128, 512], BF16) for i in range(2)]
            sg = [S.sb("fsg%d" % i, [128, 512], F32) for i in range(2)]
            acc = S.sb("facc", [128, 4, D], F32)
            h1t = [S.sb("fh1%d" % i, [128, D], F32) for i in range(1)]
            tmp = S.sb("ftmp", [128, D], F32)
            ss = S.sb("fss", [128, 1], F32)
            rstd = S.sb("frstd", [128, 1], F32)
            psg = [S.ps("fpsg%d" % i, [128, 512], F32) for i in range(2)]
            psu = [S.ps("fpsu%d" % i, [128, 512], F32) for i in range(2)]
            psd = [S.ps("fpsd%d" % i, [128, 512], F32) for i in range(4)]
            npg = 0
            npd = 0
            nwb = 0
            nh = 0
            for g, (t0, gsz) in enumerate(cfg.groups):
                if (g == 0) != (kind == 1):
                    continue
                nsub = gsz // 128
                uT = uTs[0]
                S.dma("sp", uT[:, :, 0:gsz], self.U2T[g][:, :, 0:gsz])
                def GU(fb):
                    nonlocal npg
                    b = fb % 2
                    f0 = fb * FB
                    S.dma("sp", wg[b].v(), self.WG[l][fb, :, :, :])
                    S.dma("sp", wu[b].v(), self.WU[l][fb, :, :, :])
                    S.dma("sp", wd[b].v(), self.WD[l][f0:f0 + FB, :].rearrange("(c p) n -> p c n", p=128))
                    at = actT[b]
                    for c in range(FB // 128):
                        pg = psg[npg % 2]
                        pu = psu[npg % 2]
                        sgt = sg[npg % 2]
                        npg += 1
                        for k in range(16):
                            S.mm(pg[:, 0:gsz], wg[b][:, k, c * 128:(c + 1) * 128], uT[:, k, 0:gsz], start=(k == 0), stop=(k == 15))
                        for k in range(16):
                            S.mm(pu[:, 0:gsz], wu[b][:, k, c * 128:(c + 1) * 128], uT[:, k, 0:gsz], start=(k == 0), stop=(k == 15))
                        S.act(sgt[:, 0:gsz], pg[:, 0:gsz], AF.Silu)
                        S.tt("dve", at[:, c, 0:gsz], sgt[:, 0:gsz], pu[:, 0:gsz], ALU.mult)

                def DN(fb):
                    nonlocal npd
                    b = fb % 2
                    at = actT[b]
                    for s in range(nsub):
                        for cb in range(4):
                            pd = psd[npd % 4]
                            npd += 1
                            for c in range(FB // 128):
                                S.mm(pd.v(), at[:, c, s * 128:(s + 1) * 128], wd[b][:, c, cb * 512:(cb + 1) * 512],
                                     start=(c == 0), stop=(c == FB // 128 - 1))
                            dst = acc[:, s, cb * 512:(cb + 1) * 512]
                            if fb == 0:
                                S.copy("dve", dst, pd.v())
                            else:
                                S.tt("dve", dst, dst, pd.v(), ALU.add)
                GU(0)
                for fb in range(nfb):
                    if fb + 1 < nfb:
                        GU(fb + 1)
                    DN(fb)
                for s in range(nsub):
                    h1 = h1t[0]
                    nh += 1
                    S.dma("sp", h1.v(), self.H1[g][s * 128:(s + 1) * 128, :])
                    S.tt("dve", tmp.v(), acc[:, s, :], gate.v(), ALU.mult)
                    S.tt("dve", h1.v(), h1.v(), tmp.v(), ALU.add)
                    if not last:
                        S.dma("sp", self.H[g][s * 128:(s + 1) * 128, :], h1.v())
                        self.dbg["h%d_%d" % (l, g)] = (self.H[g], [512, D], F32)
                    else:
                        S.act(tmp.v(), h1.v(), AF.Square, accum_out=ss.v())
                        S.act(rstd.v(), ss.v(), AF.Ln, scale=1.0 / D, bias=self.epsc.v())
                        S.act(rstd.v(), rstd.v(), AF.Exp, scale=-0.5)
                        S.stt(h1.v(), h1.v(), rstd[:, 0:1], fnw.v(), ALU.mult, ALU.mult)
                        r0 = t0 - CTX + s * 128
                        S.dma("sp", self.Y[r0:r0 + 128, :], h1.v())
            S.pop()

def _rope_tables(seq):
    half = 32
    inv = (1.0 / (10000.0 ** (np.arange(0, half, 2, dtype=np.float32) / np.float32(half)))).astype(np.float32)
    rows = seq // GRID_W
    r = np.repeat(np.arange(rows, dtype=np.float32), GRID_W)
    col = np.tile(np.arange(GRID_W, dtype=np.float32), rows)
    ar = (r[:, None] * inv[None, :]).astype(np.float32)
    ac = (col[:, None] * inv[None, :]).astype(np.float32)
    cr, sr, cc_, sc_ = np.cos(ar), np.sin(ar), np.cos(ac), np.sin(ac)
    C64 = np.concatenate([cr, cr, cc_, cc_], axis=1)
    S64 = np.concatenate([-sr, sr, -sc_, sc_], axis=1)
    C = np.concatenate([C64, C64], axis=1).T
    Sg = np.concatenate([S64, S64], axis=1).T
    Cf = np.concatenate([np.ones((128, CTX), np.float32), C.astype(np.float32)], axis=1)
    Sf = np.concatenate([np.zeros((128, CTX), np.float32), Sg.astype(np.float32)], axis=1)
    return np.ascontiguousarray(Cf), np.ascontiguousarray(Sf)


def _w_in_ext(w_in):
    sl = lambda a, b: w_in[:, :, a:b]
    hq, ff, fb, hi, hg = sl(0, 512), sl(512, 1024), sl(1024, 1536), sl(1536, 2048), sl(2048, 2560)
    dq, dk, dv = sl(2560, 3072), sl(3072, 3584), sl(3584, 4096)
    z, xbc, dtf, dtb = sl(4096, 5120), sl(5120, 6656), sl(6656, 6672), sl(6672, 6688)
    perm64 = np.concatenate([np.arange(16, 32), np.arange(0, 16), np.arange(48, 64), np.arange(32, 48)])
    perm = np.concatenate([blk * 64 + perm64 for blk in range(8)])
    dqs, dks = dq[:, :, perm], dk[:, :, perm]
    parts = [hq, ff, fb, xbc[:, :, 0:512], dq, dqs, dk, dks, xbc[:, :, 512:1024], xbc[:, :, 1024:1536],
             hi, hg, dv, z[:, :, 0:512], z[:, :, 512:1024], dtf, dtb]
    out = np.concatenate(parts, axis=2)
    assert out.shape[2] == NCOL
    return np.ascontiguousarray(out)


def host_inputs(cfg, b, inputs, shared):
    x, ctx, c, c_ctx = inputs["x"], inputs["ctx"], inputs["c"], inputs["c_ctx"]
    m = dict(shared)
    if b is None:
        m["xin"] = np.zeros((cfg.nt, D), np.float32)
        m["cc"] = np.zeros((2, D), np.float32)
        m["b_ada"] = np.zeros_like(shared["b_ada"])
        m["conv_b"] = np.zeros_like(shared["conv_b"])
        return m
    m["xin"] = np.ascontiguousarray(np.concatenate([ctx[b], x[b]], axis=0))
    m["cc"] = np.ascontiguousarray(np.stack([c[b], c_ctx], axis=0))
    return m


def shared_inputs(cfg, inputs):
    rc, rs = _rope_tables(cfg.seq)
    f = lambda a: np.ascontiguousarray(np.asarray(a, dtype=np.float32))
    return {
        "w_ada": f(inputs["w_ada"]), "b_ada": f(inputs["b_ada"]),
        "norm1_w": f(inputs["norm1_w"]), "norm2_w": f(inputs["norm2_w"]),
        "final_norm_w": f(inputs["final_norm_w"]).reshape(1, D),
        "w_in_ext": _w_in_ext(f(inputs["w_in"])),
        "hg_lb": f(inputs["hg_lb_logits"]), "hg_norm_w": f(inputs["hg_norm_w"]),
        "da_lambda": f(inputs["da_lambda"]).reshape(DEPTH, 256), "da_subln_w": f(inputs["da_subln_w"]),
        "conv_w": f(inputs["ssm_conv_w"]), "conv_b": f(inputs["ssm_conv_b"]),
        "dt_bias": f(inputs["ssm_dt_bias"]).reshape(DEPTH, 32), "a_log": f(inputs["ssm_a_log"]).reshape(DEPTH, 32),
        "ssm_d": f(inputs["ssm_d"]), "ssm_norm_w": f(inputs["ssm_norm_w"]),
        "w_out": f(inputs["w_out"]), "w_g": f(inputs["w_ffn_gate"]), "w_u": f(inputs["w_ffn_up"]),
        "w_d": f(inputs["w_ffn_down"]),
        "rope_c": rc, "rope_s": rs,
    }


def kernel(**inputs):
    inputs = {k: np.asarray(v) for k, v in inputs.items()}
    B, seq = inputs["x"].shape[0], inputs["x"].shape[1]
    cfg = Cfg(seq=seq)
    nc = build_program(cfg)
    shared = shared_inputs(cfg, inputs)
    owner = {core: i for i, core in enumerate(ACTIVE_CORES[:B])}
    in_maps = [host_inputs(cfg, owner.get(core), inputs, shared) for core in range(N_CORES)]
    res = run_bass_kernel_spmd(nc, in_maps, core_ids=list(range(N_CORES)))
    out = np.stack([res.results[ACTIVE_CORES[b]]["y"] for b in range(B)], axis=0)
    return out.astype(np.float32)
```
